# Optimizing a Trainium2 kernel written in Bass

```python
import jax, jax.numpy as jnp
from jax import lax
import numpy as np

D_MODEL = 1024
BATCH = 8
SEQ = 2048
DEPTH = 1
DEC_BATCH = 128
DEC_SEQ = 4
PAST_LEN = 16384
PAGE_SIZE = 128

N_META = 16
RET_HEADS = 4
RET_DK = 64
RET_DV = 128
RET_CHUNK = 128
RWKV_HEADS = 8
RWKV_HD = 64
RWKV_W = RWKV_HEADS * RWKV_HD
DECAY_LORA = 64
AAA_LORA = 64
GATE_LORA = 128
D_FF = 2816
ROPE_BASE = 10000.0
NORM_EPS = 1e-6
RET_GN_EPS = 1e-6
RWKV_GN_EPS = 64e-5

RET_QK = RET_HEADS * RET_DK
RET_V = RET_HEADS * RET_DV
SHIFT_W = 3 * RWKV_W + DECAY_LORA + AAA_LORA + GATE_LORA
GATE_W = 2 * D_MODEL
PROJ_W = 2 * RET_QK + 2 * RET_V + SHIFT_W + GATE_W

kernel_name = 'hybrid_retention_rwkv7_decoder'


def _rmsnorm(x, g):
    x32 = x.astype(jnp.float32)
    y = x32 * lax.rsqrt(jnp.mean(x32 * x32, axis=-1, keepdims=True) + NORM_EPS)
    return (y * g.astype(jnp.float32)).astype(x.dtype)


def _swiglu(x, w_gate, w_up, w_down):
    return (jax.nn.silu(x @ w_gate) * (x @ w_up)) @ w_down


def _rotary(x, pos):
    half = x.shape[-1] // 2
    inv_freq = ROPE_BASE ** (-jnp.arange(half, dtype=jnp.float32) / half)
    ang = pos.astype(jnp.float32)[:, None] * inv_freq[None, :]
    cos = jnp.cos(ang)[None, :, None, :]
    sin = jnp.sin(ang)[None, :, None, :]
    x1, x2 = x[..., :half], x[..., half:]
    return jnp.concatenate([x1 * cos - x2 * sin, x1 * sin + x2 * cos], axis=-1)


def _head_norm(y, eps):
    mu = jnp.mean(y, axis=-1, keepdims=True)
    yc = y - mu
    var = jnp.mean(yc * yc, axis=-1, keepdims=True)
    out = yc * lax.rsqrt(var + eps)
    return out.reshape(y.shape[0], y.shape[1], -1)


def _retention_chunk(S, q, k, v, log_gamma):
    L = q.shape[1]
    idx = jnp.arange(L, dtype=jnp.float32)
    diff = idx[:, None] - idx[None, :]
    mask = jnp.where(diff >= 0, jnp.exp(log_gamma[:, None, None] * jnp.maximum(diff, 0.0)), 0.0)
    scores = jnp.einsum('bihd,bjhd->bhij', q, k) * mask[None]
    o = jnp.einsum('bhij,bjhv->bihv', scores, v)
    q_decay = jnp.exp(log_gamma[:, None] * (idx + 1.0)[None, :])
    o = o + jnp.einsum('bihd,bhdv,hi->bihv', q, S, q_decay)
    k_decay = jnp.exp(log_gamma[:, None] * (L - 1.0 - idx)[None, :])
    S_new = jnp.exp(log_gamma * L)[None, :, None, None] * S + jnp.einsum('bjhd,hj,bjhv->bhdv', k, k_decay, v)
    return o, S_new


def _retention_mix(q, k, v, S, log_gamma, lead):
    B, T = q.shape[0], q.shape[1]
    outs = []
    if lead > 0:
        o, S = _retention_chunk(S, q[:, :lead], k[:, :lead], v[:, :lead], log_gamma)
        outs.append(o)
    n_full, rem = divmod(T - lead, RET_CHUNK)
    if n_full > 0:
        def to_blocks(a):
            s = a[:, lead:lead + n_full * RET_CHUNK]
            return s.reshape((B, n_full, RET_CHUNK) + a.shape[2:]).swapaxes(0, 1)

        def step(S_c, blk):
            qc, kc, vc = blk
            o_c, S_c = _retention_chunk(S_c, qc, kc, vc, log_gamma)
            return S_c, o_c

        S, o_blocks = lax.scan(step, S, (to_blocks(q), to_blocks(k), to_blocks(v)))
        outs.append(o_blocks.swapaxes(0, 1).reshape((B, n_full * RET_CHUNK) + v.shape[2:]))
    if rem > 0:
        s0 = lead + n_full * RET_CHUNK
        o, S = _retention_chunk(S, q[:, s0:], k[:, s0:], v[:, s0:], log_gamma)
        outs.append(o)
    o_all = jnp.concatenate(outs, axis=1) if len(outs) > 1 else outs[0]
    return o_all, S


def _rwkv7_mix(p, prev, S, mu_shift, w0, w2, a0, a2, g2, k_k, k_a, r_k, lnx_g, lnx_b):
    B, T, _ = p.shape
    p_prev = jnp.concatenate([prev[:, None, :], p[:, :-1]], axis=1)
    pm = p + (p_prev - p) * mu_shift
    splits = [RWKV_W, 2 * RWKV_W, 3 * RWKV_W, 3 * RWKV_W + DECAY_LORA, 3 * RWKV_W + DECAY_LORA + AAA_LORA]
    r, k, v, xw, xa, xg = jnp.split(pm, splits, axis=-1)
    w = -jax.nn.softplus(-(w0 + jnp.tanh(xw) @ w2)) - 0.5
    decay = jnp.exp(-jnp.exp(w))
    a = jax.nn.sigmoid(a0 + xa @ a2)
    g = jax.nn.sigmoid(xg) @ g2

    def hs(t):
        return t.reshape(B, T, RWKV_HEADS, RWKV_HD)

    kk = hs(k * k_k)
    kk = kk * lax.rsqrt(jnp.maximum(jnp.sum(kk * kk, axis=-1, keepdims=True), 1e-24))
    k = k * (1.0 + (a - 1.0) * k_a)
    r_h, k_h, v_h, a_h = hs(r), hs(k), hs(v), hs(a)
    b_h = kk * a_h

    def step(S_t, inp):
        r_t, w_t, k_t, v_t, kk_t, b_t = inp
        sa = jnp.einsum('bhvk,bhk->bhv', S_t, -kk_t)
        S_t = S_t * w_t[:, :, None, :] + sa[..., :, None] * b_t[..., None, :] + v_t[..., :, None] * k_t[..., None, :]
        y_t = jnp.einsum('bhvk,bhk->bhv', S_t, r_t)
        return S_t, y_t

    seq = tuple(jnp.moveaxis(t, 1, 0) for t in (r_h, hs(decay), k_h, v_h, kk, b_h))
    S, y = lax.scan(step, S, seq)
    y = jnp.moveaxis(y, 0, 1)
    out = _head_norm(y, RWKV_GN_EPS) * lnx_g + lnx_b
    bonus = jnp.sum(r_h * k_h * r_k, axis=-1, keepdims=True) * v_h
    out = (out + bonus.reshape(B, T, RWKV_W)) * g
    return out, S, p[:, -1]


def _layer(x, pos, lead, S_ret, S_wkv, shift_prev,
           ffn1_norm, ffn1_w_gate, ffn1_w_up, ffn1_w_down, mix_norm, w_in, ret_gn_g,
           mu_shift, w0, w2, a0, a2, g2, k_k, k_a, r_k, lnx_g, lnx_b,
           w_out_ret, w_out_rwkv, w_out, ffn2_norm, ffn2_w_gate, ffn2_w_up, ffn2_w_down):
    B, T, _ = x.shape
    h = x + 0.5 * _swiglu(_rmsnorm(x, ffn1_norm), ffn1_w_gate, ffn1_w_up, ffn1_w_down)
    u = _rmsnorm(h, mix_norm)
    p = (u @ w_in).astype(jnp.float32)
    splits = [RET_QK, 2 * RET_QK, 2 * RET_QK + RET_V, 2 * RET_QK + 2 * RET_V, 2 * RET_QK + 2 * RET_V + SHIFT_W]
    q, k, v, g_ret, p_rwkv, gates = jnp.split(p, splits, axis=-1)
    log_gamma = jnp.log1p(-jnp.exp2(-5.0 - jnp.arange(RET_HEADS, dtype=jnp.float32)))
    q = _rotary(q.reshape(B, T, RET_HEADS, RET_DK), pos)
    k = _rotary(k.reshape(B, T, RET_HEADS, RET_DK), pos) * (RET_DK ** -0.5)
    v = v.reshape(B, T, RET_HEADS, RET_DV)
    o_ret, S_ret_new = _retention_mix(q, k, v, S_ret.astype(jnp.float32), log_gamma, lead)
    o_ret = _head_norm(o_ret, RET_GN_EPS) * ret_gn_g * jax.nn.silu(g_ret)
    o_rwkv, S_wkv_new, shift_new = _rwkv7_mix(p_rwkv, shift_prev.astype(jnp.float32), S_wkv.astype(jnp.float32),
                                             mu_shift, w0, w2, a0, a2, g2, k_k, k_a, r_k, lnx_g, lnx_b)
    gate_a, gate_b = jnp.split(jax.nn.sigmoid(gates), 2, axis=-1)
    merged = gate_a * (o_ret @ w_out_ret) + gate_b * (o_rwkv @ w_out_rwkv)
    h = h + (merged @ w_out).astype(h.dtype)
    h = h + 0.5 * _swiglu(_rmsnorm(h, ffn2_norm), ffn2_w_gate, ffn2_w_up, ffn2_w_down)
    return h, S_ret_new, S_wkv_new, shift_new


def setup_inputs(seed: int = 0) -> dict:
    key = jax.random.key(seed)
    ks = jax.random.split(key, 32)
    f32 = jnp.float32

    def nrm(k, shape, scale):
        return scale * jax.random.normal(k, shape, f32)

    def gain(k, n):
        return 1.0 + 0.05 * jax.random.normal(k, (n,), f32)

    return {
        'x_prompt': nrm(ks[0], (BATCH, SEQ, D_MODEL), 1.0),
        'x_sample': nrm(ks[1], (DEC_BATCH, DEC_SEQ, D_MODEL), 1.0),
        'state_ret': nrm(ks[2], (DEC_BATCH, RET_HEADS, RET_DK, RET_DV), 0.5),
        'state_wkv': nrm(ks[3], (DEC_BATCH, RWKV_HEADS, RWKV_HD, RWKV_HD), 0.3),
        'state_shift': nrm(ks[4], (DEC_BATCH, SHIFT_W), 1.0),
        'meta_tokens': nrm(ks[5], (N_META, D_MODEL), 1.0),
        'ffn1_norm': gain(ks[6], D_MODEL),
        'ffn1_w_gate': nrm(ks[7], (D_MODEL, D_FF), D_MODEL ** -0.5),
        'ffn1_w_up': nrm(ks[8], (D_MODEL, D_FF), D_MODEL ** -0.5),
        'ffn1_w_down': nrm(ks[9], (D_FF, D_MODEL), D_FF ** -0.5),
        'mix_norm': gain(ks[10], D_MODEL),
        'w_in': nrm(ks[11], (D_MODEL, PROJ_W), D_MODEL ** -0.5),
        'ret_gn_g': gain(ks[12], RET_V),
        'mu_shift': jax.random.uniform(ks[13], (SHIFT_W,), f32, 0.0, 1.0),
        'w0': jax.random.uniform(ks[14], (RWKV_W,), f32, -6.5, -1.5),
        'w2': nrm(ks[15], (DECAY_LORA, RWKV_W), 0.1 * DECAY_LORA ** -0.5),
        'a0': nrm(ks[16], (RWKV_W,), 0.1),
        'a2': nrm(ks[17], (AAA_LORA, RWKV_W), 0.1 * AAA_LORA ** -0.5),
        'g2': nrm(ks[18], (GATE_LORA, RWKV_W), GATE_LORA ** -0.5),
        'k_k': 0.85 + 0.05 * jax.random.normal(ks[19], (RWKV_W,), f32),
        'k_a': gain(ks[20], RWKV_W),
        'r_k': nrm(ks[21], (RWKV_HEADS, RWKV_HD), 0.1),
        'lnx_g': gain(ks[22], RWKV_W),
        'lnx_b': nrm(ks[23], (RWKV_W,), 0.02),
        'w_out_ret': nrm(ks[24], (RET_V, D_MODEL), RET_V ** -0.5),
        'w_out_rwkv': nrm(ks[25], (RWKV_W, D_MODEL), RWKV_W ** -0.5),
        'w_out': nrm(ks[26], (D_MODEL, D_MODEL), D_MODEL ** -0.5),
        'ffn2_norm': gain(ks[27], D_MODEL),
        'ffn2_w_gate': nrm(ks[28], (D_MODEL, D_FF), D_MODEL ** -0.5),
        'ffn2_w_up': nrm(ks[29], (D_MODEL, D_FF), D_MODEL ** -0.5),
        'ffn2_w_down': nrm(ks[30], (D_FF, D_MODEL), D_FF ** -0.5),
        'final_norm': gain(ks[31], D_MODEL),
    }


def reference(x_prompt, x_sample, state_ret, state_wkv, state_shift, meta_tokens,
              ffn1_norm, ffn1_w_gate, ffn1_w_up, ffn1_w_down, mix_norm, w_in, ret_gn_g,
              mu_shift, w0, w2, a0, a2, g2, k_k, k_a, r_k, lnx_g, lnx_b,
              w_out_ret, w_out_rwkv, w_out, ffn2_norm, ffn2_w_gate, ffn2_w_up, ffn2_w_down, final_norm):
    weights = (ffn1_norm, ffn1_w_gate, ffn1_w_up, ffn1_w_down, mix_norm, w_in, ret_gn_g,
               mu_shift, w0, w2, a0, a2, g2, k_k, k_a, r_k, lnx_g, lnx_b,
               w_out_ret, w_out_rwkv, w_out, ffn2_norm, ffn2_w_gate, ffn2_w_up, ffn2_w_down)
    st_dtype = state_ret.dtype
    Bp = x_prompt.shape[0]
    meta = jnp.broadcast_to(meta_tokens.astype(x_prompt.dtype)[None], (Bp, N_META, D_MODEL))
    h_p = jnp.concatenate([meta, x_prompt], axis=1)
    pos_p = jnp.arange(h_p.shape[1], dtype=jnp.int32)
    S_ret_p = jnp.zeros((Bp, RET_HEADS, RET_DK, RET_DV), jnp.float32)
    S_wkv_p = jnp.zeros((Bp, RWKV_HEADS, RWKV_HD, RWKV_HD), jnp.float32)
    sh_p = jnp.zeros((Bp, SHIFT_W), jnp.float32)
    h_s = x_sample
    pos_s = PAST_LEN + jnp.arange(x_sample.shape[1], dtype=jnp.int32)
    S_ret_s, S_wkv_s, sh_s = state_ret, state_wkv, state_shift
    for _ in range(DEPTH):
        h_p, S_ret_p, S_wkv_p, sh_p = _layer(h_p, pos_p, N_META, S_ret_p, S_wkv_p, sh_p, *weights)
        h_s, S_ret_s, S_wkv_s, sh_s = _layer(h_s, pos_s, 0, S_ret_s, S_wkv_s, sh_s, *weights)
    y_prompt = _rmsnorm(h_p, final_norm)[:, N_META:].astype(x_prompt.dtype)
    y_sample = _rmsnorm(h_s, final_norm).astype(x_sample.dtype)
    return (y_prompt, y_sample,
            S_ret_p.astype(st_dtype), S_wkv_p.astype(st_dtype), sh_p.astype(st_dtype),
            S_ret_s.astype(st_dtype), S_wkv_s.astype(st_dtype), sh_s.astype(st_dtype))
```

```python
import contextlib
import numpy as np
import concourse.bass as bass
import concourse.mybir as mybir
from concourse.bass_utils import run_bass_kernel_spmd

F32 = mybir.dt.float32
BF16 = mybir.dt.bfloat16
ALU = mybir.AluOpType
AF = mybir.ActivationFunctionType
AX = mybir.AxisListType

N_CORES = 8
N_META = 16
RET_HEADS, RET_DK, RET_DV = 4, 64, 128
RWKV_HEADS, RWKV_HD = 8, 64
RWKV_W = 512
SHIFT_W = 1792
ROPE_BASE = 10000.0
NORM_EPS = 1e-6
RET_GN_EPS = 1e-6
RWKV_GN_EPS = 64e-5
PAST_LEN = 16384
LS = 4
NSEQ = 16
SAME_ENGINE_SYNC = True
RAW_ONLY_SELF = False


class Buf:
    __slots__ = ("w", "r")

    def __init__(self):
        self.w = None
        self.r = []


class Sched:
    ENG = ("pe", "act", "dve", "pool", "sp")

    def __init__(self, nc, es, n_dma_sems=20):
        self.nc = nc
        self.prog = {e: [] for e in self.ENG}
        self.cnt = {e: 0 for e in self.ENG}
        self.waited = {e: {} for e in self.ENG}
        self.sems = {}
        for e in self.ENG:
            self.sems[e] = es.enter_context(nc.semaphore("s_" + e))
        self.dma_sems = {}
        self.dma_rr = {}
        self.dma_uses = {}
        for q in ("sp", "pool", "act"):
            ks = []
            for i in range(n_dma_sems if q != "act" else 8):
                k = "d_%s_%d" % (q, i)
                self.sems[k] = es.enter_context(nc.semaphore(k))
                self.dma_uses[k] = 0
                ks.append(k)
            self.dma_sems[q] = ks
            self.dma_rr[q] = 0
        self.n_ops = 0

    def _deps(self, e, reads, writes):
        deps = {}

        def add(ev):
            if ev is None:
                return
            k, v = ev
            if deps.get(k, 0) < v:
                deps[k] = v

        for b in reads:
            add(b.w)
        raw_self = deps.get(e, 0)
        for b in writes:
            add(b.w)
            for ev in b.r:
                add(ev)
        if RAW_ONLY_SELF and e in deps:
            if raw_self:
                deps[e] = raw_self
            else:
                del deps[e]
        waits = []
        for k, v in deps.items():
            if k == e:
                if e == "pe" or not SAME_ENGINE_SYNC:
                    continue
                if v > self.cnt[e]:
                    continue
            if self.waited[e].get(k, 0) >= v:
                continue
            if k in self.cnt and k != e and v > self.cnt[k]:
                import traceback
                self.pending_waits = getattr(self, "pending_waits", 0) + 1
                if self.pending_waits <= 3:
                    print("WARNING: %s waits on unsignalled %s event %d (cnt %d)" % (e, k, v, self.cnt[k]))
                    traceback.print_stack(limit=6)
            self.waited[e][k] = v
            waits.append((k, v))
        return waits

    def op(self, e, fn, reads=(), writes=(), signal=True):
        waits = self._deps(e, reads, writes)
        ev = (e, self.cnt[e] + 1)
        if signal:
            self.cnt[e] += 1
        self.prog[e].append((waits, fn, (e, 1) if signal else None))
        for b in reads:
            b.r.append(ev)
        for b in writes:
            b.w = ev
            b.r = []
        self.n_ops += 1

    def dma(self, q, out, in_, reads=(), writes=()):
        waits = self._deps(q, reads, writes)
        ks = self.dma_sems[q]
        k = ks[self.dma_rr[q] % len(ks)]
        self.dma_rr[q] += 1
        prev = 16 * self.dma_uses[k]
        if prev and self.waited[q].get(k, 0) < prev:
            self.waited[q][k] = prev
            waits.append((k, prev))
        self.dma_uses[k] += 1
        ev = (k, 16 * self.dma_uses[k])
        self.prog[q].append((waits, lambda eng, o=out, i=in_: eng.dma_start(out=o, in_=i), (k, 16)))
        for b in reads:
            b.r.append(ev)
        for b in writes:
            b.w = ev
            b.r = []
        return ev

    def wait_event(self, e, ev):
        k, v = ev
        if self.waited[e].get(k, 0) < v:
            self.waited[e][k] = v
            self.prog[e].append(([(k, v)], None, None))

    def barrier(self):
        for e in self.ENG:
            for f in self.ENG:
                if f != e and self.cnt[f] > 0:
                    self.wait_event(e, (f, self.cnt[f]))
            for k, u in self.dma_uses.items():
                if u > 0:
                    self.wait_event(e, (k, 16 * u))

    def emit(self, block):
        nc = self.nc
        sems = self.sems

        def runner(name):
            def run(eng):
                for waits, fn, sig in self.prog[name]:
                    for k, v in waits:
                        eng.wait_ge(sems[k], v)
                    if fn is None:
                        continue
                    ins = fn(eng)
                    if sig is not None:
                        ins.then_inc(sems[sig[0]], sig[1])
            return run

        block.tensor(runner("pe"))
        block.scalar(runner("act"))
        block.vector(runner("dve"))
        block.gpsimd(runner("pool"))
        block.sync(runner("sp"))
        self._clear = True

    def clear(self):
        self.prog = {e: [] for e in self.ENG}


def _rr(lst, i):
    return lst[i % len(lst)]


class Cfg:
    def __init__(self, DM=1024, DFF=2816, NT=16):
        self.DM, self.DFF, self.NT = DM, DFF, NT
        self.KC = DM // 128
        self.FC = DFF // 128
        self.PROJ_W = 512 + 1024 + SHIFT_W + 2 * DM
        self.NSP = N_META + NSEQ * LS
        self.TB = 512
        assert NT % 2 == 0 or NT < 2
        self.TPB = min(2, NT)
        self.NBLK = 1 + NT // self.TPB
        self.NCOL = self.NSP + 128 * NT


def host_constants(cfg):
    f32 = np.float32
    c = {}
    c["ident"] = np.eye(128, dtype=f32)
    bo = np.zeros((128, 128), f32)
    bo[:64, :64] = 1.0
    bo[64:, 64:] = 1.0
    c["blockones"] = bo
    hs = np.zeros((128, 2), f32)
    hs[:64, 0] = 1.0
    hs[64:, 1] = 1.0
    c["hsel"] = hs
    P = np.zeros((128, 128), f32)
    for p in range(128):
        d = p % 64
        src = p + 32 if d < 32 else p - 32
        P[src, p] = 1.0
    c["rotP"] = P
    half = 32
    inv_freq = (f32(ROPE_BASE) ** (-np.arange(half, dtype=f32) / f32(half))).astype(f32)
    pos = np.concatenate([
        np.arange(N_META, dtype=np.int32),
        np.tile(PAST_LEN + np.arange(LS, dtype=np.int32), NSEQ),
        N_META + np.arange(128 * cfg.NT, dtype=np.int32),
    ])
    ang = pos.astype(f32)[None, :] * inv_freq[:, None]
    cs, sn = np.cos(ang).astype(f32), np.sin(ang).astype(f32)
    cosT = np.zeros((128, cfg.NCOL), f32)
    sinT = np.zeros((128, cfg.NCOL), f32)
    for p in range(128):
        d = p % 64
        j = d % 32
        cosT[p] = cs[j]
        sinT[p] = -sn[j] if d < 32 else sn[j]
    c["cosT"], c["sinT"] = cosT, sinT
    lg = np.log1p(-np.exp2(-5.0 - np.arange(RET_HEADS, dtype=f32))).astype(f32)
    c["_lg"] = lg
    idx = np.arange(128, dtype=f32)
    diff = idx[None, :] - idx[:, None]
    m = np.zeros((128, 4, 128), f32)
    for h in range(4):
        m[:, h, :] = np.where(diff >= 0, np.exp(lg[h] * np.maximum(diff, 0.0)), 0.0)
    c["rmaskT"] = m
    ms = np.zeros((64, 4, 64), f32)
    seq = np.arange(64) // LS
    same = (seq[:, None] == seq[None, :])
    d64 = (np.arange(64, dtype=f32)[None, :] - np.arange(64, dtype=f32)[:, None])
    for h in range(4):
        ms[:, h, :] = np.where((d64 >= 0) & same, np.exp(lg[h] * np.maximum(d64, 0.0)), 0.0)
    c["rmaskT_s"] = ms
    qd = np.zeros((128, 2, 128), f32)
    qds = np.zeros((128, 2, 64), f32)
    for ch in range(2):
        for hp in range(2):
            h = 2 * ch + hp
            qd[64 * hp:64 * hp + 64, ch, :] = np.exp(lg[h] * (idx + 1.0))[None, :]
            qds[64 * hp:64 * hp + 64, ch, :] = np.exp(lg[h] * ((np.arange(64) % LS) + 1.0))[None, :].astype(f32)
    c["qdec"], c["qdec_s"] = qd, qds
    kd = np.zeros((128, 3, 4), f32)
    for h in range(4):
        kd[:, 0, h] = np.exp(lg[h] * (127.0 - idx))
        kd[:16, 1, h] = np.exp(lg[h] * (15.0 - idx[:16]))
        kd[:64, 2, h] = np.exp(lg[h] * (LS - 1.0 - (np.arange(64) % LS)))
    c["kdec"] = kd
    su = (idx[:, None] < idx[None, :]).astype(f32)
    iu = (idx[:, None] <= idx[None, :]).astype(f32)
    sl = (idx[:, None] > idx[None, :]).astype(f32)
    wm = np.stack([su, sl, su, iu, iu, su], axis=1)
    c["wmask"] = wm.astype(f32)
    sm = same.astype(f32)
    wms = np.zeros((64, 6, 64), f32)
    for i, mm in enumerate([su, sl, su, iu, iu, su]):
        wms[:, i, :] = mm[:64, :64] * sm
    c["wmask_s"] = wms
    c["seqsel"] = (seq[:, None] == np.arange(NSEQ)[None, :]).astype(f32)
    c["seqselT"] = np.broadcast_to((np.arange(NSEQ)[:, None] == seq[None, :]).astype(f32)[None], (128, NSEQ, 64)).copy()
    return c


class Tile:
    def __init__(self, t, nb=1):
        self.t = t
        self.bs = [Buf() for _ in range(nb)]

    @property
    def b(self):
        return self.bs[0]


class Builder:
    D_PER_M = 1
    DEFER_QKVG = True
    PAIR_IL = True
    NPAIR_IL = 2
    NW8 = 11
    SEP_RET = True
    BF16_CHAIN = True
    PP_ENG = "dve"

    def __init__(self, cfg, debug=()):
        self.cfg = cfg
        self.debug = debug
        self.dbg_outs = {}

    def sb(self, name, shape, dt=F32, nb=1):
        self._uid = getattr(self, "_uid", 0) + 1
        t = self.es.enter_context(self.nc.sbuf_tensor("%s_%d" % (name, self._uid), list(shape), dt))
        return Tile(t, nb)

    def dram_in(self, name, shape):
        return self.nc.dram_tensor(name, list(shape), F32, kind="ExternalInput").ap()

    def dram_out(self, name, shape):
        return self.nc.dram_tensor(name, list(shape), F32, kind="ExternalOutput").ap()

    def ps(self, hold=False):
        while True:
            p = self.psb[self.ps_i % len(self.psb)]
            self.ps_i += 1
            if id(p) not in self.held:
                break
        if hold:
            self.held.add(id(p))
        return p

    def release(self, p):
        self.held.discard(id(p))

    def mm(self, out, lhsT, rhs, start, stop, reads, writes, signal):
        self.S.op("pe", lambda e: e.matmul(out, lhsT, rhs, start=start, stop=stop),
                  reads=reads, writes=writes, signal=signal)

    def tr(self, out, in_, ident, reads, writes, signal=True):
        self.S.op("pe", lambda e: e.transpose(out, in_, ident), reads=reads, writes=writes, signal=signal)

    def act(self, out, in_, func, reads, writes, bias=None, scale=None):
        kw = {}
        if bias is not None:
            kw["bias"] = bias
        if scale is not None:
            kw["scale"] = scale
        self.S.op("act", lambda e: e.activation(out=out, in_=in_, func=func, **kw), reads=reads, writes=writes)

    def tt(self, out, in0, in1, op, reads, writes, eng="dve"):
        self.S.op(eng, lambda e: e.tensor_tensor(out=out, in0=in0, in1=in1, op=op), reads=reads, writes=writes)

    def ts(self, out, in0, s1, s2, op0, op1, reads, writes, eng="dve"):
        if op1 is None:
            self.S.op(eng, lambda e: e.tensor_scalar(out=out, in0=in0, scalar1=s1, scalar2=None, op0=op0),
                      reads=reads, writes=writes)
        else:
            self.S.op(eng, lambda e: e.tensor_scalar(out=out, in0=in0, scalar1=s1, scalar2=s2, op0=op0, op1=op1),
                      reads=reads, writes=writes)

    def stt(self, out, in0, scalar, in1, op0, op1, reads, writes, eng="dve"):
        self.S.op(eng, lambda e: e.scalar_tensor_tensor(out=out, in0=in0, scalar=scalar, in1=in1, op0=op0, op1=op1),
                  reads=reads, writes=writes)

    def cp(self, out, in_, reads, writes, eng="dve"):
        if eng == "act":
            self.S.op("act", lambda e: e.copy(out=out, in_=in_), reads=reads, writes=writes)
        else:
            self.S.op(eng, lambda e: e.tensor_copy(out=out, in_=in_), reads=reads, writes=writes)

    def dbg(self, name, tile_ap, shape, reads):
        if name not in self.debug:
            return
        d = self.dram_out("dbg_" + name, shape)
        self.dbg_outs[name] = shape
        ev = self.S.dma(self.q_misc, d, tile_ap, reads=reads, writes=[])
        self.out_events.append(ev)

    def _wsrc(self, name, nk, j, k0, k1):
        if self.use_scr or (name, j, k0) in self.scr_done:
            return self.d_scr[name][j, :, k0:k1, :], [self.scr_buf[(name, j, k0)]]
        W = self.d_w[name]
        return W[k0 * 128:k1 * 128, j * 128:(j + 1) * 128].rearrange("(k p) m -> p k m", p=128), []

    def _wload(self, t, name, nk, c0, dst3=None):
        j = c0 // 128
        for k0 in range(0, nk, 8):
            k1 = min(nk, k0 + 8)
            src, rb = self._wsrc(name, nk, j, k0, k1)
            dst = t.t[:, k0:k1, :] if dst3 is None else dst3[:, k0:k1, :]
            from_scr = self.use_scr or (name, j, k0) in self.scr_done
            q = self.q_w if self.use_scr else "pool"
            self.S.dma(q, dst, src, reads=rb, writes=[t.b])
            if self.d_scr and not from_scr:
                key = (name, j, k0)
                self.scr_done.add(key)
                self.S.dma("sp", self.d_scr[name][j, :, k0:k1, :], dst, reads=[t.b], writes=[self.scr_buf[key]])

    def load_w8(self, name, nk, c0):
        t = _rr(self.w8, self.w8_i)
        self.w8_i += 1
        self._wload(t, name, nk, c0)
        return t

    def load_wd(self, name, nk, c0):
        t = _rr(self.wd, self.wd_i)
        self.wd_i += 1
        self._wload(t, name, nk, c0)
        return t

    def load_wt(self, name, nk, c0):
        t = _rr(self.wt, self.wt_i)
        self.wt_i += 1
        for q in range(4):
            self._wload(t, name, nk, c0 + q * 128, dst3=t.t[:, :, q * 128:(q + 1) * 128])
        return t

    def conv_phase_b(self):
        cfg = self.cfg
        names = ["w_out_ret", "w_out_rwkv", "w_out", "ffn2_w_gate", "ffn2_w_up", "ffn2_w_down"]
        todo = [("w_in", j) for j in range((512 + 1024 + SHIFT_W) // 128, cfg.PROJ_W // 128)]
        for name in names:
            todo += [(name, j) for j in range(self.weight_shapes()[name][1] // 128)]
        for name, j in todo:
            R_ = self.weight_shapes()[name][0] // 128
            for k0 in range(0, R_, 8):
                self.conv_queue.append((name, j, k0, min(R_, k0 + 8)))
        while self.conv_queue:
            self.emit_conv(1)
            yield

    def emit_conv(self, n):
        while n > 0 and self.conv_queue:
            name, j, k0, k1 = self.conv_queue.pop(0)
            W = self.d_w[name]
            src = W[k0 * 128:k1 * 128, j * 128:(j + 1) * 128].rearrange("(k p) m -> p k m", p=128)
            self.S.dma("pool", self.d_scr[name][j, :, k0:k1, :], src, reads=[], writes=[self.scr_buf[(name, j, k0)]])
            self.scr_done.add((name, j, k0))
            n -= 1

    def rms_stats(self, xT, n):
        cfg = self.cfg
        KC = cfg.KC
        pss = self.ps()
        for kc in range(KC):
            sq = _rr(self.sq, self.sq_i)
            self.sq_i += 1
            self.act(sq.t[:, 0:n], xT.t[:, kc, 0:n], AF.Square, reads=[xT.bs[kc]], writes=[sq.b])
            self.mm(pss.t[:, 0:n], self.ones_bf.t[:, :], sq.t[:, 0:n], kc == 0, kc == KC - 1,
                    reads=[sq.b, self.ones_bf.b], writes=[pss.b], signal=True)
        rs = self.rstd
        self.act(rs.t[:, 0:n], pss.t[:, 0:n], AF.Sqrt, reads=[pss.b, self.eps_norm.b], writes=[rs.b],
                 bias=self.eps_norm.t[:, 0:1], scale=1.0 / cfg.DM)
        self.S.op("dve", lambda e: e.reciprocal(out=rs.t[:, 0:n], in_=rs.t[:, 0:n]), reads=[rs.b], writes=[rs.b])

    def rms_apply(self, xT, gcol, out, c0, nt, o0):
        rs = self.rstd
        for kc in range(self.cfg.KC):
            self.stt(out.t[:, kc, o0:o0 + nt], xT.t[:, kc, c0:c0 + nt], self.vec.t[:, gcol + kc:gcol + kc + 1],
                     rs.t[:, c0:c0 + nt], ALU.mult, ALU.mult, reads=[xT.bs[kc], rs.b, self.vec.b], writes=[out.bs[kc]])

    def rmsnorm(self, xT, n, gcol, out):
        self.rms_stats(xT, n)
        self.rms_apply(xT, gcol, out, 0, n, 0)

    def ffn(self, xT, n, gcol, Wg, Wu, Wd):
        cfg = self.cfg
        KC, FC = cfg.KC, cfg.FC
        xn, h1 = self.xn, self.h1
        self.rmsnorm(xT, n, gcol, xn)
        yield
        for j in range(FC):
            wg = self.load_w8(Wg, KC, j * 128)
            wu = self.load_w8(Wu, KC, j * 128)
            pg, pu = self.ps(hold=True), self.ps(hold=True)
            for kc in range(KC):
                self.mm(pg.t[:, 0:n], wg.t[:, kc, :], xn.t[:, kc, 0:n], kc == 0, kc == KC - 1,
                        reads=[wg.b, xn.bs[kc]], writes=[pg.b], signal=(kc == KC - 1))
            yield
            for kc in range(KC):
                self.mm(pu.t[:, 0:n], wu.t[:, kc, :], xn.t[:, kc, 0:n], kc == 0, kc == KC - 1,
                        reads=[wu.b, xn.bs[kc]], writes=[pu.b], signal=(kc == KC - 1))
            sg = _rr(self.sg, self.sg_i)
            self.sg_i += 1
            self.act(sg.t[:, 0:n], pg.t[:, 0:n], AF.Silu, reads=[pg.b], writes=[sg.b])
            self.tt(h1.t[:, j, 0:n], sg.t[:, 0:n], pu.t[:, 0:n], ALU.mult, reads=[sg.b, pu.b], writes=[h1.bs[j]])
            self.release(pg)
            self.release(pu)
            yield
        for m in range(KC):
            wd = self.load_wd(Wd, FC, m * 128)
            po = self.ps(hold=True)
            for j in range(FC):
                self.mm(po.t[:, 0:n], wd.t[:, j, :], h1.t[:, j, 0:n], j == 0, j == FC - 1,
                        reads=[wd.b, h1.bs[j]], writes=[po.b], signal=(j == FC - 1))
                if j % 8 == 7 and j != FC - 1:
                    yield
            self.stt(xT.t[:, m, 0:n], po.t[:, 0:n], 0.5, xT.t[:, m, 0:n], ALU.mult, ALU.add,
                     reads=[po.b, xT.bs[m]], writes=[xT.bs[m]])
            self.release(po)
            yield

    def load_block_x(self, blk, xT):
        cfg = self.cfg
        KC = cfg.KC
        if blk == 0:
            tiles = [(0, cfg.NSP, None)]
        else:
            tiles = [(128 * i, 128, (blk - 1) * cfg.TPB + i) for i in range(cfg.TPB)]
        for c0, nt, pt in tiles:
            xin = _rr(self.xin, self.xin_i)
            self.xin_i += 1
            if pt is None:
                self.S.dma(self.q_misc, xin.t[0:N_META, :], self.d_meta[:, :], reads=[], writes=[xin.b])
                self.S.dma(self.q_misc, xin.t[N_META:cfg.NSP, :], self.d_xs[:, :], reads=[], writes=[xin.b])
            else:
                self.S.dma(self.q_misc, xin.t[:, :], self.d_xp[pt * 128:(pt + 1) * 128, :], reads=[], writes=[xin.b])
            for k0 in range(0, KC, 4):
                nk = min(4, KC - k0)
                p = self.ps()
                for kk in range(nk):
                    self.tr(p.t[:, kk * 128:kk * 128 + nt], xin.t[0:nt, (k0 + kk) * 128:(k0 + kk + 1) * 128],
                            self.ident.t[0:nt, 0:nt], reads=[xin.b, self.ident.b], writes=[p.b], signal=(kk == nk - 1))
                pv = p.t[:, 0:nk * 128].rearrange("p (k t) -> p k t", t=128)
                self.cp(xT.t[:, k0:k0 + nk, c0:c0 + nt], pv[:, :, 0:nt], reads=[p.b],
                        writes=[xT.bs[k] for k in range(k0, k0 + nk)], eng="act")
            yield

    def store_block_y(self, blk, xT, n):
        cfg = self.cfg
        KC = cfg.KC
        yT = self.yT
        self.rms_stats(xT, n)
        if blk == 0:
            tiles = [(N_META, NSEQ * LS, None)]
        else:
            tiles = [(128 * i, 128, (blk - 1) * cfg.TPB + i) for i in range(cfg.TPB)]
        for c0, nt, pt in tiles:
            self.rms_apply(xT, 3 * KC, yT, c0, nt, 0)
            yo = _rr(self.yo, self.yo_i)
            self.yo_i += 1
            for k0 in range(0, KC, 4):
                nk = min(4, KC - k0)
                p = self.ps()
                for kk in range(nk):
                    self.tr(p.t[0:nt, kk * 128:(kk + 1) * 128], yT.t[:, k0 + kk, 0:nt], self.ident.t[:, :],
                            reads=[yT.bs[k0 + kk], self.ident.b], writes=[p.b], signal=(kk == nk - 1))
                self.cp(yo.t[0:nt, k0 * 128:(k0 + nk) * 128], p.t[0:nt, 0:nk * 128], reads=[p.b], writes=[yo.b],
                        eng="act")
            dst = self.d_ys[:, :] if pt is None else self.d_yp[pt * 128:(pt + 1) * 128, :]
            ev = self.S.dma(self.q_misc, dst, yo.t[0:nt, :], reads=[yo.b], writes=[])
            self.out_events.append(ev)
            yield

    def in_proj_a(self, blk, hT, n, mtiles):
        cfg = self.cfg
        KC = cfg.KC
        uT = self.uT
        self.rmsnorm(hT, n, KC, uT)
        W = "w_in"
        col0 = self.blk_col0(blk)
        self.S.dma(self.q_misc, self.cosb.t[:, 0:n], self.d_c["cosT"][:, col0:col0 + n], reads=[], writes=[self.cosb.b])
        self.S.dma(self.q_misc, self.sinb.t[:, 0:n], self.d_c["sinT"][:, col0:col0 + n], reads=[], writes=[self.sinb.b])
        pT = self.pT
        for ch in range(14):
            w = self.load_w8(W, KC, 1536 + ch * 128)
            p = self.ps()
            for kc in range(KC):
                self.mm(p.t[:, 0:n], w.t[:, kc, :], uT.t[:, kc, 0:n], kc == 0, kc == KC - 1,
                        reads=[w.b, uT.bs[kc]], writes=[p.b], signal=(kc == KC - 1))
            self.cp(pT.t[:, ch, 1:n + 1], p.t[:, 0:n], reads=[p.b], writes=[pT.bs[ch]], eng="act")
        self.cp(pT.t[:, :, 0], self.pprev.t[:, :], reads=[self.pprev.b], writes=[pT.b], eng="act")

    def in_proj_b(self, blk, n, mtiles):
        cfg = self.cfg
        KC = cfg.KC
        uT = self.uT
        W = "w_in"
        qkr = self.qkr
        qkr = self.qkr
        for c in range(4):
            w = self.load_w8(W, KC, c * 128)
            p = self.ps()
            for kc in range(KC):
                self.mm(p.t[:, 0:n], w.t[:, kc, :], uT.t[:, kc, 0:n], kc == 0, kc == KC - 1,
                        reads=[w.b, uT.bs[kc]], writes=[p.b], signal=(kc == KC - 1))
            qc = _rr(self.sg, self.sg_i)
            self.sg_i += 1
            self.act(qc.t[:, 0:n], p.t[:, 0:n], AF.Copy, reads=[p.b], writes=[qc.b],
                     scale=(1.0 if c < 2 else RET_DK ** -0.5))
            p2 = self.ps()
            self.mm(p2.t[:, 0:n], self.rotP.t[:, :], qc.t[:, 0:n], True, True,
                    reads=[self.rotP.b, qc.b], writes=[p2.b], signal=True)
            t1 = _rr(self.sg, self.sg_i)
            self.sg_i += 1
            self.tt(t1.t[:, 0:n], qc.t[:, 0:n], self.cosb.t[:, 0:n], ALU.mult, reads=[qc.b, self.cosb.b], writes=[t1.b])
            self.tt(qkr.t[:, c, 0:n], p2.t[:, 0:n], self.sinb.t[:, 0:n], ALU.mult, reads=[p2.b, self.sinb.b], writes=[qkr.bs[c]])
            self.tt(qkr.t[:, c, 0:n], qkr.t[:, c, 0:n], t1.t[:, 0:n], ALU.add, reads=[qkr.bs[c], t1.b], writes=[qkr.bs[c]])
            yield
        for (cbase, is_g) in ((512, False), (1024, True)):
            banks = [self.ps(hold=True) for _ in mtiles]
            for q in range(4):
                w = self.load_w8(W, KC, cbase + q * 128)
                for i, (c0, nt, kind) in enumerate(mtiles):
                    for kc in range(KC):
                        self.mm(banks[i].t[0:nt, q * 128:(q + 1) * 128], uT.t[:, kc, c0:c0 + nt], w.t[:, kc, :], kc == 0, kc == KC - 1,
                                reads=[w.b, uT.bs[kc]], writes=[banks[i].b], signal=(kc == KC - 1))
                yield
            for i, (c0, nt, kind) in enumerate(mtiles):
                if is_g:
                    self.act(self.gtok.t[0:nt, i, :], banks[i].t[0:nt, :], AF.Silu, reads=[banks[i].b], writes=[self.gtok.bs[i]])
                else:
                    self.cp(self.vtok.t[0:nt, i, :], banks[i].t[0:nt, :], reads=[banks[i].b], writes=[self.vtok.bs[i]], eng="dve")
                self.release(banks[i])

        self.vg_ready = True

    def in_proj(self, blk, hT, n, mtiles):
        self.in_proj_a(blk, hT, n, mtiles)
        self.run(self.in_proj_b(blk, n, mtiles))

    @staticmethod
    def run(gen):
        if gen is not None:
            for _ in gen:
                pass

    @staticmethod
    def interleave(gm, gd, m_per_d):
        am, ad = True, True
        while am or ad:
            for _ in range(m_per_d):
                if am:
                    try:
                        next(gm)
                    except StopIteration:
                        am = False
            for _ in range(Builder.D_PER_M):
                if ad:
                    try:
                        next(gd)
                    except StopIteration:
                        ad = False

    def blk_col0(self, blk):
        return 0 if blk == 0 else self.cfg.NSP + (blk - 1) * 128 * self.cfg.TPB

    def merge_out(self, blk, hT, n):
        cfg = self.cfg
        KC = cfg.KC
        uT, oret, orw, mg = self.uT, self.oretT, self.orwT, self.mgT
        W = "w_in"
        gcol = 512 + 1024 + SHIFT_W
        for m in range(KC):
            wa = self.load_w8("w_out_ret", 4, m * 128)
            wb = self.load_w8("w_out_rwkv", 4, m * 128)
            wga = self.load_w8(W, KC, gcol + m * 128)
            wgb = self.load_w8(W, KC, gcol + cfg.DM + m * 128)
            pa, pb, pga, pgb = self.ps(), self.ps(), self.ps(), self.ps()
            for c in range(4):
                self.mm(pa.t[:, 0:n], wa.t[:, c, :], oret.t[:, c, 0:n], c == 0, c == 3,
                        reads=[wa.b, oret.bs[c]], writes=[pa.b], signal=(c == 3))
            for c in range(4):
                self.mm(pb.t[:, 0:n], wb.t[:, c, :], orw.t[:, c, 0:n], c == 0, c == 3,
                        reads=[wb.b, orw.bs[c]], writes=[pb.b], signal=(c == 3))
            for kc in range(KC):
                self.mm(pga.t[:, 0:n], wga.t[:, kc, :], uT.t[:, kc, 0:n], kc == 0, kc == KC - 1,
                        reads=[wga.b, uT.bs[kc]], writes=[pga.b], signal=(kc == KC - 1))
            for kc in range(KC):
                self.mm(pgb.t[:, 0:n], wgb.t[:, kc, :], uT.t[:, kc, 0:n], kc == 0, kc == KC - 1,
                        reads=[wgb.b, uT.bs[kc]], writes=[pgb.b], signal=(kc == KC - 1))
            ga = _rr(self.sg, self.sg_i)
            self.sg_i += 1
            gb = _rr(self.sg, self.sg_i)
            self.sg_i += 1
            self.act(ga.t[:, 0:n], pga.t[:, 0:n], AF.Sigmoid, reads=[pga.b], writes=[ga.b])
            self.act(gb.t[:, 0:n], pgb.t[:, 0:n], AF.Sigmoid, reads=[pgb.b], writes=[gb.b])
            self.tt(ga.t[:, 0:n], ga.t[:, 0:n], pa.t[:, 0:n], ALU.mult, reads=[ga.b, pa.b], writes=[ga.b])
            self.tt(gb.t[:, 0:n], gb.t[:, 0:n], pb.t[:, 0:n], ALU.mult, reads=[gb.b, pb.b], writes=[gb.b])
            self.tt(mg.t[:, m, 0:n], ga.t[:, 0:n], gb.t[:, 0:n], ALU.add, reads=[ga.b, gb.b], writes=[mg.bs[m]])
        for m in range(KC):
            w = self.load_w8("w_out", KC, m * 128)
            po = self.ps()
            for kc in range(KC):
                self.mm(po.t[:, 0:n], w.t[:, kc, :], mg.t[:, kc, 0:n], kc == 0, kc == KC - 1,
                        reads=[w.b, mg.bs[kc]], writes=[po.b], signal=(kc == KC - 1))
            self.tt(hT.t[:, m, 0:n], hT.t[:, m, 0:n], po.t[:, 0:n], ALU.add, reads=[hT.bs[m], po.b], writes=[hT.bs[m]])

    WEIGHTS = ("ffn1_w_gate", "ffn1_w_up", "ffn1_w_down", "w_in", "w_out_ret", "w_out_rwkv", "w_out",
               "ffn2_w_gate", "ffn2_w_up", "ffn2_w_down")

    def weight_shapes(self):
        c = self.cfg
        return {"ffn1_w_gate": (c.DM, c.DFF), "ffn1_w_up": (c.DM, c.DFF), "ffn1_w_down": (c.DFF, c.DM),
                "w_in": (c.DM, c.PROJ_W), "w_out_ret": (512, c.DM), "w_out_rwkv": (512, c.DM), "w_out": (c.DM, c.DM),
                "ffn2_w_gate": (c.DM, c.DFF), "ffn2_w_up": (c.DM, c.DFF), "ffn2_w_down": (c.DFF, c.DM)}

    def alloc_block_buffers(self, es, TB, pipelined=False):
        cfg = self.cfg
        KC, FC = cfg.KC, cfg.FC
        self.es = es
        self.TBcur = TB
        self.xT = self.sb("xT", [128, KC, TB], F32, KC)
        self.xTs = [self.xT, self.sb("xTb", [128, KC, TB], F32, KC)] if pipelined else [self.xT, self.xT]
        self.xn = self.sb("xn", [128, KC, TB], BF16, KC)
        self.mgT = self.xn
        self.uT = self.sb("uT", [128, KC, TB], BF16, KC)
        nbig = 14 * (TB + 1) if pipelined else max(14 * (TB + 1), (FC * TB + 1) // 2)
        big = self.sb("big", [128, nbig], F32, 1)
        self.big = big
        self.pT = Tile(big.t[:, 0:14 * (TB + 1)].rearrange("p (c t) -> p c t", t=TB + 1), 1)
        self.pT.bs = [big.b] * 14
        if pipelined:
            self.h1 = self.sb("h1", [128, FC, TB], BF16, FC)
        else:
            h1ap = big.t[:, :].bitcast(BF16)[:, 0:FC * TB].rearrange("p (f t) -> p f t", t=TB)
            self.h1 = Tile(h1ap, 1)
            self.h1.bs = [big.b] * FC
        self.yT = self.sb("yT", [128, KC, 128], F32, KC)
        self.sq = [self.sb("sq%d" % i, [128, TB], BF16) for i in range(2)]
        self.rstd = self.sb("rstd", [128, TB], F32)
        self.sg = [self.sb("sg%d" % i, [128, TB], F32) for i in range(3)]
        self.cosb = self.sb("cosb", [128, TB], F32)
        self.sinb = self.sb("sinb", [128, TB], F32)
        self.qkr = self.sb("qkr", [128, 4, TB], F32, 4)
        ntile = max(2, TB // 128)
        self.vtok = self.sb("vtok", [128, ntile, 512], F32, ntile)
        self.gtok = self.sb("gtok", [128, ntile, 512], BF16, ntile)
        self.oretT = self.sb("oretT", [128, 4, TB], BF16, 4)
        self.orwT = self.sb("orwT", [128, 4, TB], BF16, 4)
        self.sq_i = self.sg_i = 0

    def build(self, stub_mixers=False, stub_rwkv=False):
        cfg = self.cfg
        KC, FC = cfg.KC, cfg.FC
        nc = bass.Bass("TRN2", target_bir_lowering=False)
        self.nc = nc
        self.out_events = []
        self.vg_ready = True
        self.d_xp = self.dram_in("xp", [128 * cfg.NT, cfg.DM])
        self.d_xs = self.dram_in("xs", [NSEQ * LS, cfg.DM])
        self.d_meta = self.dram_in("meta", [N_META, cfg.DM])
        self.d_st_ret = self.dram_in("st_ret", [NSEQ, 4, 64, 128])
        self.d_st_wkv = self.dram_in("st_wkv", [NSEQ, 8, 64, 64])
        self.d_st_shift = self.dram_in("st_shift", [NSEQ, SHIFT_W])
        self.NV = 4 * KC + 14 + 7 * 4
        self.d_vec = self.dram_in("vecs", [128, self.NV])
        self.d_gng = self.dram_in("gng", [128, 512])
        self.d_small = {"w2": self.dram_in("w2", [64, 512]), "a2": self.dram_in("a2", [64, 512]),
                        "g2": self.dram_in("g2", [128, 512])}
        self.d_w = {k: self.dram_in(k, list(s)) for k, s in self.weight_shapes().items()}
        self.d_scr, self.scr_buf, self.conv_queue = {}, {}, []
        self.scr_done = set()
        self.wq_i = 0
        self.use_scr = False
        self.q_misc, self.q_w = "sp", "pool"
        if getattr(self, "scratch", True) and cfg.NBLK > 2:
            order = ["ffn2_w_gate", "ffn2_w_up", "ffn2_w_down", "ffn1_w_gate", "ffn1_w_up", "ffn1_w_down",
                     "w_in", "w_out_ret", "w_out_rwkv", "w_out"]
            for name in ["ffn1_w_gate", "ffn1_w_up", "ffn1_w_down", "w_in", "w_out_ret", "w_out_rwkv", "w_out",
                         "ffn2_w_gate", "ffn2_w_up", "ffn2_w_down"]:
                R_, C_ = self.weight_shapes()[name]
                self.d_scr[name] = nc.dram_tensor("scr_" + name, [C_ // 128, 128, R_ // 128, 128], BF16, kind="Internal").ap()
                for j in range(C_ // 128):
                    for k0 in range(0, R_ // 128, 8):
                        self.scr_buf[(name, j, k0)] = Buf()
        consts = host_constants(cfg)
        self.d_c = {k: self.dram_in("c_" + k, list(v.shape)) for k, v in consts.items() if not k.startswith("_")}
        self.lg = consts["_lg"]
        self.d_yp = self.dram_out("yp", [128 * cfg.NT, cfg.DM])
        self.d_ys = self.dram_out("ys", [NSEQ * LS, cfg.DM])
        self.d_ret_p = self.dram_out("ret_p", [4, 64, 128])
        self.d_wkv_p = self.dram_out("wkv_p", [8, 64, 64])
        self.d_shift_p = self.dram_out("shift_p", [14, 128])
        self.d_ret_s = self.dram_out("ret_s", [NSEQ, 4, 64, 128])
        self.d_wkv_s = self.dram_out("wkv_s", [NSEQ, 8, 64, 64])
        self.d_shift_s = self.dram_out("shift_s", [NSEQ, SHIFT_W])

        with contextlib.ExitStack() as es:
            self.es = es
            self.S = Sched(nc, es)
            S = self.S
            self.psb = []
            for i in range(8):
                t = es.enter_context(nc.psum_tensor("ps%d" % i, [128, 512], F32))
                self.psb.append(Tile(t))
            self.ps_i = 0
            self.held = set()
            if getattr(self, "pad_kb", 0):
                self.sb("pad", [128, self.pad_kb * 256])
            self.vec = self.sb("vec", [128, self.NV])
            self.ident = self.sb("ident", [128, 128])
            self.identb = self.sb("identb", [128, 128], BF16)
            self.ones_bf = self.sb("ones_bf", [128, 128], BF16)
            self.rotP = self.sb("rotP", [128, 128])
            self.blockones = self.sb("blockones", [128, 128])
            self.eps_norm = self.sb("eps_norm", [128, 4])
            self.w8 = [self.sb("w8_%d" % i, [128, 8, 128], BF16) for i in range(Builder.NW8)]
            self.wd = [self.sb("wd_%d" % i, [128, FC, 128], BF16) for i in range(2)]
            self.xin = [self.sb("xin%d" % i, [128, cfg.DM]) for i in range(1)]
            self.yo = self.xin
            self.w8_i = self.wd_i = self.wt_i = self.xin_i = self.yo_i = 0
            self.alloc_common_mixer()
            S.dma(self.q_misc, self.vec.t[:, :], self.d_vec[:, :], writes=[self.vec.b])
            S.dma(self.q_misc, self.ident.t[:, :], self.d_c["ident"][:, :], writes=[self.ident.b])
            S.dma(self.q_misc, self.rotP.t[:, :], self.d_c["rotP"][:, :], writes=[self.rotP.b])
            S.dma(self.q_misc, self.blockones.t[:, :], self.d_c["blockones"][:, :], writes=[self.blockones.b])
            S.op("dve", lambda e: e.memset(self.ones_bf.t[:, :], 1.0), writes=[self.ones_bf.b])
            S.op("dve", lambda e: e.tensor_copy(out=self.identb.t[:, :], in_=self.ident.t[:, :]),
                 reads=[self.ident.b], writes=[self.identb.b])
            S.op("dve", lambda e: e.memset(self.eps_norm.t[:, 0:1], NORM_EPS), writes=[self.eps_norm.b])
            S.op("dve", lambda e: e.memset(self.eps_norm.t[:, 1:2], RET_GN_EPS), writes=[self.eps_norm.b])
            S.op("dve", lambda e: e.memset(self.eps_norm.t[:, 2:3], RWKV_GN_EPS), writes=[self.eps_norm.b])
            self.load_common_mixer()

            for scope in (0, 1):
                with contextlib.ExitStack() as es2:
                    TB = cfg.NSP if scope == 0 else 128 * cfg.TPB
                    pipelined = (scope == 1) and (not stub_mixers) and getattr(self, "pipeline", True) and cfg.NBLK > 2
                    self.alloc_block_buffers(es2, TB, pipelined=pipelined)
                    if scope == 0:
                        self.alloc_sample(es2)
                    if stub_mixers or stub_rwkv:
                        for t in ((self.oretT, self.orwT) if (stub_mixers or getattr(self, "rtrunc", 99) < 5) else (self.orwT,)):
                            S.op("dve", lambda e, t=t: e.memset(t.t[:, :, :], 0.0), writes=t.bs)
                    self.sbuf_used = max(getattr(self, "sbuf_used", 0), 229344 - nc.sbuf_bytes_remaining)
                    self.sbuf_scope = getattr(self, "sbuf_scope", []) + [229344 - nc.sbuf_bytes_remaining]
                    blocks = [0] if scope == 0 else list(range(1, cfg.NBLK))
                    W = self.d_w
                    F1 = ("ffn1_w_gate", "ffn1_w_up", "ffn1_w_down")
                    F2 = ("ffn2_w_gate", "ffn2_w_up", "ffn2_w_down")

                    def mtiles_of(blk):
                        if blk == 0:
                            return [(0, N_META, "meta"), (N_META, NSEQ * LS, "sample")]
                        return [(128 * i, 128, "full") for i in range(cfg.TPB)]

                    def mixers(blk):
                        mt = mtiles_of(blk)
                        if pipelined and Builder.SEP_RET and not stub_rwkv:
                            def R_():
                                for i, (c0, nt, kind) in enumerate(mt):
                                    yield from self.retention(blk, i, c0, nt, kind)

                            def W_():
                                for i, (c0, nt, kind) in enumerate(mt):
                                    yield from self.rwkv(blk, i, c0, nt, kind, last=(blk == cfg.NBLK - 1 and i == len(mt) - 1))
                            gr, gw = R_(), W_()
                            ar = aw = True
                            while ar or aw:
                                for _ in range(2):
                                    if aw:
                                        try:
                                            next(gw)
                                            yield
                                        except StopIteration:
                                            aw = False
                                if ar and self.vg_ready:
                                    try:
                                        next(gr)
                                        yield
                                    except StopIteration:
                                        ar = False
                                if not aw and ar and not self.vg_ready:
                                    yield
                            return
                        for i, (c0, nt, kind) in enumerate(mt):
                            yield from self.retention(blk, i, c0, nt, kind)
                            if not stub_rwkv:
                                yield from self.rwkv(blk, i, c0, nt, kind, last=(blk == cfg.NBLK - 1 and i == len(mt) - 1))

                    run = self.run
                    if not pipelined:
                        for blk in blocks:
                            n = cfg.NSP if blk == 0 else 128 * cfg.TPB
                            xT = self.xTs[0]
                            trunc = getattr(self, "trunc", 99)
                            run(self.load_block_x(blk, xT))
                            if trunc >= 2:
                                run(self.ffn(xT, n, 0, *F1))
                            if trunc >= 3:
                                self.in_proj(blk, xT, n, mtiles_of(blk))
                            if not stub_mixers:
                                if scope == 0 and self.d_scr:
                                    self.interleave(mixers(blk), self.conv_phase_b(), 2)
                                else:
                                    run(mixers(blk))
                            if trunc >= 4:
                                self.merge_out(blk, xT, n)
                            if trunc >= 5:
                                run(self.ffn(xT, n, 2 * KC, *F2))
                            if trunc >= 1:
                                run(self.store_block_y(blk, xT, n))
                    else:
                        n = 128 * cfg.TPB
                        first, lastb = blocks[0], blocks[-1]
                        x0 = self.xTs[first % 2]
                        run(self.load_block_x(first, x0))
                        run(self.ffn(x0, n, 0, *F1))
                        self.in_proj(first, x0, n, mtiles_of(first))
                        for blk in blocks:
                            def dense(blk=blk):
                                if Builder.DEFER_QKVG and blk > first:
                                    yield from self.in_proj_b(blk, n, mtiles_of(blk))
                                if blk > first:
                                    xp = self.xTs[(blk - 1) % 2]
                                    yield from self.ffn(xp, n, 2 * KC, *F2)
                                    yield from self.store_block_y(blk - 1, xp, n)
                                if blk < lastb:
                                    xq = self.xTs[(blk + 1) % 2]
                                    yield from self.load_block_x(blk + 1, xq)
                                    yield from self.ffn(xq, n, 0, *F1)
                            self.interleave(mixers(blk), dense(), getattr(self, "m_per_d", 1))
                            self.merge_out(blk, self.xTs[blk % 2], n)
                            if blk < lastb:
                                if Builder.DEFER_QKVG:
                                    self.in_proj_a(blk + 1, self.xTs[(blk + 1) % 2], n, mtiles_of(blk + 1))
                                    self.vg_ready = False
                                else:
                                    self.in_proj(blk + 1, self.xTs[(blk + 1) % 2], n, mtiles_of(blk + 1))
                        xl = self.xTs[lastb % 2]
                        run(self.ffn(xl, n, 2 * KC, *F2))
                        run(self.store_block_y(lastb, xl, n))
                    if scope == 0 and self.d_scr:
                        assert len(self.scr_done) == len(self.scr_buf), (len(self.scr_done), len(self.scr_buf))
                        self.use_scr = True
                        self.q_misc, self.q_w = "pool", "sp"
                    if scope == 1:
                        if not stub_mixers:
                            self.store_prompt_states()
                        for ev in self.out_events:
                            S.wait_event("sp", ev)
                    S.barrier()
                    with nc.Block() as block:
                        S.emit(block)
                    S.clear()
        return nc

    def alloc_common_mixer(self):
        self.rmaskT = self.sb("rmaskT", [128, 4, 128])
        self.qdec = self.sb("qdec", [128, 2, 128])
        self.kdec = self.sb("kdec", [128, 3, 4])
        self.gng = self.sb("gng", [128, 512])
        self.wmask = self.sb("wmask", [128, 6, 128])
        self.Sret = self.sb("Sret", [128, 2, 128], F32, 2)
        self.Hwkv = self.sb("Hwkv", [128, 4, 64], F32, 4)
        self.pprev = self.sb("pprev", [128, 14])
        self.w2t = self.sb("w2t", [128, 512])
        self.a2t = self.sb("a2t", [128, 512])
        self.g2t = self.sb("g2t", [128, 512])
        self.omka = self.sb("omka", [128, 4])
        self.onesf = self.sb("onesf", [128, 128])
        self.qtm = self.sb("qtm", [128, 2, 2, 128])
        self.kmsk = self.sb("kmskh", [128, 2, 2, 128])
        self.hsel = self.sb("hsel", [128, 2])
        self.atm = self.sb("atm", [128, 2, 128])
        self.rtm = self.sb("rtm", [128, 2, 128])
        self.ktk = self.sb("ktk", [128, 4, 64])
        self.sc = self.sb("sc", [128, 4, 128])
        self.osb = self.sb("osb", [128, 512])
        self.osq = self.sb("osq", [128, 512])
        self.onb = self.sb("onb", [128, 512], BF16)
        self.st4 = self.sb("st4", [128, 4, 4])
        self.pm = self.sb("pm", [128, 14, 128])
        self.rw = {nm: self.sb("rw_" + nm, [128, 4, 128]) for nm in
                   ("logw", "cum", "eg", "ee", "a", "kk", "km", "b", "g", "bon", "tmp")}
        for a_, b_ in (("rt", "eg"), ("kt", "km"), ("bt", "b"), ("at", "ee"), ("ei", "tmp")):
            self.rw[a_] = self.rw[b_]
        self.thw = self.sb("thw", [128, 128])
        self.sgx = self.sb("sgx", [128, 128])
        self.vtk = self.sb("vtk", [128, 512])
        self.ktok = self.sb("ktok", [128, 512])
        self.btok = self.sb("btok", [128, 512])
        self.mA = [self.sb("mA%d" % i, [128, 4, 128]) for i in range(2)]
        self.mB = [self.sb("mB%d" % i, [128, 128]) for i in range(2)]
        self.PP = [] if Builder.BF16_CHAIN else [self.sb("PP%d" % i, [128, 4, 128]) for i in range(2)]
        self.Up = [self.sb("Up%d" % i, [128, 128]) for i in range(3)]
        self.ysb = self.osb
        self.ysq = self.osq
        self.yst = self.sb("yst", [128, 4, 8])
        self.gcum = self.sb("gcum", [128, 4, NSEQ])
        self.mA_i = self.mB_i = self.PP_i = self.Up_i = self.am_i = 0

        def alias(tile, view):
            t = Tile(view)
            t.bs = tile.bs
            return t
        if Builder.SEP_RET:
            self.mA = self.mA + [self.sb("mAs%d" % i, [128, 4, 128]) for i in range(2)]
            if not Builder.BF16_CHAIN:
                self.PP = self.PP + [self.sb("PPs%d" % i, [128, 4, 128]) for i in range(2)]
            self.ysb = self.sb("ysb", [128, 512])
            self.ysq = self.sb("ysq", [128, 512])
        else:
            self.mA = self.mA + [self.sc, alias(self.qtm, self.qtm.t[:, :, :, :].rearrange("p a b t -> p (a b) t"))]
            self.PP = self.PP + [alias(self.kmsk, self.kmsk.t[:, :, :, :].rearrange("p a b t -> p (a b) t")),
                                 alias(self.osb, self.osb.t[:, :].rearrange("p (q t) -> p q t", t=128))]
        self.mP = [self.sb("mP%d" % i, [128, 2, 128], BF16) for i in range(4)]
        self.PPb = [self.sb("PPb%d" % i, [128, 4, 128], BF16) for i in range(4)]
        self.Ub = [self.sb("Ub%d" % i, [128, 128], BF16) for i in range(4)]
        self.mP_i = self.PPb_i = self.Ub_i = 0
        np_ = Builder.NPAIR_IL
        if np_ == 4:
            self.mA = self.mA + [self.sb("mAx%d" % i, [128, 4, 128]) for i in range(4)]
            self.PP = self.PP + [self.sb("PPx%d" % i, [128, 4, 128]) for i in range(4)]
        self.mB = self.mB + [self.sb("mB%d" % i, [128, 128]) for i in range(2, 2 * np_)]
        self.Up = self.Up + [self.sb("Upx%d" % i, [128, 128]) for i in range(3, 2 * np_)]
        self.atms = [self.atm] + [self.sb("atm%d" % i, [128, 2, 128]) for i in range(2, np_ + 1)]
        self.rtms = [self.rtm] + [self.sb("rtm%d" % i, [128, 2, 128]) for i in range(2, np_ + 1)]

    def load_common_mixer(self):
        S = self.S
        for t, k in ((self.rmaskT, "rmaskT"), (self.qdec, "qdec"), (self.kdec, "kdec"), (self.wmask, "wmask")):
            S.dma(self.q_misc, t.t[:], self.d_c[k][:], writes=[t.b])
        S.dma(self.q_misc, self.gng.t[:, :], self.d_gng[:, :], writes=[self.gng.b])
        S.op("dve", lambda e: e.memset(self.w2t.t[:, :], 0.0), writes=[self.w2t.b])
        S.op("dve", lambda e: e.memset(self.a2t.t[:, :], 0.0), writes=[self.a2t.b])
        S.dma(self.q_misc, self.w2t.t[0:64, :], self.d_small["w2"][:, :], writes=[self.w2t.b])
        S.dma(self.q_misc, self.a2t.t[64:128, :], self.d_small["a2"][:, :], writes=[self.a2t.b])
        S.dma(self.q_misc, self.hsel.t[:, :], self.d_c["hsel"][:, :], writes=[self.hsel.b])
        S.dma(self.q_misc, self.g2t.t[:, :], self.d_small["g2"][:, :], writes=[self.g2t.b])
        for t in (self.Sret, self.Hwkv):
            S.op("dve", lambda e, t=t: e.memset(t.t[:], 0.0), writes=t.bs)
        S.op("dve", lambda e: e.memset(self.pprev.t[:, :], 0.0), writes=[self.pprev.b])
        S.op("dve", lambda e: e.memset(self.onesf.t[:, :], 1.0), writes=[self.onesf.b])
        ka = self.vcol("k_a")
        self.ts(self.omka.t[:, :], self.vec.t[:, ka:ka + 4], -1.0, 1.0, ALU.mult, ALU.add,
                reads=[self.vec.b], writes=[self.omka.b])

    def vcol(self, name):
        KC = self.cfg.KC
        off = {"mu": 4 * KC, "w0": 4 * KC + 14, "a0": 4 * KC + 18, "k_k": 4 * KC + 22, "k_a": 4 * KC + 26,
               "r_k": 4 * KC + 30, "lnx_g": 4 * KC + 34, "lnx_b": 4 * KC + 38}
        return off[name]

    def alloc_sample(self, es):
        self.es = es
        S = self.S
        self.rmaskT_s = self.sb("rmaskT_s", [64, 4, 64])
        self.qdec_s = self.sb("qdec_s", [128, 2, 64])
        self.wmask_s = self.sb("wmask_s", [64, 6, 64])
        self.seqsel = self.sb("seqsel", [64, NSEQ])
        self.seqselT = self.sb("seqselT", [128, NSEQ, 64])
        self.Sret_s = self.sb("Sret_s", [128, NSEQ, 128])
        self.Hwkv_s = self.sb("Hwkv_s", [128, NSEQ, 64])
        self.qm = self.sb("qm", [128, NSEQ, 64])
        self.km = self.sb("kmsk", [64, NSEQ, 128])
        self.wkin = self.sb("wkin", [64, NSEQ, 2, 64])
        self.psh = self.sb("psh", [128, 14, 64])
        self.qm2 = self.sb("qm2", [128, NSEQ, 64])
        self.km2 = Tile(self.wkin.t[:, :, :, :].rearrange("v s h k -> v s (h k)"))
        self.km2.bs = self.wkin.bs
        self.shs = Tile(self.km.t[0:NSEQ, :, :].rearrange("p s d -> p (s d)")[:, 0:SHIFT_W])
        self.shs.bs = self.km.bs
        for t, k in ((self.rmaskT_s, "rmaskT_s"), (self.qdec_s, "qdec_s"), (self.wmask_s, "wmask_s"),
                     (self.seqsel, "seqsel"), (self.seqselT, "seqselT")):
            S.dma(self.q_misc, t.t[:], self.d_c[k][:], writes=[t.b])

    def retention(self, blk, i, c0, nt, kind):
        S = self.S
        sample = kind == "sample"
        qkr = self.qkr
        lg = self.lg
        Lseq = LS if sample else nt
        rmask = self.rmaskT_s if sample else self.rmaskT
        qdec = self.qdec_s if sample else self.qdec
        ksel = 2 if sample else (1 if kind == "meta" else 0)
        qtm, ktk, sc, kmsk, hsel = self.qtm, self.ktk, self.sc, self.kmsk, self.hsel
        for hp in range(2):
            self.stt(qtm.t[:, hp, :, 0:nt], qkr.t[:, 0:2, c0:c0 + nt], hsel.t[:, hp:hp + 1], qdec.t[:, :, 0:nt], ALU.mult, ALU.mult,
                     reads=[qkr.bs[0], qkr.bs[1], qdec.b, hsel.b], writes=[qtm.b])
            self.ts(kmsk.t[:, hp, :, 0:nt], qkr.t[:, 2:4, c0:c0 + nt], hsel.t[:, hp:hp + 1], None, ALU.mult, None,
                    reads=[qkr.bs[2], qkr.bs[3], hsel.b], writes=[kmsk.b])
        pk = self.ps()
        for c in range(2):
            self.tr(pk.t[0:nt, c * 128:(c + 1) * 128], qkr.t[:, 2 + c, c0:c0 + nt], self.ident.t[:, :],
                    reads=[qkr.bs[2 + c], self.ident.b], writes=[pk.b], signal=(c == 1))
        self.tt(ktk.t[0:nt, :, :], pk.t[0:nt, 0:256].rearrange("p (h d) -> p h d", d=64),
                self.kdec.t[0:nt, ksel, :].unsqueeze(2).to_broadcast([nt, 4, 64]), ALU.mult,
                reads=[pk.b, self.kdec.b], writes=[ktk.b])
        rtr = getattr(self, "rtrunc", 99)
        yield
        psc = self.ps()
        for h in range(4):
            c, pb = h // 2, 64 * (h % 2)
            self.mm(psc.t[0:nt, h * 128:h * 128 + nt], kmsk.t[:, h % 2, c, 0:nt], qkr.t[:, c, c0:c0 + nt],
                    True, True, reads=[kmsk.b, qkr.bs[c]], writes=[psc.b], signal=(h == 3))
        self.tt(sc.t[0:nt, :, 0:nt], psc.t[0:nt, :].rearrange("p (h t) -> p h t", t=128)[:, :, 0:nt],
                rmask.t[0:nt, :, 0:nt], ALU.mult, reads=[psc.b, rmask.b], writes=[sc.b])
        yield
        po = self.ps(hold=True)
        vt = self.vtok
        for h in range(4):
            c, pb = h // 2, 64 * (h % 2)
            if sample and pb == 0:
                for hp in range(2):
                    src = self.d_st_ret[:, 2 * c + hp, :, :].rearrange("s d v -> d s v")
                    S.dma(self.q_misc, self.Sret_s.t[64 * hp:64 * hp + 64, :, :], src, writes=[self.Sret_s.b])
            if sample:
                self.tt(self.qm.t[:, :, :], qtm.t[:, h % 2, c, 0:nt].unsqueeze(1).to_broadcast([128, NSEQ, 64]),
                        self.seqselT.t[:, :, :], ALU.mult, reads=[qtm.b, self.seqselT.b], writes=[self.qm.b])
            self.mm(po.t[0:nt, h * 128:(h + 1) * 128], sc.t[0:nt, h, 0:nt], vt.t[0:nt, i, h * 128:(h + 1) * 128],
                    True, False, reads=[sc.b, vt.bs[i]], writes=[po.b], signal=False)
            if not sample:
                self.mm(po.t[0:nt, h * 128:(h + 1) * 128], qtm.t[:, h % 2, c, 0:nt], self.Sret.t[:, c, :],
                        False, True, reads=[qtm.b, self.Sret.bs[c]], writes=[po.b], signal=True)
            else:
                for s in range(NSEQ):
                    self.mm(po.t[0:nt, h * 128:(h + 1) * 128], self.qm.t[:, s, :], self.Sret_s.t[:, s, :],
                            False, s == NSEQ - 1, reads=[self.qm.b, self.Sret_s.b], writes=[po.b],
                            signal=(s == NSEQ - 1))
            yield
            if pb == 0:
                continue
            if not sample:
                pu = self.ps()
                self.mm(pu.t[:, 0:256], ktk.t[0:nt, 2 * c:2 * c + 2, :].rearrange("p h d -> p (h d)"),
                        vt.t[0:nt, i, 256 * c:256 * (c + 1)], True, True, reads=[ktk.b, vt.bs[i]], writes=[pu.b], signal=True)
                for hp in range(2):
                    g = float(np.exp(np.float32(lg[2 * c + hp]) * np.float32(Lseq)))
                    self.stt(self.Sret.t[64 * hp:64 * hp + 64, c, :], self.Sret.t[64 * hp:64 * hp + 64, c, :], g,
                             pu.t[64 * hp:64 * hp + 64, 128 * hp:128 * hp + 128], ALU.mult, ALU.add,
                             reads=[self.Sret.bs[c], pu.b], writes=[self.Sret.bs[c]])
            else:
                self.tt(self.km.t[:, :, :], ktk.t[0:64, 2 * c:2 * c + 2, :].rearrange("p h d -> p (h d)").unsqueeze(1).to_broadcast([64, NSEQ, 128]),
                        self.seqsel.t[:, :].unsqueeze(2).to_broadcast([64, NSEQ, 128]), ALU.mult,
                        reads=[ktk.b, self.seqsel.b], writes=[self.km.b])
                for s0 in range(0, NSEQ, 2):
                    pu = self.ps()
                    for s in (s0, s0 + 1):
                        self.mm(pu.t[:, 256 * (s - s0):256 * (s - s0 + 1)], self.km.t[:, s, :], vt.t[0:nt, i, 256 * c:256 * (c + 1)],
                                True, True, reads=[self.km.b, vt.bs[i]], writes=[pu.b], signal=(s == s0 + 1))
                    for hp in range(2):
                        g = float(np.exp(np.float32(lg[2 * c + hp]) * np.float32(Lseq)))
                        sl = slice(64 * hp, 64 * hp + 64)
                        sv = self.Sret_s.t[sl, s0:s0 + 2, :]
                        pv_ = pu.t[sl, :].rearrange("p (s x) -> p s x", x=256)[:, :, 128 * hp:128 * hp + 128]
                        self.stt(sv, sv, g, pv_, ALU.mult, ALU.add, reads=[self.Sret_s.b, pu.b], writes=[self.Sret_s.b])
                for hp in range(2):
                    dst = self.d_ret_s[:, 2 * c + hp, :, :].rearrange("s d v -> d s v")
                    ev = S.dma(self.q_misc, dst, self.Sret_s.t[64 * hp:64 * hp + 64, :, :], reads=[self.Sret_s.b])
                    self.out_events.append(ev)
        yield
        osb, osq, st4 = self.osb, self.osq, self.st4
        self.cp(osb.t[0:nt, :], po.t[0:nt, :], reads=[po.b], writes=[osb.b], eng="act")
        self.release(po)
        o3 = osb.t[0:nt, :].rearrange("p (h v) -> p h v", v=128)
        q3 = osq.t[0:nt, :].rearrange("p (h v) -> p h v", v=128)
        S.op("dve", lambda e: e.tensor_reduce(out=st4.t[0:nt, 0, :], in_=o3, axis=AX.X, op=ALU.add),
             reads=[osb.b], writes=[st4.b])
        self.ts(st4.t[0:nt, 1, :], st4.t[0:nt, 0, :], -1.0 / 128, None, ALU.mult, None, reads=[st4.b], writes=[st4.b])
        self.tt(o3, o3, st4.t[0:nt, 1, :].unsqueeze(2).to_broadcast([nt, 4, 128]), ALU.add, reads=[osb.b, st4.b], writes=[osb.b])
        self.act(osq.t[0:nt, :], osb.t[0:nt, :], AF.Square, reads=[osb.b], writes=[osq.b])
        S.op("dve", lambda e: e.tensor_reduce(out=st4.t[0:nt, 2, :], in_=q3, axis=AX.X, op=ALU.add),
             reads=[osq.b], writes=[st4.b])
        self.act(st4.t[0:nt, 3, :], st4.t[0:nt, 2, :], AF.Sqrt, reads=[st4.b, self.eps_norm.b], writes=[st4.b],
                 bias=self.eps_norm.t[0:nt, 1:2], scale=1.0 / 128)
        S.op("dve", lambda e: e.reciprocal(out=st4.t[0:nt, 3, :], in_=st4.t[0:nt, 3, :]), reads=[st4.b], writes=[st4.b])
        self.tt(o3, o3, st4.t[0:nt, 3, :].unsqueeze(2).to_broadcast([nt, 4, 128]), ALU.mult, reads=[osb.b, st4.b], writes=[osb.b])
        self.tt(osb.t[0:nt, :], osb.t[0:nt, :], self.gng.t[0:nt, :], ALU.mult, reads=[osb.b, self.gng.b], writes=[osb.b])
        self.tt(self.onb.t[0:nt, :], osb.t[0:nt, :], self.gtok.t[0:nt, i, :], ALU.mult,
                reads=[osb.b, self.gtok.bs[i]], writes=[self.onb.b])
        yield
        pt = self.ps()
        ptb = pt.t[:, :].bitcast(BF16)
        for c in range(4):
            self.tr(ptb[:, c * 128:c * 128 + nt], self.onb.t[0:nt, c * 128:(c + 1) * 128], self.identb.t[0:nt, 0:nt],
                    reads=[self.onb.b, self.identb.b], writes=[pt.b], signal=(c == 3))
        self.cp(self.oretT.t[:, :, c0:c0 + nt], ptb[:, 0:512].rearrange("p (c t) -> p c t", t=128)[:, :, 0:nt],
                reads=[pt.b], writes=self.oretT.bs, eng="act")

    def store_prompt_states(self):
        S = self.S
        for c in range(2):
            for hp in range(2):
                ev = S.dma(self.q_misc, self.d_ret_p[2 * c + hp, :, :], self.Sret.t[64 * hp:64 * hp + 64, c, :], reads=[self.Sret.bs[c]])
                self.out_events.append(ev)
        self.store_wkv(self.Hwkv.t, self.d_wkv_p, None)

    def bc4(self, col, nt):
        return self.vec.t[:, col:col + 4].unsqueeze(2).to_broadcast([128, 4, nt])

    def rwkv(self, blk, i, c0, nt, kind, last=False):
        S = self.S
        cfg = self.cfg
        sample = kind == "sample"
        Ls = LS if sample else nt
        nlev = int(np.ceil(np.log2(Ls)))
        nseq = NSEQ if sample else 1
        pT, pm, rw = self.pT, self.pm, self.rw
        wmask = self.wmask_s if sample else self.wmask
        V = self.vec
        pcur = pT.t[:, :, c0 + 1:c0 + nt + 1]
        if sample:
            S.dma(self.q_misc, self.shs.t[:, :], self.d_st_shift[:, :], writes=[self.shs.b])
            pz = self.ps()
            for ch in range(14):
                self.tr(pz.t[:, ch * NSEQ:(ch + 1) * NSEQ], self.shs.t[0:NSEQ, ch * 128:(ch + 1) * 128], self.ident.t[0:NSEQ, 0:NSEQ],
                        reads=[self.shs.b, self.ident.b], writes=[pz.b], signal=(ch == 13))
            psh4 = self.psh.t[:, :, :].rearrange("p c (s j) -> p c s j", j=LS)
            self.cp(psh4[:, :, :, 0], pz.t[:, 0:14 * NSEQ].rearrange("p (c s) -> p c s", s=NSEQ), reads=[pz.b], writes=[self.psh.b])
            pc4 = pcur.rearrange("p c (s j) -> p c s j", j=LS)
            self.cp(psh4[:, :, :, 1:LS], pc4[:, :, :, 0:LS - 1], reads=[pT.b], writes=[self.psh.b])
            pprev_ap = self.psh.t[:, :, 0:nt]
            prev_reads = [self.psh.b]
        else:
            pprev_ap = pT.t[:, :, c0:c0 + nt]
            prev_reads = [pT.b]
        mu0 = self.vcol("mu")
        pmv = pm.t[:, :, 0:nt]
        self.tt(pmv, pprev_ap, pcur, ALU.subtract, reads=prev_reads + [pT.b], writes=[pm.b])
        self.tt(pmv, pmv, V.t[:, mu0:mu0 + 14].unsqueeze(2).to_broadcast([128, 14, nt]), ALU.mult, reads=[pm.b, V.b], writes=[pm.b])
        self.tt(pmv, pmv, pcur, ALU.add, reads=[pm.b, pT.b], writes=[pm.b])
        r_ = pm.t[:, 0:4, 0:nt]
        k_ = pm.t[:, 4:8, 0:nt]
        v_ = pm.t[:, 8:12, 0:nt]

        def R(nm):
            return rw[nm].t[:, :, 0:nt]

        def B(*nms):
            return [rw[n].b for n in nms]

        yield
        self.act(self.thw.t[:, 0:nt], pm.t[:, 12, 0:nt], AF.Tanh, reads=[pm.b], writes=[self.thw.b])
        pz = self.ps()
        for c in range(4):
            self.mm(pz.t[:, c * 128:c * 128 + nt], self.w2t.t[:, c * 128:(c + 1) * 128], self.thw.t[:, 0:nt], True, True,
                    reads=[self.w2t.b, self.thw.b], writes=[pz.b], signal=(c == 3))
        w0c = self.vcol("w0")
        for c in range(4):
            self.act(rw["logw"].t[:, c, 0:nt], pz.t[:, c * 128:c * 128 + nt], AF.Sigmoid, reads=[pz.b, V.b], writes=B("logw"),
                     bias=V.t[:, w0c + c:w0c + c + 1])
        self.ts(R("logw"), R("logw"), -float(np.exp(-0.5)), None, ALU.mult, None, reads=B("logw"), writes=B("logw"))
        yield
        pa = self.ps()
        for c in range(4):
            self.mm(pa.t[:, c * 128:c * 128 + nt], self.a2t.t[:, c * 128:(c + 1) * 128], pm.t[:, 12, 0:nt], True, True,
                    reads=[self.a2t.b, pm.b], writes=[pa.b], signal=(c == 3))
        a0c = self.vcol("a0")
        for c in range(4):
            self.act(rw["a"].t[:, c, 0:nt], pa.t[:, c * 128:c * 128 + nt], AF.Sigmoid, reads=[pa.b, V.b], writes=B("a"),
                     bias=V.t[:, a0c + c:a0c + c + 1])
        yield
        self.act(self.sgx.t[:, 0:nt], pm.t[:, 13, 0:nt], AF.Sigmoid, reads=[pm.b], writes=[self.sgx.b])
        pg = self.ps()
        for c in range(4):
            self.mm(pg.t[:, c * 128:c * 128 + nt], self.g2t.t[:, c * 128:(c + 1) * 128], self.sgx.t[:, 0:nt], True, True,
                    reads=[self.g2t.b, self.sgx.b], writes=[pg.b], signal=(c == 3))
        self.cp(R("g"), pg.t[:, :].rearrange("p (c t) -> p c t", t=128)[:, :, 0:nt], reads=[pg.b], writes=B("g"), eng="act")
        yield
        self.tt(R("kk"), k_, self.bc4(self.vcol("k_k"), nt), ALU.mult, reads=[pm.b, V.b], writes=B("kk"))
        self.tt(R("tmp"), R("kk"), R("kk"), ALU.mult, reads=B("kk"), writes=B("tmp"))
        pn = self.ps()
        for c in range(4):
            self.mm(pn.t[:, c * 128:c * 128 + nt], self.blockones.t[:, :], rw["tmp"].t[:, c, 0:nt], True, True,
                    reads=[self.blockones.b, rw["tmp"].b], writes=[pn.b], signal=(c == 3))
        self.ts(R("tmp"), pn.t[:, :].rearrange("p (c t) -> p c t", t=128)[:, :, 0:nt], 1e-24, None, ALU.max, None,
                reads=[pn.b], writes=B("tmp"))
        self.act(R("tmp"), R("tmp"), AF.Sqrt, reads=B("tmp"), writes=B("tmp"))
        S.op("dve", lambda e: e.reciprocal(out=R("tmp"), in_=R("tmp")), reads=B("tmp"), writes=B("tmp"))
        self.tt(R("kk"), R("kk"), R("tmp"), ALU.mult, reads=B("kk", "tmp"), writes=B("kk"))
        yield
        self.tt(R("tmp"), R("a"), self.bc4(self.vcol("k_a"), nt), ALU.mult, reads=B("a") + [V.b], writes=B("tmp"))
        self.tt(R("tmp"), R("tmp"), self.omka.t[:, :].unsqueeze(2).to_broadcast([128, 4, nt]), ALU.add,
                reads=B("tmp") + [self.omka.b], writes=B("tmp"))
        self.tt(R("km"), k_, R("tmp"), ALU.mult, reads=[pm.b] + B("tmp"), writes=B("km"))
        self.tt(R("b"), R("kk"), R("a"), ALU.mult, reads=B("kk", "a"), writes=B("b"))
        yield
        self.tt(R("tmp"), r_, R("km"), ALU.mult, reads=[pm.b] + B("km"), writes=B("tmp"))
        self.tt(R("tmp"), R("tmp"), self.bc4(self.vcol("r_k"), nt), ALU.mult, reads=B("tmp") + [V.b], writes=B("tmp"))
        pb_ = self.ps()
        for c in range(4):
            self.mm(pb_.t[:, c * 128:c * 128 + nt], self.blockones.t[:, :], rw["tmp"].t[:, c, 0:nt], True, True,
                    reads=[self.blockones.b, rw["tmp"].b], writes=[pb_.b], signal=(c == 3))
        self.tt(R("bon"), pb_.t[:, :].rearrange("p (c t) -> p c t", t=128)[:, :, 0:nt], v_, ALU.mult, reads=[pb_.b, pm.b], writes=B("bon"))
        yield
        for c in range(4):
            S.op("dve", lambda e, c=c: e.tensor_tensor_scan(out=rw["cum"].t[:, c, 0:nt], data0=self.onesf.t[:, 0:nt],
                                                         data1=rw["logw"].t[:, c, 0:nt], initial=0.0, op0=ALU.mult, op1=ALU.add),
                 reads=B("logw") + [self.onesf.b], writes=B("cum"))
        gc = self.gcum
        if sample:
            cum4 = rw["cum"].t[:, :, 0:nt].rearrange("p c (s j) -> p c s j", j=LS)
            S.op("dve", lambda e: e.memset(gc.t[:, :, 0:1], 0.0), writes=[gc.b])
            self.cp(gc.t[:, :, 1:NSEQ], cum4[:, :, 0:NSEQ - 1, LS - 1], reads=B("cum"), writes=[gc.b])
            self.tt(cum4, cum4, gc.t[:, :, :].unsqueeze(3).to_broadcast([128, 4, NSEQ, LS]), ALU.subtract,
                    reads=B("cum") + [gc.b], writes=B("cum"))
        self.act(R("eg"), R("cum"), AF.Exp, reads=B("cum"), writes=B("eg"))
        self.tt(R("tmp"), R("cum"), R("logw"), ALU.subtract, reads=B("cum", "logw"), writes=B("tmp"))
        self.act(R("ee"), R("tmp"), AF.Exp, reads=B("tmp"), writes=B("ee"))
        self.act(R("ei"), R("cum"), AF.Exp, reads=B("cum"), writes=B("ei"), scale=-1.0)
        yield
        if sample:
            eg4 = rw["eg"].t[:, :, 0:nt].rearrange("p c (s j) -> p c s j", j=LS)
            self.cp(gc.t[:, :, :], eg4[:, :, :, LS - 1], reads=B("eg"), writes=[gc.b])
        else:
            self.cp(gc.t[:, :, 0:1], rw["eg"].t[:, :, nt - 1:nt], reads=B("eg"), writes=[gc.b])
        self.tt(R("rt"), r_, R("eg"), ALU.mult, reads=[pm.b] + B("eg"), writes=B("eg"))
        self.tt(R("kt"), R("km"), R("ei"), ALU.mult, reads=B("km", "ei"), writes=B("km"))
        self.tt(R("bt"), R("b"), R("ei"), ALU.mult, reads=B("b", "ei"), writes=B("b"))
        self.stt(R("at"), R("kk"), -1.0, R("ee"), ALU.mult, ALU.mult, reads=B("kk", "ee"), writes=B("ee"))
        yield
        for src, srcb, dstt in ((v_, pm.b, self.vtk), (R("kt"), rw["kt"].b, self.ktok), (R("bt"), rw["bt"].b, self.btok)):
            pq = self.ps()
            for c in range(4):
                self.tr(pq.t[0:nt, c * 128:(c + 1) * 128], src[:, c, :], self.ident.t[:, :],
                        reads=[srcb, self.ident.b], writes=[pq.b], signal=(c == 3))
            self.cp(dstt.t[0:nt, :], pq.t[0:nt, :], reads=[pq.b], writes=[dstt.b], eng="act")
        vtk, ktok, btok = self.vtk, self.ktok, self.btok
        rt, kt, bt, at = rw["rt"], rw["kt"], rw["bt"], rw["at"]
        py = self.ps(hold=True)
        idn = self.ident.t[0:nt, 0:nt]
        def pair(c):
            if sample:
                for hd in range(2):
                    S.dma(self.q_misc, self.wkin.t[:, :, hd, :], self.d_st_wkv[:, 2 * c + hd, :, :].rearrange("s v k -> v s k"),
                          writes=[self.wkin.b])
                for s0 in range(0, NSEQ, 8):
                    p = self.ps()
                    for s in range(s0, s0 + 8):
                        self.tr(p.t[:, (s - s0) * 64:(s - s0 + 1) * 64], self.wkin.t[:, s, :, :].rearrange("v h k -> v (h k)"),
                                self.ident.t[0:64, 0:64], reads=[self.wkin.b, self.ident.b], writes=[p.b], signal=(s == s0 + 7))
                    self.cp(self.Hwkv_s.t[:, s0:s0 + 8, :], p.t[:, :].rearrange("p (s v) -> p s v", v=64), reads=[p.b],
                            writes=[self.Hwkv_s.b], eng="act")
                Hb = self.Hwkv_s.b
            else:
                Hb = self.Hwkv.bs[c]
            mAs, mBs, mPs = [], [], []
            atm = _rr(self.atms, self.am_i)
            rtm = _rr(self.rtms, self.am_i)
            self.am_i += 1
            hsel = self.hsel
            for hd in range(2):
                self.ts(atm.t[:, hd, 0:nt], at.t[:, c, 0:nt], hsel.t[:, hd:hd + 1], None, ALU.mult, None,
                        reads=[at.b, hsel.b], writes=[atm.b])
                self.ts(rtm.t[:, hd, 0:nt], rt.t[:, c, 0:nt], hsel.t[:, hd:hd + 1], None, ALU.mult, None,
                        reads=[rt.b, hsel.b], writes=[rtm.b])
            for hd in range(2):
                p1 = self.ps()
                pb2 = self.ps()
                A_ = atm.t[:, hd, 0:nt]
                R_ = rtm.t[:, hd, 0:nt]
                pairs = ((bt.t[:, c, 0:nt], A_, [bt.b, atm.b]), (A_, bt.t[:, c, 0:nt], [bt.b, atm.b]),
                         (kt.t[:, c, 0:nt], A_, [kt.b, atm.b]), (bt.t[:, c, 0:nt], R_, [bt.b, rtm.b]))
                for q, (l_, r2, rd) in enumerate(pairs):
                    self.mm(p1.t[0:nt, q * 128:q * 128 + nt], l_, r2, True, True,
                            reads=rd, writes=[p1.b], signal=(q == 3))
                self.mm(pb2.t[0:nt, hd * 128:hd * 128 + nt], kt.t[:, c, 0:nt], R_, True, True,
                        reads=[kt.b, rtm.b], writes=[pb2.b], signal=True)
                mA = _rr(self.mA, self.mA_i)
                self.mA_i += 1
                mB = _rr(self.mB, self.mB_i)
                self.mB_i += 1
                self.tt(mA.t[0:nt, :, 0:nt], p1.t[0:nt, :].rearrange("p (q t) -> p q t", t=128)[:, :, 0:nt], wmask.t[0:nt, 0:4, 0:nt],
                        ALU.mult, reads=[p1.b, wmask.b], writes=[mA.b])
                self.tt(mB.t[0:nt, 0:nt], pb2.t[0:nt, hd * 128:hd * 128 + nt], wmask.t[0:nt, 4, 0:nt], ALU.mult,
                        reads=[pb2.b, wmask.b], writes=[mB.b])
                if Builder.BF16_CHAIN:
                    mP = _rr(self.mP, self.mP_i)
                    self.mP_i += 1
                    self.cp(mP.t[0:nt, :, 0:nt], mA.t[0:nt, 0:2, 0:nt], reads=[mA.b], writes=[mP.b], eng="act")
                    mPs.append(mP)
                mAs.append(mA)
                mBs.append(mB)
                yield
            yield
            px = self.ps()
            for hd in range(2):
                h = 2 * c + hd
                sl = slice(64 * hd, 64 * hd + 64)
                o = px.t[0:nt, hd * 64:(hd + 1) * 64]
                if sample:
                    self.tt(self.qm.t[:, :, :], atm.t[:, hd, 0:nt].unsqueeze(1).to_broadcast([128, NSEQ, 64]), self.seqselT.t[:, :, :],
                            ALU.mult, reads=[atm.b, self.seqselT.b], writes=[self.qm.b])
                    for s in range(NSEQ):
                        self.mm(o, self.qm.t[:, s, :], self.Hwkv_s.t[:, s, :], s == 0, False,
                                reads=[self.qm.b, Hb], writes=[px.b], signal=False)
                else:
                    self.mm(o, atm.t[:, hd, 0:nt], self.Hwkv.t[:, c, :], True, False, reads=[atm.b, Hb], writes=[px.b], signal=False)
                self.mm(o, mAs[hd].t[0:nt, 2, 0:nt], vtk.t[0:nt, h * 64:(h + 1) * 64], False, True,
                        reads=[mAs[hd].b, vtk.b], writes=[px.b], signal=True)
            bfc = Builder.BF16_CHAIN
            if bfc:
                U = _rr(self.Ub, self.Ub_i)
                self.Ub_i += 1
            else:
                U = _rr(self.Up, self.Up_i)
                self.Up_i += 1
            self.cp(U.t[0:nt, :], px.t[0:nt, 0:128], reads=[px.b], writes=[U.b], eng="act")
            if bfc:
                PT = [mPs[0].t[0:nt, 0, 0:nt], mPs[1].t[0:nt, 0, 0:nt]]
                Pm = [mPs[0].t[0:nt, 1, 0:nt], mPs[1].t[0:nt, 1, 0:nt]]
                Pb = [[mPs[0].b], [mPs[1].b]]
                idl, idb = self.identb.t[0:nt, 0:nt], self.identb.b
            else:
                PT = [mAs[0].t[0:nt, 0, 0:nt], mAs[1].t[0:nt, 0, 0:nt]]
                Pm = [mAs[0].t[0:nt, 1, 0:nt], mAs[1].t[0:nt, 1, 0:nt]]
                Pb = [[mAs[0].b], [mAs[1].b]]
                idl, idb = idn, self.ident.b
            for lv in range(nlev):
                pu = self.ps()
                for hd in range(2):
                    o = pu.t[0:nt, hd * 64:(hd + 1) * 64]
                    self.mm(o, idl, U.t[0:nt, hd * 64:(hd + 1) * 64], True, False, reads=[idb, U.b], writes=[pu.b], signal=False)
                    self.mm(o, PT[hd], U.t[0:nt, hd * 64:(hd + 1) * 64], False, True, reads=Pb[hd] + [U.b], writes=[pu.b], signal=(hd == 1))
                if bfc and lv < nlev - 1:
                    Un = _rr(self.Ub, self.Ub_i)
                    self.Ub_i += 1
                else:
                    Un = _rr(self.Up, self.Up_i)
                    self.Up_i += 1
                self.cp(Un.t[0:nt, :], pu.t[0:nt, 0:128], reads=[pu.b], writes=[Un.b], eng="act")
                U = Un
                yield
                if lv < nlev - 1:
                    pp = self.ps()
                    for hd in range(2):
                        self.mm(pp.t[0:nt, (2 * hd) * 128:(2 * hd) * 128 + nt], Pm[hd], PT[hd], True, True,
                                reads=Pb[hd], writes=[pp.b], signal=False)
                        self.mm(pp.t[0:nt, (2 * hd + 1) * 128:(2 * hd + 1) * 128 + nt], PT[hd], Pm[hd], True, True,
                                reads=Pb[hd], writes=[pp.b], signal=(hd == 1))
                    if bfc:
                        PPt = _rr(self.PPb, self.PPb_i)
                        self.PPb_i += 1
                    else:
                        PPt = _rr(self.PP, self.PP_i)
                        self.PP_i += 1
                    self.cp(PPt.t[0:nt, :, 0:nt], pp.t[0:nt, :].rearrange("p (q t) -> p q t", t=128)[:, :, 0:nt],
                            reads=[pp.b], writes=[PPt.b], eng=Builder.PP_ENG)
                    PT = [PPt.t[0:nt, 0, 0:nt], PPt.t[0:nt, 2, 0:nt]]
                    Pm = [PPt.t[0:nt, 1, 0:nt], PPt.t[0:nt, 3, 0:nt]]
                    Pb = [[PPt.b], [PPt.b]]
                    yield
            yield
            for hd in range(2):
                h = 2 * c + hd
                sl = slice(64 * hd, 64 * hd + 64)
                o = py.t[0:nt, h * 64:(h + 1) * 64]
                if sample:
                    self.tt(self.qm2.t[:, :, :], rtm.t[:, hd, 0:nt].unsqueeze(1).to_broadcast([128, NSEQ, 64]), self.seqselT.t[:, :, :],
                            ALU.mult, reads=[rtm.b, self.seqselT.b], writes=[self.qm2.b])
                    for s in range(NSEQ):
                        self.mm(o, self.qm2.t[:, s, :], self.Hwkv_s.t[:, s, :], s == 0, False,
                                reads=[self.qm2.b, Hb], writes=[py.b], signal=False)
                else:
                    self.mm(o, rtm.t[:, hd, 0:nt], self.Hwkv.t[:, c, :], True, False, reads=[rtm.b, Hb], writes=[py.b], signal=False)
                self.mm(o, mAs[hd].t[0:nt, 3, 0:nt], U.t[0:nt, hd * 64:(hd + 1) * 64], False, False,
                        reads=[mAs[hd].b, U.b], writes=[py.b], signal=False)
                self.mm(o, mBs[hd].t[0:nt, 0:nt], vtk.t[0:nt, h * 64:(h + 1) * 64], False, True,
                        reads=[mBs[hd].b, vtk.b], writes=[py.b], signal=True)
            yield
            if not sample:
                ph = self.ps()
                self.mm(ph.t[:, 0:128], btok.t[0:nt, c * 128:(c + 1) * 128], U.t[0:nt, :], True, False,
                        reads=[btok.b, U.b], writes=[ph.b], signal=False)
                self.mm(ph.t[:, 0:128], ktok.t[0:nt, c * 128:(c + 1) * 128], vtk.t[0:nt, c * 128:(c + 1) * 128], False, True,
                        reads=[ktok.b, vtk.b], writes=[ph.b], signal=True)
                for hd in range(2):
                    sl = slice(64 * hd, 64 * hd + 64)
                    self.tt(self.Hwkv.t[sl, c, :], self.Hwkv.t[sl, c, :], ph.t[sl, 64 * hd:64 * hd + 64], ALU.add,
                            reads=[Hb, ph.b], writes=[Hb])
                    self.ts(self.Hwkv.t[sl, c, :], self.Hwkv.t[sl, c, :], gc.t[sl, c, 0:1], None, ALU.mult, None,
                            reads=[Hb, gc.b], writes=[Hb])
            else:
                self.tt(self.km.t[:, :, :], btok.t[0:64, c * 128:(c + 1) * 128].unsqueeze(1).to_broadcast([64, NSEQ, 128]),
                        self.seqsel.t[:, :].unsqueeze(2).to_broadcast([64, NSEQ, 128]), ALU.mult,
                        reads=[btok.b, self.seqsel.b], writes=[self.km.b])
                self.tt(self.km2.t[:, :, :], ktok.t[0:64, c * 128:(c + 1) * 128].unsqueeze(1).to_broadcast([64, NSEQ, 128]),
                        self.seqsel.t[:, :].unsqueeze(2).to_broadcast([64, NSEQ, 128]), ALU.mult,
                        reads=[ktok.b, self.seqsel.b], writes=[self.km2.b])
                for s0 in range(0, NSEQ, 4):
                    ph = self.ps()
                    for s in range(s0, s0 + 4):
                        o = ph.t[:, (s - s0) * 128:(s - s0 + 1) * 128]
                        self.mm(o, self.km.t[:, s, :], U.t[0:nt, :], True, False, reads=[self.km.b, U.b], writes=[ph.b], signal=False)
                        self.mm(o, self.km2.t[:, s, :], vtk.t[0:nt, c * 128:(c + 1) * 128], False, True,
                                reads=[self.km2.b, vtk.b], writes=[ph.b], signal=(s == s0 + 3))
                    for hd in range(2):
                        sl = slice(64 * hd, 64 * hd + 64)
                        hv = self.Hwkv_s.t[sl, s0:s0 + 4, :]
                        pv_ = ph.t[sl, :].rearrange("p (s x) -> p s x", x=128)[:, :, 64 * hd:64 * hd + 64]
                        self.tt(hv, hv, pv_, ALU.add, reads=[Hb, ph.b], writes=[Hb])
                        self.tt(hv, hv, gc.t[sl, c, s0:s0 + 4].unsqueeze(2).to_broadcast([64, 4, 64]), ALU.mult,
                                reads=[Hb, gc.b], writes=[Hb])
                for s0 in range(0, NSEQ, 4):
                    p = self.ps()
                    for s in range(s0, s0 + 4):
                        self.tr(p.t[0:64, (s - s0) * 128:(s - s0 + 1) * 128], self.Hwkv_s.t[:, s, :], self.ident.t[:, :],
                                reads=[Hb, self.ident.b], writes=[p.b], signal=(s == s0 + 3))
                    self.cp(self.wkin.t[:, s0:s0 + 4, :, :].rearrange("v s h k -> v s (h k)"),
                            p.t[0:64, :].rearrange("v (s x) -> v s x", x=128), reads=[p.b], writes=[self.wkin.b], eng="act")
                for hd in range(2):
                    ev = S.dma(self.q_misc, self.d_wkv_s[:, 2 * c + hd, :, :].rearrange("s v k -> v s k"), self.wkin.t[:, :, hd, :],
                               reads=[self.wkin.b])
                    self.out_events.append(ev)

        if sample or not Builder.PAIR_IL:
            for c in range(4):
                yield from pair(c)
        else:
            for cs in (((0, 1, 2, 3),) if Builder.NPAIR_IL == 4 else ((0, 1), (2, 3))):
                gens = [pair(c) for c in cs]
                alive = [True] * len(gens)
                while any(alive):
                    for gi, g_ in enumerate(gens):
                        if alive[gi]:
                            try:
                                next(g_)
                            except StopIteration:
                                alive[gi] = False
                    yield
        yield
        ysb, ysq, yst = self.ysb, self.ysq, self.yst
        self.cp(ysb.t[0:nt, :], py.t[0:nt, :], reads=[py.b], writes=[ysb.b], eng="act")
        self.release(py)
        y3 = ysb.t[0:nt, :].rearrange("p (h v) -> p h v", v=64)
        q3 = ysq.t[0:nt, :].rearrange("p (h v) -> p h v", v=64)
        S.op("dve", lambda e: e.tensor_reduce(out=yst.t[0:nt, 0, :], in_=y3, axis=AX.X, op=ALU.add), reads=[ysb.b], writes=[yst.b])
        self.ts(yst.t[0:nt, 1, :], yst.t[0:nt, 0, :], -1.0 / 64, None, ALU.mult, None, reads=[yst.b], writes=[yst.b])
        self.tt(y3, y3, yst.t[0:nt, 1, :].unsqueeze(2).to_broadcast([nt, 8, 64]), ALU.add, reads=[ysb.b, yst.b], writes=[ysb.b])
        self.act(ysq.t[0:nt, :], ysb.t[0:nt, :], AF.Square, reads=[ysb.b], writes=[ysq.b])
        S.op("dve", lambda e: e.tensor_reduce(out=yst.t[0:nt, 2, :], in_=q3, axis=AX.X, op=ALU.add), reads=[ysq.b], writes=[yst.b])
        self.act(yst.t[0:nt, 3, :], yst.t[0:nt, 2, :], AF.Sqrt, reads=[yst.b, self.eps_norm.b], writes=[yst.b],
                 bias=self.eps_norm.t[0:nt, 2:3], scale=1.0 / 64)
        S.op("dve", lambda e: e.reciprocal(out=yst.t[0:nt, 3, :], in_=yst.t[0:nt, 3, :]), reads=[yst.b], writes=[yst.b])
        self.tt(y3, y3, yst.t[0:nt, 3, :].unsqueeze(2).to_broadcast([nt, 8, 64]), ALU.mult, reads=[ysb.b, yst.b], writes=[ysb.b])
        pt = self.ps()
        for c in range(4):
            self.tr(pt.t[:, c * 128:c * 128 + nt], ysb.t[0:nt, c * 128:(c + 1) * 128], idn,
                    reads=[ysb.b, self.ident.b], writes=[pt.b], signal=(c == 3))
        self.tt(R("tmp"), pt.t[:, :].rearrange("p (c t) -> p c t", t=128)[:, :, 0:nt], self.bc4(self.vcol("lnx_g"), nt), ALU.mult,
                reads=[pt.b, V.b], writes=B("tmp"))
        self.tt(R("tmp"), R("tmp"), self.bc4(self.vcol("lnx_b"), nt), ALU.add, reads=B("tmp") + [V.b], writes=B("tmp"))
        self.tt(R("tmp"), R("tmp"), R("bon"), ALU.add, reads=B("tmp", "bon"), writes=B("tmp"))
        self.tt(self.orwT.t[:, :, c0:c0 + nt], R("tmp"), R("g"), ALU.mult, reads=B("tmp", "g"), writes=self.orwT.bs)
        yield
        if sample:
            pz = self.ps()
            pc4 = pcur.rearrange("p c (s j) -> p c s j", j=LS)
            for ch in range(14):
                self.tr(pz.t[0:NSEQ, (ch % 4) * 128:(ch % 4 + 1) * 128], pc4[:, ch, :, LS - 1], self.ident.t[:, :],
                        reads=[pT.b, self.ident.b], writes=[pz.b], signal=(ch % 4 == 3 or ch == 13))
                if ch % 4 == 3 or ch == 13:
                    c_lo = ch - (ch % 4)
                    self.cp(self.shs.t[0:NSEQ, c_lo * 128:(ch + 1) * 128], pz.t[0:NSEQ, 0:(ch - c_lo + 1) * 128], reads=[pz.b],
                            writes=[self.shs.b], eng="act")
                    if ch != 13:
                        pz = self.ps()
            ev = S.dma(self.q_misc, self.d_shift_s[:, :], self.shs.t[0:NSEQ, :], reads=[self.shs.b])
            self.out_events.append(ev)
        else:
            is_last_prompt_tile_of_block = (kind == "meta") or (c0 + nt == self.TBcur)
            if is_last_prompt_tile_of_block:
                self.cp(self.pprev.t[:, :], pT.t[:, :, c0 + nt], reads=[pT.b], writes=[self.pprev.b], eng="act")

    def store_wkv(self, H, dst, s):
        S = self.S
        p = self.ps()
        for c in range(4):
            self.tr(p.t[0:64, c * 128:(c + 1) * 128], self.Hwkv.t[:, c, :], self.ident.t[:, :],
                    reads=[self.Hwkv.bs[c], self.ident.b], writes=[p.b], signal=(c == 3))
        self.cp(self.ysb.t[0:64, :], p.t[0:64, :], reads=[p.b], writes=[self.ysb.b], eng="act")
        ev = S.dma(self.q_misc, self.d_wkv_p.rearrange("h v k -> v h k"), self.ysb.t[0:64, :].rearrange("v (h k) -> v h k", k=64),
                   reads=[self.ysb.b])
        self.out_events.append(ev)
        p2 = self.ps()
        self.tr(p2.t[0:14, 0:128], self.pprev.t[:, :], self.ident.t[:, :], reads=[self.pprev.b, self.ident.b], writes=[p2.b])
        self.cp(self.osq.t[0:14, 0:128], p2.t[0:14, 0:128], reads=[p2.b], writes=[self.osq.b], eng="act")
        ev = S.dma(self.q_misc, self.d_shift_p[:, :], self.osq.t[0:14, 0:128], reads=[self.osq.b])
        self.out_events.append(ev)


def pack_vecs(cfg, inp):
    def fm(v):
        v = np.asarray(v, np.float32).reshape(-1)
        return v.reshape(-1, 128).T
    cols = [fm(inp["ffn1_norm"]), fm(inp["mix_norm"]), fm(inp["ffn2_norm"]), fm(inp["final_norm"]),
            fm(inp["mu_shift"]), fm(inp["w0"]), fm(inp["a0"]), fm(inp["k_k"]), fm(inp["k_a"]),
            fm(inp["r_k"]), fm(inp["lnx_g"]), fm(inp["lnx_b"])]
    return np.ascontiguousarray(np.concatenate(cols, axis=1), dtype=np.float32)


def make_in_maps(cfg, inp, n_cores):
    consts = host_constants(cfg)
    vecs = pack_vecs(cfg, inp)
    gng = np.ascontiguousarray(np.broadcast_to(np.asarray(inp["ret_gn_g"], np.float32)[None, :], (128, 512)))
    maps = []
    for c in range(n_cores):
        m = {
            "xp": np.ascontiguousarray(inp["x_prompt"][c]),
            "xs": np.ascontiguousarray(inp["x_sample"][c * NSEQ:(c + 1) * NSEQ].reshape(NSEQ * LS, cfg.DM)),
            "meta": np.ascontiguousarray(inp["meta_tokens"]),
            "st_ret": np.ascontiguousarray(inp["state_ret"][c * NSEQ:(c + 1) * NSEQ]),
            "st_wkv": np.ascontiguousarray(inp["state_wkv"][c * NSEQ:(c + 1) * NSEQ]),
            "st_shift": np.ascontiguousarray(inp["state_shift"][c * NSEQ:(c + 1) * NSEQ]),
            "vecs": vecs, "gng": gng,
            "w2": np.ascontiguousarray(inp["w2"]), "a2": np.ascontiguousarray(inp["a2"]),
            "g2": np.ascontiguousarray(inp["g2"]),
        }
        for k in Builder.WEIGHTS:
            m[k] = np.ascontiguousarray(inp[k], dtype=np.float32)
        for k, v in consts.items():
            if not k.startswith("_"):
                m["c_" + k] = v
        maps.append(m)
    return maps


def gather_outputs(cfg, res, n_cores):
    B = n_cores
    yp = np.stack([res[c]["yp"] for c in range(B)]).reshape(B, 128 * cfg.NT, cfg.DM)
    ys = np.concatenate([res[c]["ys"].reshape(NSEQ, LS, cfg.DM) for c in range(B)])
    ret_p = np.stack([res[c]["ret_p"] for c in range(B)])
    wkv_p = np.stack([res[c]["wkv_p"] for c in range(B)])
    shift_p = np.stack([res[c]["shift_p"].reshape(SHIFT_W) for c in range(B)])
    ret_s = np.concatenate([res[c]["ret_s"] for c in range(B)])
    wkv_s = np.concatenate([res[c]["wkv_s"] for c in range(B)])
    shift_s = np.concatenate([res[c]["shift_s"] for c in range(B)])
    return tuple(np.ascontiguousarray(a, dtype=np.float32) for a in (yp, ys, ret_p, wkv_p, shift_p, ret_s, wkv_s, shift_s))


def kernel(**inputs):
    cfg = Cfg()
    inp = {k: np.asarray(v) for k, v in inputs.items()}
    nc = Builder(cfg).build()
    in_maps = make_in_maps(cfg, inp, N_CORES)
    res = run_bass_kernel_spmd(nc, in_maps, core_ids=list(range(N_CORES)))
    return gather_outputs(cfg, res.results, N_CORES)
```

```python
import contextlib
import numpy as np
import concourse.bass as bass
import concourse.mybir as mybir
from concourse.bass_utils import run_bass_kernel_spmd

F32 = mybir.dt.float32
BF16 = mybir.dt.bfloat16
ALU = mybir.AluOpType
AF = mybir.ActivationFunctionType
AX = mybir.AxisListType

N_CORES = 8
N_META = 16
RET_HEADS, RET_DK, RET_DV = 4, 64, 128
RWKV_HEADS, RWKV_HD = 8, 64
RWKV_W = 512
SHIFT_W = 1792
ROPE_BASE = 10000.0
NORM_EPS = 1e-6
RET_GN_EPS = 1e-6
RWKV_GN_EPS = 64e-5
PAST_LEN = 16384
LS = 4
NSEQ = 16
SAME_ENGINE_SYNC = True


class Buf:
    __slots__ = ("w", "r")

    def __init__(self):
        self.w = None
        self.r = []


class Sched:
    ENG = ("pe", "act", "dve", "pool", "sp")

    def __init__(self, nc, es, n_dma_sems=20):
        self.nc = nc
        self.prog = {e: [] for e in self.ENG}
        self.cnt = {e: 0 for e in self.ENG}
        self.waited = {e: {} for e in self.ENG}
        self.sems = {}
        for e in self.ENG:
            self.sems[e] = es.enter_context(nc.semaphore("s_" + e))
        self.dma_sems = {}
        self.dma_rr = {}
        self.dma_uses = {}
        for q in ("sp", "pool", "act"):
            ks = []
            for i in range(n_dma_sems if q != "act" else 8):
                k = "d_%s_%d" % (q, i)
                self.sems[k] = es.enter_context(nc.semaphore(k))
                self.dma_uses[k] = 0
                ks.append(k)
            self.dma_sems[q] = ks
            self.dma_rr[q] = 0
        self.n_ops = 0

    def _deps(self, e, reads, writes):
        deps = {}

        def add(ev):
            if ev is None:
                return
            k, v = ev
            if deps.get(k, 0) < v:
                deps[k] = v

        for b in reads:
            add(b.w)
        for b in writes:
            add(b.w)
            for ev in b.r:
                add(ev)
        waits = []
        for k, v in deps.items():
            if k == e:
                if e == "pe" or not SAME_ENGINE_SYNC:
                    continue
                if v > self.cnt[e]:
                    continue
            if self.waited[e].get(k, 0) >= v:
                continue
            if k in self.cnt and k != e and v > self.cnt[k]:
                import traceback
                self.pending_waits = getattr(self, "pending_waits", 0) + 1
                if self.pending_waits <= 3:
                    print("WARNING: %s waits on unsignalled %s event %d (cnt %d)" % (e, k, v, self.cnt[k]))
                    traceback.print_stack(limit=6)
            self.waited[e][k] = v
            waits.append((k, v))
        return waits

    def op(self, e, fn, reads=(), writes=(), signal=True):
        waits = self._deps(e, reads, writes)
        ev = (e, self.cnt[e] + 1)
        if signal:
            self.cnt[e] += 1
        self.prog[e].append((waits, fn, (e, 1) if signal else None))
        for b in reads:
            b.r.append(ev)
        for b in writes:
            b.w = ev
            b.r = []
        self.n_ops += 1

    def dma(self, q, out, in_, reads=(), writes=()):
        waits = self._deps(q, reads, writes)
        ks = self.dma_sems[q]
        k = ks[self.dma_rr[q] % len(ks)]
        self.dma_rr[q] += 1
        prev = 16 * self.dma_uses[k]
        if prev and self.waited[q].get(k, 0) < prev:
            self.waited[q][k] = prev
            waits.append((k, prev))
        self.dma_uses[k] += 1
        ev = (k, 16 * self.dma_uses[k])
        self.prog[q].append((waits, lambda eng, o=out, i=in_: eng.dma_start(out=o, in_=i), (k, 16)))
        for b in reads:
            b.r.append(ev)
        for b in writes:
            b.w = ev
            b.r = []
        return ev

    def wait_event(self, e, ev):
        k, v = ev
        if self.waited[e].get(k, 0) < v:
            self.waited[e][k] = v
            self.prog[e].append(([(k, v)], None, None))

    def barrier(self):
        for e in self.ENG:
            for f in self.ENG:
                if f != e and self.cnt[f] > 0:
                    self.wait_event(e, (f, self.cnt[f]))
            for k, u in self.dma_uses.items():
                if u > 0:
                    self.wait_event(e, (k, 16 * u))

    def emit(self, block):
        nc = self.nc
        sems = self.sems

        def runner(name):
            def run(eng):
                for waits, fn, sig in self.prog[name]:
                    for k, v in waits:
                        eng.wait_ge(sems[k], v)
                    if fn is None:
                        continue
                    ins = fn(eng)
                    if sig is not None:
                        ins.then_inc(sems[sig[0]], sig[1])
            return run

        block.tensor(runner("pe"))
        block.scalar(runner("act"))
        block.vector(runner("dve"))
        block.gpsimd(runner("pool"))
        block.sync(runner("sp"))
        self._clear = True

    def clear(self):
        self.prog = {e: [] for e in self.ENG}


def _rr(lst, i):
    return lst[i % len(lst)]


class Cfg:
    def __init__(self, DM=1024, DFF=2816, NT=16):
        self.DM, self.DFF, self.NT = DM, DFF, NT
        self.KC = DM // 128
        self.FC = DFF // 128
        self.PROJ_W = 512 + 1024 + SHIFT_W + 2 * DM
        self.NSP = N_META + NSEQ * LS
        self.TB = 512
        assert NT % 2 == 0 or NT < 2
        self.TPB = min(2, NT)
        self.NBLK = 1 + NT // self.TPB
        self.NCOL = self.NSP + 128 * NT


def host_constants(cfg):
    f32 = np.float32
    c = {}
    c["ident"] = np.eye(128, dtype=f32)
    bo = np.zeros((128, 128), f32)
    bo[:64, :64] = 1.0
    bo[64:, 64:] = 1.0
    c["blockones"] = bo
    hs = np.zeros((128, 2), f32)
    hs[:64, 0] = 1.0
    hs[64:, 1] = 1.0
    c["hsel"] = hs
    P = np.zeros((128, 128), f32)
    for p in range(128):
        d = p % 64
        src = p + 32 if d < 32 else p - 32
        P[src, p] = 1.0
    c["rotP"] = P
    half = 32
    inv_freq = (f32(ROPE_BASE) ** (-np.arange(half, dtype=f32) / f32(half))).astype(f32)
    pos = np.concatenate([
        np.arange(N_META, dtype=np.int32),
        np.tile(PAST_LEN + np.arange(LS, dtype=np.int32), NSEQ),
        N_META + np.arange(128 * cfg.NT, dtype=np.int32),
    ])
    ang = pos.astype(f32)[None, :] * inv_freq[:, None]
    cs, sn = np.cos(ang).astype(f32), np.sin(ang).astype(f32)
    cosT = np.zeros((128, cfg.NCOL), f32)
    sinT = np.zeros((128, cfg.NCOL), f32)
    for p in range(128):
        d = p % 64
        j = d % 32
        cosT[p] = cs[j]
        sinT[p] = -sn[j] if d < 32 else sn[j]
    c["cosT"], c["sinT"] = cosT, sinT
    lg = np.log1p(-np.exp2(-5.0 - np.arange(RET_HEADS, dtype=f32))).astype(f32)
    c["_lg"] = lg
    idx = np.arange(128, dtype=f32)
    diff = idx[None, :] - idx[:, None]
    m = np.zeros((128, 4, 128), f32)
    for h in range(4):
        m[:, h, :] = np.where(diff >= 0, np.exp(lg[h] * np.maximum(diff, 0.0)), 0.0)
    c["rmaskT"] = m
    ms = np.zeros((64, 4, 64), f32)
    seq = np.arange(64) // LS
    same = (seq[:, None] == seq[None, :])
    d64 = (np.arange(64, dtype=f32)[None, :] - np.arange(64, dtype=f32)[:, None])
    for h in range(4):
        ms[:, h, :] = np.where((d64 >= 0) & same, np.exp(lg[h] * np.maximum(d64, 0.0)), 0.0)
    c["rmaskT_s"] = ms
    qd = np.zeros((128, 2, 128), f32)
    qds = np.zeros((128, 2, 64), f32)
    for ch in range(2):
        for hp in range(2):
            h = 2 * ch + hp
            qd[64 * hp:64 * hp + 64, ch, :] = np.exp(lg[h] * (idx + 1.0))[None, :]
            qds[64 * hp:64 * hp + 64, ch, :] = np.exp(lg[h] * ((np.arange(64) % LS) + 1.0))[None, :].astype(f32)
    c["qdec"], c["qdec_s"] = qd, qds
    kd = np.zeros((128, 3, 4), f32)
    for h in range(4):
        kd[:, 0, h] = np.exp(lg[h] * (127.0 - idx))
        kd[:16, 1, h] = np.exp(lg[h] * (15.0 - idx[:16]))
        kd[:64, 2, h] = np.exp(lg[h] * (LS - 1.0 - (np.arange(64) % LS)))
    c["kdec"] = kd
    su = (idx[:, None] < idx[None, :]).astype(f32)
    iu = (idx[:, None] <= idx[None, :]).astype(f32)
    sl = (idx[:, None] > idx[None, :]).astype(f32)
    wm = np.stack([su, sl, su, iu, iu, su], axis=1)
    c["wmask"] = wm.astype(f32)
    sm = same.astype(f32)
    wms = np.zeros((64, 6, 64), f32)
    for i, mm in enumerate([su, sl, su, iu, iu, su]):
        wms[:, i, :] = mm[:64, :64] * sm
    c["wmask_s"] = wms
    c["seqsel"] = (seq[:, None] == np.arange(NSEQ)[None, :]).astype(f32)
    c["seqselT"] = np.broadcast_to((np.arange(NSEQ)[:, None] == seq[None, :]).astype(f32)[None], (128, NSEQ, 64)).copy()
    return c


class Tile:
    def __init__(self, t, nb=1):
        self.t = t
        self.bs = [Buf() for _ in range(nb)]

    @property
    def b(self):
        return self.bs[0]


class Builder:
    D_PER_M = 1
    PAIR_IL = True
    NPAIR_IL = 2
    NW8 = 11
    SEP_RET = True
    BF16_CHAIN = True
    PP_ENG = "dve"

    def __init__(self, cfg, debug=()):
        self.cfg = cfg
        self.debug = debug
        self.dbg_outs = {}

    def sb(self, name, shape, dt=F32, nb=1):
        self._uid = getattr(self, "_uid", 0) + 1
        t = self.es.enter_context(self.nc.sbuf_tensor("%s_%d" % (name, self._uid), list(shape), dt))
        return Tile(t, nb)

    def dram_in(self, name, shape):
        return self.nc.dram_tensor(name, list(shape), F32, kind="ExternalInput").ap()

    def dram_out(self, name, shape):
        return self.nc.dram_tensor(name, list(shape), F32, kind="ExternalOutput").ap()

    def ps(self, hold=False):
        while True:
            p = self.psb[self.ps_i % len(self.psb)]
            self.ps_i += 1
            if id(p) not in self.held:
                break
        if hold:
            self.held.add(id(p))
        return p

    def release(self, p):
        self.held.discard(id(p))

    def mm(self, out, lhsT, rhs, start, stop, reads, writes, signal):
        self.S.op("pe", lambda e: e.matmul(out, lhsT, rhs, start=start, stop=stop),
                  reads=reads, writes=writes, signal=signal)

    def tr(self, out, in_, ident, reads, writes, signal=True):
        self.S.op("pe", lambda e: e.transpose(out, in_, ident), reads=reads, writes=writes, signal=signal)

    def act(self, out, in_, func, reads, writes, bias=None, scale=None):
        kw = {}
        if bias is not None:
            kw["bias"] = bias
        if scale is not None:
            kw["scale"] = scale
        self.S.op("act", lambda e: e.activation(out=out, in_=in_, func=func, **kw), reads=reads, writes=writes)

    def tt(self, out, in0, in1, op, reads, writes, eng="dve"):
        self.S.op(eng, lambda e: e.tensor_tensor(out=out, in0=in0, in1=in1, op=op), reads=reads, writes=writes)

    def ts(self, out, in0, s1, s2, op0, op1, reads, writes, eng="dve"):
        if op1 is None:
            self.S.op(eng, lambda e: e.tensor_scalar(out=out, in0=in0, scalar1=s1, scalar2=None, op0=op0),
                      reads=reads, writes=writes)
        else:
            self.S.op(eng, lambda e: e.tensor_scalar(out=out, in0=in0, scalar1=s1, scalar2=s2, op0=op0, op1=op1),
                      reads=reads, writes=writes)

    def stt(self, out, in0, scalar, in1, op0, op1, reads, writes, eng="dve"):
        self.S.op(eng, lambda e: e.scalar_tensor_tensor(out=out, in0=in0, scalar=scalar, in1=in1, op0=op0, op1=op1),
                  reads=reads, writes=writes)

    def cp(self, out, in_, reads, writes, eng="dve"):
        if eng == "act":
            self.S.op("act", lambda e: e.copy(out=out, in_=in_), reads=reads, writes=writes)
        else:
            self.S.op(eng, lambda e: e.tensor_copy(out=out, in_=in_), reads=reads, writes=writes)

    def dbg(self, name, tile_ap, shape, reads):
        if name not in self.debug:
            return
        d = self.dram_out("dbg_" + name, shape)
        self.dbg_outs[name] = shape
        ev = self.S.dma(self.q_misc, d, tile_ap, reads=reads, writes=[])
        self.out_events.append(ev)

    def _wsrc(self, name, nk, j, k0, k1):
        if self.use_scr or (name, j, k0) in self.scr_done:
            return self.d_scr[name][j, :, k0:k1, :], [self.scr_buf[(name, j, k0)]]
        W = self.d_w[name]
        return W[k0 * 128:k1 * 128, j * 128:(j + 1) * 128].rearrange("(k p) m -> p k m", p=128), []

    def _wload(self, t, name, nk, c0, dst3=None):
        j = c0 // 128
        for k0 in range(0, nk, 8):
            k1 = min(nk, k0 + 8)
            src, rb = self._wsrc(name, nk, j, k0, k1)
            dst = t.t[:, k0:k1, :] if dst3 is None else dst3[:, k0:k1, :]
            from_scr = self.use_scr or (name, j, k0) in self.scr_done
            q = self.q_w if self.use_scr else "pool"
            self.S.dma(q, dst, src, reads=rb, writes=[t.b])
            if self.d_scr and not from_scr:
                key = (name, j, k0)
                self.scr_done.add(key)
                self.S.dma("sp", self.d_scr[name][j, :, k0:k1, :], dst, reads=[t.b], writes=[self.scr_buf[key]])

    def load_w8(self, name, nk, c0):
        t = _rr(self.w8, self.w8_i)
        self.w8_i += 1
        self._wload(t, name, nk, c0)
        return t

    def load_wd(self, name, nk, c0):
        t = _rr(self.wd, self.wd_i)
        self.wd_i += 1
        self._wload(t, name, nk, c0)
        return t

    def load_wt(self, name, nk, c0):
        t = _rr(self.wt, self.wt_i)
        self.wt_i += 1
        for q in range(4):
            self._wload(t, name, nk, c0 + q * 128, dst3=t.t[:, :, q * 128:(q + 1) * 128])
        return t

    def conv_phase_b(self):
        cfg = self.cfg
        names = ["w_out_ret", "w_out_rwkv", "w_out", "ffn2_w_gate", "ffn2_w_up", "ffn2_w_down"]
        todo = [("w_in", j) for j in range((512 + 1024 + SHIFT_W) // 128, cfg.PROJ_W // 128)]
        for name in names:
            todo += [(name, j) for j in range(self.weight_shapes()[name][1] // 128)]
        for name, j in todo:
            R_ = self.weight_shapes()[name][0] // 128
            for k0 in range(0, R_, 8):
                self.conv_queue.append((name, j, k0, min(R_, k0 + 8)))
        while self.conv_queue:
            self.emit_conv(1)
            yield

    def emit_conv(self, n):
        while n > 0 and self.conv_queue:
            name, j, k0, k1 = self.conv_queue.pop(0)
            W = self.d_w[name]
            src = W[k0 * 128:k1 * 128, j * 128:(j + 1) * 128].rearrange("(k p) m -> p k m", p=128)
            self.S.dma("pool", self.d_scr[name][j, :, k0:k1, :], src, reads=[], writes=[self.scr_buf[(name, j, k0)]])
            self.scr_done.add((name, j, k0))
            n -= 1

    def rms_stats(self, xT, n):
        cfg = self.cfg
        KC = cfg.KC
        pss = self.ps()
        for kc in range(KC):
            sq = _rr(self.sq, self.sq_i)
            self.sq_i += 1
            self.act(sq.t[:, 0:n], xT.t[:, kc, 0:n], AF.Square, reads=[xT.bs[kc]], writes=[sq.b])
            self.mm(pss.t[:, 0:n], self.ones_bf.t[:, :], sq.t[:, 0:n], kc == 0, kc == KC - 1,
                    reads=[sq.b, self.ones_bf.b], writes=[pss.b], signal=True)
        rs = self.rstd
        self.act(rs.t[:, 0:n], pss.t[:, 0:n], AF.Sqrt, reads=[pss.b, self.eps_norm.b], writes=[rs.b],
                 bias=self.eps_norm.t[:, 0:1], scale=1.0 / cfg.DM)
        self.S.op("dve", lambda e: e.reciprocal(out=rs.t[:, 0:n], in_=rs.t[:, 0:n]), reads=[rs.b], writes=[rs.b])

    def rms_apply(self, xT, gcol, out, c0, nt, o0):
        rs = self.rstd
        for kc in range(self.cfg.KC):
            self.stt(out.t[:, kc, o0:o0 + nt], xT.t[:, kc, c0:c0 + nt], self.vec.t[:, gcol + kc:gcol + kc + 1],
                     rs.t[:, c0:c0 + nt], ALU.mult, ALU.mult, reads=[xT.bs[kc], rs.b, self.vec.b], writes=[out.bs[kc]])

    def rmsnorm(self, xT, n, gcol, out):
        self.rms_stats(xT, n)
        self.rms_apply(xT, gcol, out, 0, n, 0)

    def ffn(self, xT, n, gcol, Wg, Wu, Wd):
        cfg = self.cfg
        KC, FC = cfg.KC, cfg.FC
        xn, h1 = self.xn, self.h1
        self.rmsnorm(xT, n, gcol, xn)
        yield
        for j in range(FC):
            wg = self.load_w8(Wg, KC, j * 128)
            wu = self.load_w8(Wu, KC, j * 128)
            if n <= 256:
                pgu = self.ps(hold=True)
                pg = Tile(pgu.t[:, 0:256])
                pu = Tile(pgu.t[:, 256:512])
                pg.bs = pgu.bs
                pu.bs = pgu.bs
                held_gu = (pgu,)
            else:
                pg, pu = self.ps(hold=True), self.ps(hold=True)
                held_gu = (pg, pu)
            for kc in range(KC):
                self.mm(pg.t[:, 0:n], wg.t[:, kc, :], xn.t[:, kc, 0:n], kc == 0, kc == KC - 1,
                        reads=[wg.b, xn.bs[kc]], writes=[pg.b], signal=(kc == KC - 1))
            yield
            for kc in range(KC):
                self.mm(pu.t[:, 0:n], wu.t[:, kc, :], xn.t[:, kc, 0:n], kc == 0, kc == KC - 1,
                        reads=[wu.b, xn.bs[kc]], writes=[pu.b], signal=(kc == KC - 1))
            sg = _rr(self.sg, self.sg_i)
            self.sg_i += 1
            self.act(sg.t[:, 0:n], pg.t[:, 0:n], AF.Silu, reads=[pg.b], writes=[sg.b])
            self.tt(h1.t[:, j, 0:n], sg.t[:, 0:n], pu.t[:, 0:n], ALU.mult, reads=[sg.b, pu.b], writes=[h1.bs[j]])
            for p_ in held_gu:
                self.release(p_)
            yield
        for m in range(KC):
            wd = self.load_wd(Wd, FC, m * 128)
            po = self.ps(hold=True)
            for j in range(FC):
                self.mm(po.t[:, 0:n], wd.t[:, j, :], h1.t[:, j, 0:n], j == 0, j == FC - 1,
                        reads=[wd.b, h1.bs[j]], writes=[po.b], signal=(j == FC - 1))
                if j % 8 == 7 and j != FC - 1:
                    yield
            self.stt(xT.t[:, m, 0:n], po.t[:, 0:n], 0.5, xT.t[:, m, 0:n], ALU.mult, ALU.add,
                     reads=[po.b, xT.bs[m]], writes=[xT.bs[m]])
            self.release(po)
            yield

    def load_block_x(self, blk, xT):
        cfg = self.cfg
        KC = cfg.KC
        if blk == 0:
            tiles = [(0, cfg.NSP, None)]
        else:
            tiles = [(128 * i, 128, (blk - 1) * cfg.TPB + i) for i in range(cfg.TPB)]
        for c0, nt, pt in tiles:
            xin = _rr(self.xin, self.xin_i)
            self.xin_i += 1
            if pt is None:
                self.S.dma(self.q_misc, xin.t[0:N_META, :], self.d_meta[:, :], reads=[], writes=[xin.b])
                self.S.dma(self.q_misc, xin.t[N_META:cfg.NSP, :], self.d_xs[:, :], reads=[], writes=[xin.b])
            else:
                self.S.dma(self.q_misc, xin.t[:, :], self.d_xp[pt * 128:(pt + 1) * 128, :], reads=[], writes=[xin.b])
            for k0 in range(0, KC, 4):
                nk = min(4, KC - k0)
                p = self.ps()
                for kk in range(nk):
                    self.tr(p.t[:, kk * 128:kk * 128 + nt], xin.t[0:nt, (k0 + kk) * 128:(k0 + kk + 1) * 128],
                            self.ident.t[0:nt, 0:nt], reads=[xin.b, self.ident.b], writes=[p.b], signal=(kk == nk - 1))
                pv = p.t[:, 0:nk * 128].rearrange("p (k t) -> p k t", t=128)
                self.cp(xT.t[:, k0:k0 + nk, c0:c0 + nt], pv[:, :, 0:nt], reads=[p.b],
                        writes=[xT.bs[k] for k in range(k0, k0 + nk)], eng="act")
            yield

    def store_block_y(self, blk, xT, n):
        cfg = self.cfg
        KC = cfg.KC
        yT = self.yT
        self.rms_stats(xT, n)
        if blk == 0:
            tiles = [(N_META, NSEQ * LS, None)]
        else:
            tiles = [(128 * i, 128, (blk - 1) * cfg.TPB + i) for i in range(cfg.TPB)]
        for c0, nt, pt in tiles:
            self.rms_apply(xT, 3 * KC, yT, c0, nt, 0)
            yo = _rr(self.yo, self.yo_i)
            self.yo_i += 1
            for k0 in range(0, KC, 4):
                nk = min(4, KC - k0)
                p = self.ps()
                for kk in range(nk):
                    self.tr(p.t[0:nt, kk * 128:(kk + 1) * 128], yT.t[:, k0 + kk, 0:nt], self.ident.t[:, :],
                            reads=[yT.bs[k0 + kk], self.ident.b], writes=[p.b], signal=(kk == nk - 1))
                self.cp(yo.t[0:nt, k0 * 128:(k0 + nk) * 128], p.t[0:nt, 0:nk * 128], reads=[p.b], writes=[yo.b],
                        eng="act")
            dst = self.d_ys[:, :] if pt is None else self.d_yp[pt * 128:(pt + 1) * 128, :]
            ev = self.S.dma(self.q_misc, dst, yo.t[0:nt, :], reads=[yo.b], writes=[])
            self.out_events.append(ev)
            yield

    def in_proj(self, blk, hT, n, mtiles):
        cfg = self.cfg
        KC = cfg.KC
        uT = self.uT
        self.rmsnorm(hT, n, KC, uT)
        W = "w_in"
        col0 = self.blk_col0(blk)
        self.S.dma(self.q_misc, self.cosb.t[:, 0:n], self.d_c["cosT"][:, col0:col0 + n], reads=[], writes=[self.cosb.b])
        self.S.dma(self.q_misc, self.sinb.t[:, 0:n], self.d_c["sinT"][:, col0:col0 + n], reads=[], writes=[self.sinb.b])
        qkr = self.qkr
        for c in range(4):
            w = self.load_w8(W, KC, c * 128)
            p = self.ps()
            for kc in range(KC):
                self.mm(p.t[:, 0:n], w.t[:, kc, :], uT.t[:, kc, 0:n], kc == 0, kc == KC - 1,
                        reads=[w.b, uT.bs[kc]], writes=[p.b], signal=(kc == KC - 1))
            qc = _rr(self.sg, self.sg_i)
            self.sg_i += 1
            self.act(qc.t[:, 0:n], p.t[:, 0:n], AF.Copy, reads=[p.b], writes=[qc.b],
                     scale=(1.0 if c < 2 else RET_DK ** -0.5))
            p2 = self.ps()
            self.mm(p2.t[:, 0:n], self.rotP.t[:, :], qc.t[:, 0:n], True, True,
                    reads=[self.rotP.b, qc.b], writes=[p2.b], signal=True)
            t1 = _rr(self.sg, self.sg_i)
            self.sg_i += 1
            self.tt(t1.t[:, 0:n], qc.t[:, 0:n], self.cosb.t[:, 0:n], ALU.mult, reads=[qc.b, self.cosb.b], writes=[t1.b])
            self.tt(qkr.t[:, c, 0:n], p2.t[:, 0:n], self.sinb.t[:, 0:n], ALU.mult, reads=[p2.b, self.sinb.b], writes=[qkr.bs[c]])
            self.tt(qkr.t[:, c, 0:n], qkr.t[:, c, 0:n], t1.t[:, 0:n], ALU.add, reads=[qkr.bs[c], t1.b], writes=[qkr.bs[c]])
        pT = self.pT
        for ch in range(14):
            w = self.load_w8(W, KC, 1536 + ch * 128)
            p = self.ps()
            for kc in range(KC):
                self.mm(p.t[:, 0:n], w.t[:, kc, :], uT.t[:, kc, 0:n], kc == 0, kc == KC - 1,
                        reads=[w.b, uT.bs[kc]], writes=[p.b], signal=(kc == KC - 1))
            self.cp(pT.t[:, ch, 1:n + 1], p.t[:, 0:n], reads=[p.b], writes=[pT.bs[ch]], eng="act")
        self.cp(pT.t[:, :, 0], self.pprev.t[:, :], reads=[self.pprev.b], writes=[pT.b], eng="act")
        for (cbase, is_g) in ((512, False), (1024, True)):
            banks = [self.ps(hold=True) for _ in mtiles]
            for q in range(4):
                w = self.load_w8(W, KC, cbase + q * 128)
                for i, (c0, nt, kind) in enumerate(mtiles):
                    for kc in range(KC):
                        self.mm(banks[i].t[0:nt, q * 128:(q + 1) * 128], uT.t[:, kc, c0:c0 + nt], w.t[:, kc, :], kc == 0, kc == KC - 1,
                                reads=[w.b, uT.bs[kc]], writes=[banks[i].b], signal=(kc == KC - 1))
            for i, (c0, nt, kind) in enumerate(mtiles):
                if is_g:
                    self.act(self.gtok.t[0:nt, i, :], banks[i].t[0:nt, :], AF.Silu, reads=[banks[i].b], writes=[self.gtok.bs[i]])
                else:
                    self.cp(self.vtok.t[0:nt, i, :], banks[i].t[0:nt, :], reads=[banks[i].b], writes=[self.vtok.bs[i]], eng="dve")
                self.release(banks[i])

    @staticmethod
    def run(gen):
        if gen is not None:
            for _ in gen:
                pass

    @staticmethod
    def interleave(gm, gd, m_per_d):
        am, ad = True, True
        while am or ad:
            for _ in range(m_per_d):
                if am:
                    try:
                        next(gm)
                    except StopIteration:
                        am = False
            for _ in range(Builder.D_PER_M):
                if ad:
                    try:
                        next(gd)
                    except StopIteration:
                        ad = False

    def blk_col0(self, blk):
        return 0 if blk == 0 else self.cfg.NSP + (blk - 1) * 128 * self.cfg.TPB

    def merge_out(self, blk, hT, n):
        cfg = self.cfg
        KC = cfg.KC
        uT, oret, orw, mg = self.uT, self.oretT, self.orwT, self.mgT
        W = "w_in"
        gcol = 512 + 1024 + SHIFT_W
        for m in range(KC):
            wa = self.load_w8("w_out_ret", 4, m * 128)
            wb = self.load_w8("w_out_rwkv", 4, m * 128)
            wga = self.load_w8(W, KC, gcol + m * 128)
            wgb = self.load_w8(W, KC, gcol + cfg.DM + m * 128)
            pa, pb, pga, pgb = self.ps(), self.ps(), self.ps(), self.ps()
            for c in range(4):
                self.mm(pa.t[:, 0:n], wa.t[:, c, :], oret.t[:, c, 0:n], c == 0, c == 3,
                        reads=[wa.b, oret.bs[c]], writes=[pa.b], signal=(c == 3))
            for c in range(4):
                self.mm(pb.t[:, 0:n], wb.t[:, c, :], orw.t[:, c, 0:n], c == 0, c == 3,
                        reads=[wb.b, orw.bs[c]], writes=[pb.b], signal=(c == 3))
            for kc in range(KC):
                self.mm(pga.t[:, 0:n], wga.t[:, kc, :], uT.t[:, kc, 0:n], kc == 0, kc == KC - 1,
                        reads=[wga.b, uT.bs[kc]], writes=[pga.b], signal=(kc == KC - 1))
            for kc in range(KC):
                self.mm(pgb.t[:, 0:n], wgb.t[:, kc, :], uT.t[:, kc, 0:n], kc == 0, kc == KC - 1,
                        reads=[wgb.b, uT.bs[kc]], writes=[pgb.b], signal=(kc == KC - 1))
            ga = _rr(self.sg, self.sg_i)
            self.sg_i += 1
            gb = _rr(self.sg, self.sg_i)
            self.sg_i += 1
            self.act(ga.t[:, 0:n], pga.t[:, 0:n], AF.Sigmoid, reads=[pga.b], writes=[ga.b])
            self.act(gb.t[:, 0:n], pgb.t[:, 0:n], AF.Sigmoid, reads=[pgb.b], writes=[gb.b])
            self.tt(ga.t[:, 0:n], ga.t[:, 0:n], pa.t[:, 0:n], ALU.mult, reads=[ga.b, pa.b], writes=[ga.b])
            self.tt(gb.t[:, 0:n], gb.t[:, 0:n], pb.t[:, 0:n], ALU.mult, reads=[gb.b, pb.b], writes=[gb.b])
            self.tt(mg.t[:, m, 0:n], ga.t[:, 0:n], gb.t[:, 0:n], ALU.add, reads=[ga.b, gb.b], writes=[mg.bs[m]])
        for m in range(KC):
            w = self.load_w8("w_out", KC, m * 128)
            po = self.ps()
            for kc in range(KC):
                self.mm(po.t[:, 0:n], w.t[:, kc, :], mg.t[:, kc, 0:n], kc == 0, kc == KC - 1,
                        reads=[w.b, mg.bs[kc]], writes=[po.b], signal=(kc == KC - 1))
            self.tt(hT.t[:, m, 0:n], hT.t[:, m, 0:n], po.t[:, 0:n], ALU.add, reads=[hT.bs[m], po.b], writes=[hT.bs[m]])

    WEIGHTS = ("ffn1_w_gate", "ffn1_w_up", "ffn1_w_down", "w_in", "w_out_ret", "w_out_rwkv", "w_out",
               "ffn2_w_gate", "ffn2_w_up", "ffn2_w_down")

    def weight_shapes(self):
        c = self.cfg
        return {"ffn1_w_gate": (c.DM, c.DFF), "ffn1_w_up": (c.DM, c.DFF), "ffn1_w_down": (c.DFF, c.DM),
                "w_in": (c.DM, c.PROJ_W), "w_out_ret": (512, c.DM), "w_out_rwkv": (512, c.DM), "w_out": (c.DM, c.DM),
                "ffn2_w_gate": (c.DM, c.DFF), "ffn2_w_up": (c.DM, c.DFF), "ffn2_w_down": (c.DFF, c.DM)}

    def alloc_block_buffers(self, es, TB, pipelined=False):
        cfg = self.cfg
        KC, FC = cfg.KC, cfg.FC
        self.es = es
        self.TBcur = TB
        self.xT = self.sb("xT", [128, KC, TB], F32, KC)
        self.xTs = [self.xT, self.sb("xTb", [128, KC, TB], F32, KC)] if pipelined else [self.xT, self.xT]
        self.xn = self.sb("xn", [128, KC, TB], BF16, KC)
        self.mgT = self.xn
        self.uT = self.sb("uT", [128, KC, TB], BF16, KC)
        nbig = 14 * (TB + 1) if pipelined else max(14 * (TB + 1), (FC * TB + 1) // 2)
        big = self.sb("big", [128, nbig], F32, 1)
        self.big = big
        self.pT = Tile(big.t[:, 0:14 * (TB + 1)].rearrange("p (c t) -> p c t", t=TB + 1), 1)
        self.pT.bs = [big.b] * 14
        if pipelined:
            self.h1 = self.sb("h1", [128, FC, TB], BF16, FC)
        else:
            h1ap = big.t[:, :].bitcast(BF16)[:, 0:FC * TB].rearrange("p (f t) -> p f t", t=TB)
            self.h1 = Tile(h1ap, 1)
            self.h1.bs = [big.b] * FC
        self.yT = self.sb("yT", [128, KC, 128], F32, KC)
        self.sq = [self.sb("sq%d" % i, [128, TB], BF16) for i in range(2)]
        self.rstd = self.sb("rstd", [128, TB], F32)
        self.sg = [self.sb("sg%d" % i, [128, TB], F32) for i in range(3)]
        self.cosb = self.sb("cosb", [128, TB], F32)
        self.sinb = self.sb("sinb", [128, TB], F32)
        self.qkr = self.sb("qkr", [128, 4, TB], F32, 4)
        ntile = max(2, TB // 128)
        self.vtok = self.sb("vtok", [128, ntile, 512], F32, ntile)
        self.gtok = self.sb("gtok", [128, ntile, 512], BF16, ntile)
        self.oretT = self.sb("oretT", [128, 4, TB], BF16, 4)
        self.orwT = self.sb("orwT", [128, 4, TB], BF16, 4)
        self.sq_i = self.sg_i = 0

    def build(self, stub_mixers=False, stub_rwkv=False):
        cfg = self.cfg
        KC, FC = cfg.KC, cfg.FC
        nc = bass.Bass("TRN2", target_bir_lowering=False)
        self.nc = nc
        self.out_events = []
        self.d_xp = self.dram_in("xp", [128 * cfg.NT, cfg.DM])
        self.d_xs = self.dram_in("xs", [NSEQ * LS, cfg.DM])
        self.d_meta = self.dram_in("meta", [N_META, cfg.DM])
        self.d_st_ret = self.dram_in("st_ret", [NSEQ, 4, 64, 128])
        self.d_st_wkv = self.dram_in("st_wkv", [NSEQ, 8, 64, 64])
        self.d_st_shift = self.dram_in("st_shift", [NSEQ, SHIFT_W])
        self.NV = 4 * KC + 14 + 7 * 4
        self.d_vec = self.dram_in("vecs", [128, self.NV])
        self.d_gng = self.dram_in("gng", [128, 512])
        self.d_small = {"w2": self.dram_in("w2", [64, 512]), "a2": self.dram_in("a2", [64, 512]),
                        "g2": self.dram_in("g2", [128, 512])}
        self.d_w = {k: self.dram_in(k, list(s)) for k, s in self.weight_shapes().items()}
        self.d_scr, self.scr_buf, self.conv_queue = {}, {}, []
        self.scr_done = set()
        self.wq_i = 0
        self.use_scr = False
        self.q_misc, self.q_w = "sp", "pool"
        if getattr(self, "scratch", True) and cfg.NBLK > 2:
            order = ["ffn2_w_gate", "ffn2_w_up", "ffn2_w_down", "ffn1_w_gate", "ffn1_w_up", "ffn1_w_down",
                     "w_in", "w_out_ret", "w_out_rwkv", "w_out"]
            for name in ["ffn1_w_gate", "ffn1_w_up", "ffn1_w_down", "w_in", "w_out_ret", "w_out_rwkv", "w_out",
                         "ffn2_w_gate", "ffn2_w_up", "ffn2_w_down"]:
                R_, C_ = self.weight_shapes()[name]
                self.d_scr[name] = nc.dram_tensor("scr_" + name, [C_ // 128, 128, R_ // 128, 128], BF16, kind="Internal").ap()
                for j in range(C_ // 128):
                    for k0 in range(0, R_ // 128, 8):
                        self.scr_buf[(name, j, k0)] = Buf()
        consts = host_constants(cfg)
        self.d_c = {k: self.dram_in("c_" + k, list(v.shape)) for k, v in consts.items() if not k.startswith("_")}
        self.lg = consts["_lg"]
        self.d_yp = self.dram_out("yp", [128 * cfg.NT, cfg.DM])
        self.d_ys = self.dram_out("ys", [NSEQ * LS, cfg.DM])
        self.d_ret_p = self.dram_out("ret_p", [4, 64, 128])
        self.d_wkv_p = self.dram_out("wkv_p", [8, 64, 64])
        self.d_shift_p = self.dram_out("shift_p", [14, 128])
        self.d_ret_s = self.dram_out("ret_s", [NSEQ, 4, 64, 128])
        self.d_wkv_s = self.dram_out("wkv_s", [NSEQ, 8, 64, 64])
        self.d_shift_s = self.dram_out("shift_s", [NSEQ, SHIFT_W])

        with contextlib.ExitStack() as es:
            self.es = es
            self.S = Sched(nc, es)
            S = self.S
            self.psb = []
            for i in range(8):
                t = es.enter_context(nc.psum_tensor("ps%d" % i, [128, 512], F32))
                self.psb.append(Tile(t))
            self.ps_i = 0
            self.held = set()
            if getattr(self, "pad_kb", 0):
                self.sb("pad", [128, self.pad_kb * 256])
            self.vec = self.sb("vec", [128, self.NV])
            self.ident = self.sb("ident", [128, 128])
            self.identb = self.sb("identb", [128, 128], BF16)
            self.ones_bf = self.sb("ones_bf", [128, 128], BF16)
            self.rotP = self.sb("rotP", [128, 128])
            self.blockones = self.sb("blockones", [128, 128])
            self.eps_norm = self.sb("eps_norm", [128, 4])
            self.w8 = [self.sb("w8_%d" % i, [128, 8, 128], BF16) for i in range(Builder.NW8)]
            self.wd = [self.sb("wd_%d" % i, [128, FC, 128], BF16) for i in range(2)]
            self.xin = [self.sb("xin%d" % i, [128, cfg.DM]) for i in range(1)]
            self.yo = self.xin
            self.w8_i = self.wd_i = self.wt_i = self.xin_i = self.yo_i = 0
            self.alloc_common_mixer()
            S.dma(self.q_misc, self.vec.t[:, :], self.d_vec[:, :], writes=[self.vec.b])
            S.dma(self.q_misc, self.ident.t[:, :], self.d_c["ident"][:, :], writes=[self.ident.b])
            S.dma(self.q_misc, self.rotP.t[:, :], self.d_c["rotP"][:, :], writes=[self.rotP.b])
            S.dma(self.q_misc, self.blockones.t[:, :], self.d_c["blockones"][:, :], writes=[self.blockones.b])
            S.op("dve", lambda e: e.memset(self.ones_bf.t[:, :], 1.0), writes=[self.ones_bf.b])
            S.op("dve", lambda e: e.tensor_copy(out=self.identb.t[:, :], in_=self.ident.t[:, :]),
                 reads=[self.ident.b], writes=[self.identb.b])
            S.op("dve", lambda e: e.memset(self.eps_norm.t[:, 0:1], NORM_EPS), writes=[self.eps_norm.b])
            S.op("dve", lambda e: e.memset(self.eps_norm.t[:, 1:2], RET_GN_EPS), writes=[self.eps_norm.b])
            S.op("dve", lambda e: e.memset(self.eps_norm.t[:, 2:3], RWKV_GN_EPS), writes=[self.eps_norm.b])
            self.load_common_mixer()

            for scope in (0, 1):
                with contextlib.ExitStack() as es2:
                    TB = cfg.NSP if scope == 0 else 128 * cfg.TPB
                    pipelined = (scope == 1) and (not stub_mixers) and getattr(self, "pipeline", True) and cfg.NBLK > 2
                    self.alloc_block_buffers(es2, TB, pipelined=pipelined)
                    if scope == 0:
                        self.alloc_sample(es2)
                    if stub_mixers or stub_rwkv:
                        for t in ((self.oretT, self.orwT) if (stub_mixers or getattr(self, "rtrunc", 99) < 5) else (self.orwT,)):
                            S.op("dve", lambda e, t=t: e.memset(t.t[:, :, :], 0.0), writes=t.bs)
                    self.sbuf_used = max(getattr(self, "sbuf_used", 0), 229344 - nc.sbuf_bytes_remaining)
                    self.sbuf_scope = getattr(self, "sbuf_scope", []) + [229344 - nc.sbuf_bytes_remaining]
                    blocks = [0] if scope == 0 else list(range(1, cfg.NBLK))
                    W = self.d_w
                    F1 = ("ffn1_w_gate", "ffn1_w_up", "ffn1_w_down")
                    F2 = ("ffn2_w_gate", "ffn2_w_up", "ffn2_w_down")

                    def mtiles_of(blk):
                        if blk == 0:
                            return [(0, N_META, "meta"), (N_META, NSEQ * LS, "sample")]
                        return [(128 * i, 128, "full") for i in range(cfg.TPB)]

                    def mixers(blk):
                        mt = mtiles_of(blk)
                        if pipelined and Builder.SEP_RET and not stub_rwkv:
                            def R_():
                                for i, (c0, nt, kind) in enumerate(mt):
                                    yield from self.retention(blk, i, c0, nt, kind)

                            def W_():
                                for i, (c0, nt, kind) in enumerate(mt):
                                    yield from self.rwkv(blk, i, c0, nt, kind, last=(blk == cfg.NBLK - 1 and i == len(mt) - 1))
                            gr, gw = R_(), W_()
                            ar = aw = True
                            while ar or aw:
                                for _ in range(2):
                                    if aw:
                                        try:
                                            next(gw)
                                            yield
                                        except StopIteration:
                                            aw = False
                                if ar:
                                    try:
                                        next(gr)
                                        yield
                                    except StopIteration:
                                        ar = False
                            return
                        for i, (c0, nt, kind) in enumerate(mt):
                            yield from self.retention(blk, i, c0, nt, kind)
                            if not stub_rwkv:
                                yield from self.rwkv(blk, i, c0, nt, kind, last=(blk == cfg.NBLK - 1 and i == len(mt) - 1))

                    run = self.run
                    if not pipelined:
                        for blk in blocks:
                            n = cfg.NSP if blk == 0 else 128 * cfg.TPB
                            xT = self.xTs[0]
                            trunc = getattr(self, "trunc", 99)
                            run(self.load_block_x(blk, xT))
                            if trunc >= 2:
                                run(self.ffn(xT, n, 0, *F1))
                            if trunc >= 3:
                                self.in_proj(blk, xT, n, mtiles_of(blk))
                            if not stub_mixers:
                                if scope == 0 and self.d_scr:
                                    self.interleave(mixers(blk), self.conv_phase_b(), 2)
                                else:
                                    run(mixers(blk))
                            if trunc >= 4:
                                self.merge_out(blk, xT, n)
                            if trunc >= 5:
                                run(self.ffn(xT, n, 2 * KC, *F2))
                            if trunc >= 1:
                                run(self.store_block_y(blk, xT, n))
                    else:
                        n = 128 * cfg.TPB
                        first, lastb = blocks[0], blocks[-1]
                        x0 = self.xTs[first % 2]
                        run(self.load_block_x(first, x0))
                        run(self.ffn(x0, n, 0, *F1))
                        self.in_proj(first, x0, n, mtiles_of(first))
                        for blk in blocks:
                            def dense(blk=blk):
                                if blk > first:
                                    xp = self.xTs[(blk - 1) % 2]
                                    yield from self.ffn(xp, n, 2 * KC, *F2)
                                    yield from self.store_block_y(blk - 1, xp, n)
                                if blk < lastb:
                                    xq = self.xTs[(blk + 1) % 2]
                                    yield from self.load_block_x(blk + 1, xq)
                                    yield from self.ffn(xq, n, 0, *F1)
                            self.interleave(mixers(blk), dense(), getattr(self, "m_per_d", 1))
                            self.merge_out(blk, self.xTs[blk % 2], n)
                            if blk < lastb:
                                self.in_proj(blk + 1, self.xTs[(blk + 1) % 2], n, mtiles_of(blk + 1))
                        xl = self.xTs[lastb % 2]
                        run(self.ffn(xl, n, 2 * KC, *F2))
                        run(self.store_block_y(lastb, xl, n))
                    if scope == 0 and self.d_scr:
                        assert len(self.scr_done) == len(self.scr_buf), (len(self.scr_done), len(self.scr_buf))
                        self.use_scr = True
                        self.q_misc, self.q_w = "pool", "sp"
                    if scope == 1:
                        if not stub_mixers:
                            self.store_prompt_states()
                        for ev in self.out_events:
                            S.wait_event("sp", ev)
                    S.barrier()
                    with nc.Block() as block:
                        S.emit(block)
                    S.clear()
        return nc

    def alloc_common_mixer(self):
        self.rmaskT = self.sb("rmaskT", [128, 4, 128])
        self.qdec = self.sb("qdec", [128, 2, 128])
        self.kdec = self.sb("kdec", [128, 3, 4])
        self.gng = self.sb("gng", [128, 512])
        self.wmask = self.sb("wmask", [128, 6, 128])
        self.Sret = self.sb("Sret", [128, 2, 128], F32, 2)
        self.Hwkv = self.sb("Hwkv", [128, 4, 64], F32, 4)
        self.pprev = self.sb("pprev", [128, 14])
        self.w2t = self.sb("w2t", [128, 512])
        self.a2t = self.sb("a2t", [128, 512])
        self.g2t = self.sb("g2t", [128, 512])
        self.omka = self.sb("omka", [128, 4])
        self.onesf = self.sb("onesf", [128, 128])
        self.qtm = self.sb("qtm", [128, 2, 2, 128])
        self.kmsk = self.sb("kmskh", [128, 2, 2, 128])
        self.hsel = self.sb("hsel", [128, 2])
        self.atm = self.sb("atm", [128, 2, 128])
        self.rtm = self.sb("rtm", [128, 2, 128])
        self.ktk = self.sb("ktk", [128, 4, 64])
        self.sc = self.sb("sc", [128, 4, 128])
        self.osb = self.sb("osb", [128, 512])
        self.osq = self.sb("osq", [128, 512])
        self.onb = self.sb("onb", [128, 512], BF16)
        self.st4 = self.sb("st4", [128, 4, 4])
        self.pm = self.sb("pm", [128, 14, 128])
        self.rw = {nm: self.sb("rw_" + nm, [128, 4, 128]) for nm in
                   ("logw", "cum", "eg", "ee", "a", "kk", "km", "b", "g", "bon", "tmp")}
        for a_, b_ in (("rt", "eg"), ("kt", "km"), ("bt", "b"), ("at", "ee"), ("ei", "tmp")):
            self.rw[a_] = self.rw[b_]
        self.thw = self.sb("thw", [128, 128])
        self.sgx = self.sb("sgx", [128, 128])
        self.vtk = self.sb("vtk", [128, 512])
        self.ktok = self.sb("ktok", [128, 512])
        self.btok = self.sb("btok", [128, 512])
        self.mA = [self.sb("mA%d" % i, [128, 4, 128]) for i in range(2)]
        self.mB = [self.sb("mB%d" % i, [128, 128]) for i in range(2)]
        self.PP = [] if Builder.BF16_CHAIN else [self.sb("PP%d" % i, [128, 4, 128]) for i in range(2)]
        self.Up = [self.sb("Up%d" % i, [128, 128]) for i in range(3)]
        self.ysb = self.osb
        self.ysq = self.osq
        self.yst = self.sb("yst", [128, 4, 8])
        self.gcum = self.sb("gcum", [128, 4, NSEQ])
        self.mA_i = self.mB_i = self.PP_i = self.Up_i = self.am_i = 0

        def alias(tile, view):
            t = Tile(view)
            t.bs = tile.bs
            return t
        if Builder.SEP_RET:
            self.mA = self.mA + [self.sb("mAs%d" % i, [128, 4, 128]) for i in range(2)]
            if not Builder.BF16_CHAIN:
                self.PP = self.PP + [self.sb("PPs%d" % i, [128, 4, 128]) for i in range(2)]
            self.ysb = self.sb("ysb", [128, 512])
            self.ysq = self.sb("ysq", [128, 512])
        else:
            self.mA = self.mA + [self.sc, alias(self.qtm, self.qtm.t[:, :, :, :].rearrange("p a b t -> p (a b) t"))]
            self.PP = self.PP + [alias(self.kmsk, self.kmsk.t[:, :, :, :].rearrange("p a b t -> p (a b) t")),
                                 alias(self.osb, self.osb.t[:, :].rearrange("p (q t) -> p q t", t=128))]
        self.mP = [self.sb("mP%d" % i, [128, 2, 128], BF16) for i in range(4)]
        self.PPb = [self.sb("PPb%d" % i, [128, 4, 128], BF16) for i in range(4)]
        self.Ub = [self.sb("Ub%d" % i, [128, 128], BF16) for i in range(4)]
        self.mP_i = self.PPb_i = self.Ub_i = 0
        np_ = Builder.NPAIR_IL
        if np_ == 4:
            self.mA = self.mA + [self.sb("mAx%d" % i, [128, 4, 128]) for i in range(4)]
            self.PP = self.PP + [self.sb("PPx%d" % i, [128, 4, 128]) for i in range(4)]
        self.mB = self.mB + [self.sb("mB%d" % i, [128, 128]) for i in range(2, 2 * np_)]
        self.Up = self.Up + [self.sb("Upx%d" % i, [128, 128]) for i in range(3, 2 * np_)]
        self.atms = [self.atm] + [self.sb("atm%d" % i, [128, 2, 128]) for i in range(2, np_ + 1)]
        self.rtms = [self.rtm] + [self.sb("rtm%d" % i, [128, 2, 128]) for i in range(2, np_ + 1)]

    def load_common_mixer(self):
        S = self.S
        for t, k in ((self.rmaskT, "rmaskT"), (self.qdec, "qdec"), (self.kdec, "kdec"), (self.wmask, "wmask")):
            S.dma(self.q_misc, t.t[:], self.d_c[k][:], writes=[t.b])
        S.dma(self.q_misc, self.gng.t[:, :], self.d_gng[:, :], writes=[self.gng.b])
        S.op("dve", lambda e: e.memset(self.w2t.t[:, :], 0.0), writes=[self.w2t.b])
        S.op("dve", lambda e: e.memset(self.a2t.t[:, :], 0.0), writes=[self.a2t.b])
        S.dma(self.q_misc, self.w2t.t[0:64, :], self.d_small["w2"][:, :], writes=[self.w2t.b])
        S.dma(self.q_misc, self.a2t.t[64:128, :], self.d_small["a2"][:, :], writes=[self.a2t.b])
        S.dma(self.q_misc, self.hsel.t[:, :], self.d_c["hsel"][:, :], writes=[self.hsel.b])
        S.dma(self.q_misc, self.g2t.t[:, :], self.d_small["g2"][:, :], writes=[self.g2t.b])
        for t in (self.Sret, self.Hwkv):
            S.op("dve", lambda e, t=t: e.memset(t.t[:], 0.0), writes=t.bs)
        S.op("dve", lambda e: e.memset(self.pprev.t[:, :], 0.0), writes=[self.pprev.b])
        S.op("dve", lambda e: e.memset(self.onesf.t[:, :], 1.0), writes=[self.onesf.b])
        ka = self.vcol("k_a")
        self.ts(self.omka.t[:, :], self.vec.t[:, ka:ka + 4], -1.0, 1.0, ALU.mult, ALU.add,
                reads=[self.vec.b], writes=[self.omka.b])

    def vcol(self, name):
        KC = self.cfg.KC
        off = {"mu": 4 * KC, "w0": 4 * KC + 14, "a0": 4 * KC + 18, "k_k": 4 * KC + 22, "k_a": 4 * KC + 26,
               "r_k": 4 * KC + 30, "lnx_g": 4 * KC + 34, "lnx_b": 4 * KC + 38}
        return off[name]

    def alloc_sample(self, es):
        self.es = es
        S = self.S
        self.rmaskT_s = self.sb("rmaskT_s", [64, 4, 64])
        self.qdec_s = self.sb("qdec_s", [128, 2, 64])
        self.wmask_s = self.sb("wmask_s", [64, 6, 64])
        self.seqsel = self.sb("seqsel", [64, NSEQ])
        self.seqselT = self.sb("seqselT", [128, NSEQ, 64])
        self.Sret_s = self.sb("Sret_s", [128, NSEQ, 128])
        self.Hwkv_s = self.sb("Hwkv_s", [128, NSEQ, 64])
        self.qm = self.sb("qm", [128, NSEQ, 64])
        self.km = self.sb("kmsk", [64, NSEQ, 128])
        self.wkin = self.sb("wkin", [64, NSEQ, 2, 64])
        self.psh = self.sb("psh", [128, 14, 64])
        self.qm2 = self.sb("qm2", [128, NSEQ, 64])
        self.km2 = Tile(self.wkin.t[:, :, :, :].rearrange("v s h k -> v s (h k)"))
        self.km2.bs = self.wkin.bs
        self.shs = Tile(self.km.t[0:NSEQ, :, :].rearrange("p s d -> p (s d)")[:, 0:SHIFT_W])
        self.shs.bs = self.km.bs
        for t, k in ((self.rmaskT_s, "rmaskT_s"), (self.qdec_s, "qdec_s"), (self.wmask_s, "wmask_s"),
                     (self.seqsel, "seqsel"), (self.seqselT, "seqselT")):
            S.dma(self.q_misc, t.t[:], self.d_c[k][:], writes=[t.b])

    def retention(self, blk, i, c0, nt, kind):
        S = self.S
        sample = kind == "sample"
        qkr = self.qkr
        lg = self.lg
        Lseq = LS if sample else nt
        rmask = self.rmaskT_s if sample else self.rmaskT
        qdec = self.qdec_s if sample else self.qdec
        ksel = 2 if sample else (1 if kind == "meta" else 0)
        qtm, ktk, sc, kmsk, hsel = self.qtm, self.ktk, self.sc, self.kmsk, self.hsel
        for hp in range(2):
            self.stt(qtm.t[:, hp, :, 0:nt], qkr.t[:, 0:2, c0:c0 + nt], hsel.t[:, hp:hp + 1], qdec.t[:, :, 0:nt], ALU.mult, ALU.mult,
                     reads=[qkr.bs[0], qkr.bs[1], qdec.b, hsel.b], writes=[qtm.b])
            self.ts(kmsk.t[:, hp, :, 0:nt], qkr.t[:, 2:4, c0:c0 + nt], hsel.t[:, hp:hp + 1], None, ALU.mult, None,
                    reads=[qkr.bs[2], qkr.bs[3], hsel.b], writes=[kmsk.b])
        pk = self.ps()
        for c in range(2):
            self.tr(pk.t[0:nt, c * 128:(c + 1) * 128], qkr.t[:, 2 + c, c0:c0 + nt], self.ident.t[:, :],
                    reads=[qkr.bs[2 + c], self.ident.b], writes=[pk.b], signal=(c == 1))
        self.tt(ktk.t[0:nt, :, :], pk.t[0:nt, 0:256].rearrange("p (h d) -> p h d", d=64),
                self.kdec.t[0:nt, ksel, :].unsqueeze(2).to_broadcast([nt, 4, 64]), ALU.mult,
                reads=[pk.b, self.kdec.b], writes=[ktk.b])
        rtr = getattr(self, "rtrunc", 99)
        yield
        psc = self.ps()
        for h in range(4):
            c, pb = h // 2, 64 * (h % 2)
            self.mm(psc.t[0:nt, h * 128:h * 128 + nt], kmsk.t[:, h % 2, c, 0:nt], qkr.t[:, c, c0:c0 + nt],
                    True, True, reads=[kmsk.b, qkr.bs[c]], writes=[psc.b], signal=(h == 3))
        self.tt(sc.t[0:nt, :, 0:nt], psc.t[0:nt, :].rearrange("p (h t) -> p h t", t=128)[:, :, 0:nt],
                rmask.t[0:nt, :, 0:nt], ALU.mult, reads=[psc.b, rmask.b], writes=[sc.b])
        yield
        po = self.ps(hold=True)
        vt = self.vtok
        for h in range(4):
            c, pb = h // 2, 64 * (h % 2)
            if sample and pb == 0:
                for hp in range(2):
                    src = self.d_st_ret[:, 2 * c + hp, :, :].rearrange("s d v -> d s v")
                    S.dma(self.q_misc, self.Sret_s.t[64 * hp:64 * hp + 64, :, :], src, writes=[self.Sret_s.b])
            if sample:
                self.tt(self.qm.t[:, :, :], qtm.t[:, h % 2, c, 0:nt].unsqueeze(1).to_broadcast([128, NSEQ, 64]),
                        self.seqselT.t[:, :, :], ALU.mult, reads=[qtm.b, self.seqselT.b], writes=[self.qm.b])
            self.mm(po.t[0:nt, h * 128:(h + 1) * 128], sc.t[0:nt, h, 0:nt], vt.t[0:nt, i, h * 128:(h + 1) * 128],
                    True, False, reads=[sc.b, vt.bs[i]], writes=[po.b], signal=False)
            if not sample:
                self.mm(po.t[0:nt, h * 128:(h + 1) * 128], qtm.t[:, h % 2, c, 0:nt], self.Sret.t[:, c, :],
                        False, True, reads=[qtm.b, self.Sret.bs[c]], writes=[po.b], signal=True)
            else:
                for s in range(NSEQ):
                    self.mm(po.t[0:nt, h * 128:(h + 1) * 128], self.qm.t[:, s, :], self.Sret_s.t[:, s, :],
                            False, s == NSEQ - 1, reads=[self.qm.b, self.Sret_s.b], writes=[po.b],
                            signal=(s == NSEQ - 1))
            yield
            if pb == 0:
                continue
            if not sample:
                pu = self.ps()
                self.mm(pu.t[:, 0:256], ktk.t[0:nt, 2 * c:2 * c + 2, :].rearrange("p h d -> p (h d)"),
                        vt.t[0:nt, i, 256 * c:256 * (c + 1)], True, True, reads=[ktk.b, vt.bs[i]], writes=[pu.b], signal=True)
                for hp in range(2):
                    g = float(np.exp(np.float32(lg[2 * c + hp]) * np.float32(Lseq)))
                    self.stt(self.Sret.t[64 * hp:64 * hp + 64, c, :], self.Sret.t[64 * hp:64 * hp + 64, c, :], g,
                             pu.t[64 * hp:64 * hp + 64, 128 * hp:128 * hp + 128], ALU.mult, ALU.add,
                             reads=[self.Sret.bs[c], pu.b], writes=[self.Sret.bs[c]])
            else:
                self.tt(self.km.t[:, :, :], ktk.t[0:64, 2 * c:2 * c + 2, :].rearrange("p h d -> p (h d)").unsqueeze(1).to_broadcast([64, NSEQ, 128]),
                        self.seqsel.t[:, :].unsqueeze(2).to_broadcast([64, NSEQ, 128]), ALU.mult,
                        reads=[ktk.b, self.seqsel.b], writes=[self.km.b])
                for s0 in range(0, NSEQ, 2):
                    pu = self.ps()
                    for s in (s0, s0 + 1):
                        self.mm(pu.t[:, 256 * (s - s0):256 * (s - s0 + 1)], self.km.t[:, s, :], vt.t[0:nt, i, 256 * c:256 * (c + 1)],
                                True, True, reads=[self.km.b, vt.bs[i]], writes=[pu.b], signal=(s == s0 + 1))
                    for hp in range(2):
                        g = float(np.exp(np.float32(lg[2 * c + hp]) * np.float32(Lseq)))
                        sl = slice(64 * hp, 64 * hp + 64)
                        sv = self.Sret_s.t[sl, s0:s0 + 2, :]
                        pv_ = pu.t[sl, :].rearrange("p (s x) -> p s x", x=256)[:, :, 128 * hp:128 * hp + 128]
                        self.stt(sv, sv, g, pv_, ALU.mult, ALU.add, reads=[self.Sret_s.b, pu.b], writes=[self.Sret_s.b])
                for hp in range(2):
                    dst = self.d_ret_s[:, 2 * c + hp, :, :].rearrange("s d v -> d s v")
                    ev = S.dma(self.q_misc, dst, self.Sret_s.t[64 * hp:64 * hp + 64, :, :], reads=[self.Sret_s.b])
                    self.out_events.append(ev)
        yield
        osb, osq, st4 = self.osb, self.osq, self.st4
        self.cp(osb.t[0:nt, :], po.t[0:nt, :], reads=[po.b], writes=[osb.b], eng="act")
        self.release(po)
        o3 = osb.t[0:nt, :].rearrange("p (h v) -> p h v", v=128)
        q3 = osq.t[0:nt, :].rearrange("p (h v) -> p h v", v=128)
        S.op("dve", lambda e: e.tensor_reduce(out=st4.t[0:nt, 0, :], in_=o3, axis=AX.X, op=ALU.add),
             reads=[osb.b], writes=[st4.b])
        self.ts(st4.t[0:nt, 1, :], st4.t[0:nt, 0, :], -1.0 / 128, None, ALU.mult, None, reads=[st4.b], writes=[st4.b])
        self.tt(o3, o3, st4.t[0:nt, 1, :].unsqueeze(2).to_broadcast([nt, 4, 128]), ALU.add, reads=[osb.b, st4.b], writes=[osb.b])
        self.act(osq.t[0:nt, :], osb.t[0:nt, :], AF.Square, reads=[osb.b], writes=[osq.b])
        S.op("dve", lambda e: e.tensor_reduce(out=st4.t[0:nt, 2, :], in_=q3, axis=AX.X, op=ALU.add),
             reads=[osq.b], writes=[st4.b])
        self.act(st4.t[0:nt, 3, :], st4.t[0:nt, 2, :], AF.Sqrt, reads=[st4.b, self.eps_norm.b], writes=[st4.b],
                 bias=self.eps_norm.t[0:nt, 1:2], scale=1.0 / 128)
        S.op("dve", lambda e: e.reciprocal(out=st4.t[0:nt, 3, :], in_=st4.t[0:nt, 3, :]), reads=[st4.b], writes=[st4.b])
        self.tt(o3, o3, st4.t[0:nt, 3, :].unsqueeze(2).to_broadcast([nt, 4, 128]), ALU.mult, reads=[osb.b, st4.b], writes=[osb.b])
        self.tt(osb.t[0:nt, :], osb.t[0:nt, :], self.gng.t[0:nt, :], ALU.mult, reads=[osb.b, self.gng.b], writes=[osb.b])
        self.tt(self.onb.t[0:nt, :], osb.t[0:nt, :], self.gtok.t[0:nt, i, :], ALU.mult,
                reads=[osb.b, self.gtok.bs[i]], writes=[self.onb.b])
        yield
        pt = self.ps()
        ptb = pt.t[:, :].bitcast(BF16)
        for c in range(4):
            self.tr(ptb[:, c * 128:c * 128 + nt], self.onb.t[0:nt, c * 128:(c + 1) * 128], self.identb.t[0:nt, 0:nt],
                    reads=[self.onb.b, self.identb.b], writes=[pt.b], signal=(c == 3))
        self.cp(self.oretT.t[:, :, c0:c0 + nt], ptb[:, 0:512].rearrange("p (c t) -> p c t", t=128)[:, :, 0:nt],
                reads=[pt.b], writes=self.oretT.bs, eng="act")

    def store_prompt_states(self):
        S = self.S
        for c in range(2):
            for hp in range(2):
                ev = S.dma(self.q_misc, self.d_ret_p[2 * c + hp, :, :], self.Sret.t[64 * hp:64 * hp + 64, c, :], reads=[self.Sret.bs[c]])
                self.out_events.append(ev)
        self.store_wkv(self.Hwkv.t, self.d_wkv_p, None)

    def bc4(self, col, nt):
        return self.vec.t[:, col:col + 4].unsqueeze(2).to_broadcast([128, 4, nt])

    def rwkv(self, blk, i, c0, nt, kind, last=False):
        S = self.S
        cfg = self.cfg
        sample = kind == "sample"
        Ls = LS if sample else nt
        nlev = int(np.ceil(np.log2(Ls)))
        nseq = NSEQ if sample else 1
        pT, pm, rw = self.pT, self.pm, self.rw
        wmask = self.wmask_s if sample else self.wmask
        V = self.vec
        pcur = pT.t[:, :, c0 + 1:c0 + nt + 1]
        if sample:
            S.dma(self.q_misc, self.shs.t[:, :], self.d_st_shift[:, :], writes=[self.shs.b])
            pz = self.ps()
            for ch in range(14):
                self.tr(pz.t[:, ch * NSEQ:(ch + 1) * NSEQ], self.shs.t[0:NSEQ, ch * 128:(ch + 1) * 128], self.ident.t[0:NSEQ, 0:NSEQ],
                        reads=[self.shs.b, self.ident.b], writes=[pz.b], signal=(ch == 13))
            psh4 = self.psh.t[:, :, :].rearrange("p c (s j) -> p c s j", j=LS)
            self.cp(psh4[:, :, :, 0], pz.t[:, 0:14 * NSEQ].rearrange("p (c s) -> p c s", s=NSEQ), reads=[pz.b], writes=[self.psh.b])
            pc4 = pcur.rearrange("p c (s j) -> p c s j", j=LS)
            self.cp(psh4[:, :, :, 1:LS], pc4[:, :, :, 0:LS - 1], reads=[pT.b], writes=[self.psh.b])
            pprev_ap = self.psh.t[:, :, 0:nt]
            prev_reads = [self.psh.b]
        else:
            pprev_ap = pT.t[:, :, c0:c0 + nt]
            prev_reads = [pT.b]
        mu0 = self.vcol("mu")
        pmv = pm.t[:, :, 0:nt]
        self.tt(pmv, pprev_ap, pcur, ALU.subtract, reads=prev_reads + [pT.b], writes=[pm.b])
        self.tt(pmv, pmv, V.t[:, mu0:mu0 + 14].unsqueeze(2).to_broadcast([128, 14, nt]), ALU.mult, reads=[pm.b, V.b], writes=[pm.b])
        self.tt(pmv, pmv, pcur, ALU.add, reads=[pm.b, pT.b], writes=[pm.b])
        r_ = pm.t[:, 0:4, 0:nt]
        k_ = pm.t[:, 4:8, 0:nt]
        v_ = pm.t[:, 8:12, 0:nt]

        def R(nm):
            return rw[nm].t[:, :, 0:nt]

        def B(*nms):
            return [rw[n].b for n in nms]

        yield
        self.act(self.thw.t[:, 0:nt], pm.t[:, 12, 0:nt], AF.Tanh, reads=[pm.b], writes=[self.thw.b])
        pz = self.ps()
        for c in range(4):
            self.mm(pz.t[:, c * 128:c * 128 + nt], self.w2t.t[:, c * 128:(c + 1) * 128], self.thw.t[:, 0:nt], True, True,
                    reads=[self.w2t.b, self.thw.b], writes=[pz.b], signal=(c == 3))
        w0c = self.vcol("w0")
        for c in range(4):
            self.act(rw["logw"].t[:, c, 0:nt], pz.t[:, c * 128:c * 128 + nt], AF.Sigmoid, reads=[pz.b, V.b], writes=B("logw"),
                     bias=V.t[:, w0c + c:w0c + c + 1])
        self.ts(R("logw"), R("logw"), -float(np.exp(-0.5)), None, ALU.mult, None, reads=B("logw"), writes=B("logw"))
        yield
        pa = self.ps()
        for c in range(4):
            self.mm(pa.t[:, c * 128:c * 128 + nt], self.a2t.t[:, c * 128:(c + 1) * 128], pm.t[:, 12, 0:nt], True, True,
                    reads=[self.a2t.b, pm.b], writes=[pa.b], signal=(c == 3))
        a0c = self.vcol("a0")
        for c in range(4):
            self.act(rw["a"].t[:, c, 0:nt], pa.t[:, c * 128:c * 128 + nt], AF.Sigmoid, reads=[pa.b, V.b], writes=B("a"),
                     bias=V.t[:, a0c + c:a0c + c + 1])
        yield
        self.act(self.sgx.t[:, 0:nt], pm.t[:, 13, 0:nt], AF.Sigmoid, reads=[pm.b], writes=[self.sgx.b])
        pg = self.ps()
        for c in range(4):
            self.mm(pg.t[:, c * 128:c * 128 + nt], self.g2t.t[:, c * 128:(c + 1) * 128], self.sgx.t[:, 0:nt], True, True,
                    reads=[self.g2t.b, self.sgx.b], writes=[pg.b], signal=(c == 3))
        self.cp(R("g"), pg.t[:, :].rearrange("p (c t) -> p c t", t=128)[:, :, 0:nt], reads=[pg.b], writes=B("g"), eng="act")
        yield
        self.tt(R("kk"), k_, self.bc4(self.vcol("k_k"), nt), ALU.mult, reads=[pm.b, V.b], writes=B("kk"))
        self.tt(R("tmp"), R("kk"), R("kk"), ALU.mult, reads=B("kk"), writes=B("tmp"))
        pn = self.ps()
        for c in range(4):
            self.mm(pn.t[:, c * 128:c * 128 + nt], self.blockones.t[:, :], rw["tmp"].t[:, c, 0:nt], True, True,
                    reads=[self.blockones.b, rw["tmp"].b], writes=[pn.b], signal=(c == 3))
        self.ts(R("tmp"), pn.t[:, :].rearrange("p (c t) -> p c t", t=128)[:, :, 0:nt], 1e-24, None, ALU.max, None,
                reads=[pn.b], writes=B("tmp"))
        self.act(R("tmp"), R("tmp"), AF.Sqrt, reads=B("tmp"), writes=B("tmp"))
        S.op("dve", lambda e: e.reciprocal(out=R("tmp"), in_=R("tmp")), reads=B("tmp"), writes=B("tmp"))
        self.tt(R("kk"), R("kk"), R("tmp"), ALU.mult, reads=B("kk", "tmp"), writes=B("kk"))
        yield
        self.tt(R("tmp"), R("a"), self.bc4(self.vcol("k_a"), nt), ALU.mult, reads=B("a") + [V.b], writes=B("tmp"))
        self.tt(R("tmp"), R("tmp"), self.omka.t[:, :].unsqueeze(2).to_broadcast([128, 4, nt]), ALU.add,
                reads=B("tmp") + [self.omka.b], writes=B("tmp"))
        self.tt(R("km"), k_, R("tmp"), ALU.mult, reads=[pm.b] + B("tmp"), writes=B("km"))
        self.tt(R("b"), R("kk"), R("a"), ALU.mult, reads=B("kk", "a"), writes=B("b"))
        yield
        self.tt(R("tmp"), r_, R("km"), ALU.mult, reads=[pm.b] + B("km"), writes=B("tmp"))
        self.tt(R("tmp"), R("tmp"), self.bc4(self.vcol("r_k"), nt), ALU.mult, reads=B("tmp") + [V.b], writes=B("tmp"))
        pb_ = self.ps()
        for c in range(4):
            self.mm(pb_.t[:, c * 128:c * 128 + nt], self.blockones.t[:, :], rw["tmp"].t[:, c, 0:nt], True, True,
                    reads=[self.blockones.b, rw["tmp"].b], writes=[pb_.b], signal=(c == 3))
        self.tt(R("bon"), pb_.t[:, :].rearrange("p (c t) -> p c t", t=128)[:, :, 0:nt], v_, ALU.mult, reads=[pb_.b, pm.b], writes=B("bon"))
        yield
        for c in range(4):
            S.op("dve", lambda e, c=c: e.tensor_tensor_scan(out=rw["cum"].t[:, c, 0:nt], data0=self.onesf.t[:, 0:nt],
                                                         data1=rw["logw"].t[:, c, 0:nt], initial=0.0, op0=ALU.mult, op1=ALU.add),
                 reads=B("logw") + [self.onesf.b], writes=B("cum"))
        gc = self.gcum
        if sample:
            cum4 = rw["cum"].t[:, :, 0:nt].rearrange("p c (s j) -> p c s j", j=LS)
            S.op("dve", lambda e: e.memset(gc.t[:, :, 0:1], 0.0), writes=[gc.b])
            self.cp(gc.t[:, :, 1:NSEQ], cum4[:, :, 0:NSEQ - 1, LS - 1], reads=B("cum"), writes=[gc.b])
            self.tt(cum4, cum4, gc.t[:, :, :].unsqueeze(3).to_broadcast([128, 4, NSEQ, LS]), ALU.subtract,
                    reads=B("cum") + [gc.b], writes=B("cum"))
        self.act(R("eg"), R("cum"), AF.Exp, reads=B("cum"), writes=B("eg"))
        self.tt(R("tmp"), R("cum"), R("logw"), ALU.subtract, reads=B("cum", "logw"), writes=B("tmp"))
        self.act(R("ee"), R("tmp"), AF.Exp, reads=B("tmp"), writes=B("ee"))
        self.act(R("ei"), R("cum"), AF.Exp, reads=B("cum"), writes=B("ei"), scale=-1.0)
        yield
        if sample:
            eg4 = rw["eg"].t[:, :, 0:nt].rearrange("p c (s j) -> p c s j", j=LS)
            self.cp(gc.t[:, :, :], eg4[:, :, :, LS - 1], reads=B("eg"), writes=[gc.b])
        else:
            self.cp(gc.t[:, :, 0:1], rw["eg"].t[:, :, nt - 1:nt], reads=B("eg"), writes=[gc.b])
        self.tt(R("rt"), r_, R("eg"), ALU.mult, reads=[pm.b] + B("eg"), writes=B("eg"))
        self.tt(R("kt"), R("km"), R("ei"), ALU.mult, reads=B("km", "ei"), writes=B("km"))
        self.tt(R("bt"), R("b"), R("ei"), ALU.mult, reads=B("b", "ei"), writes=B("b"))
        self.stt(R("at"), R("kk"), -1.0, R("ee"), ALU.mult, ALU.mult, reads=B("kk", "ee"), writes=B("ee"))
        yield
        for src, srcb, dstt in ((v_, pm.b, self.vtk), (R("kt"), rw["kt"].b, self.ktok), (R("bt"), rw["bt"].b, self.btok)):
            pq = self.ps()
            for c in range(4):
                self.tr(pq.t[0:nt, c * 128:(c + 1) * 128], src[:, c, :], self.ident.t[:, :],
                        reads=[srcb, self.ident.b], writes=[pq.b], signal=(c == 3))
            self.cp(dstt.t[0:nt, :], pq.t[0:nt, :], reads=[pq.b], writes=[dstt.b], eng="act")
        vtk, ktok, btok = self.vtk, self.ktok, self.btok
        rt, kt, bt, at = rw["rt"], rw["kt"], rw["bt"], rw["at"]
        py = self.ps(hold=True)
        idn = self.ident.t[0:nt, 0:nt]
        def pair(c):
            if sample:
                for hd in range(2):
                    S.dma(self.q_misc, self.wkin.t[:, :, hd, :], self.d_st_wkv[:, 2 * c + hd, :, :].rearrange("s v k -> v s k"),
                          writes=[self.wkin.b])
                for s0 in range(0, NSEQ, 8):
                    p = self.ps()
                    for s in range(s0, s0 + 8):
                        self.tr(p.t[:, (s - s0) * 64:(s - s0 + 1) * 64], self.wkin.t[:, s, :, :].rearrange("v h k -> v (h k)"),
                                self.ident.t[0:64, 0:64], reads=[self.wkin.b, self.ident.b], writes=[p.b], signal=(s == s0 + 7))
                    self.cp(self.Hwkv_s.t[:, s0:s0 + 8, :], p.t[:, :].rearrange("p (s v) -> p s v", v=64), reads=[p.b],
                            writes=[self.Hwkv_s.b], eng="act")
                Hb = self.Hwkv_s.b
            else:
                Hb = self.Hwkv.bs[c]
            mAs, mBs, mPs = [], [], []
            atm = _rr(self.atms, self.am_i)
            rtm = _rr(self.rtms, self.am_i)
            self.am_i += 1
            hsel = self.hsel
            for hd in range(2):
                self.ts(atm.t[:, hd, 0:nt], at.t[:, c, 0:nt], hsel.t[:, hd:hd + 1], None, ALU.mult, None,
                        reads=[at.b, hsel.b], writes=[atm.b])
                self.ts(rtm.t[:, hd, 0:nt], rt.t[:, c, 0:nt], hsel.t[:, hd:hd + 1], None, ALU.mult, None,
                        reads=[rt.b, hsel.b], writes=[rtm.b])
            for hd in range(2):
                p1 = self.ps()
                pb2 = self.ps()
                A_ = atm.t[:, hd, 0:nt]
                R_ = rtm.t[:, hd, 0:nt]
                pairs = ((bt.t[:, c, 0:nt], A_, [bt.b, atm.b]), (A_, bt.t[:, c, 0:nt], [bt.b, atm.b]),
                         (kt.t[:, c, 0:nt], A_, [kt.b, atm.b]), (bt.t[:, c, 0:nt], R_, [bt.b, rtm.b]))
                for q, (l_, r2, rd) in enumerate(pairs):
                    self.mm(p1.t[0:nt, q * 128:q * 128 + nt], l_, r2, True, True,
                            reads=rd, writes=[p1.b], signal=(q == 3))
                self.mm(pb2.t[0:nt, hd * 128:hd * 128 + nt], kt.t[:, c, 0:nt], R_, True, True,
                        reads=[kt.b, rtm.b], writes=[pb2.b], signal=True)
                mA = _rr(self.mA, self.mA_i)
                self.mA_i += 1
                mB = _rr(self.mB, self.mB_i)
                self.mB_i += 1
                self.tt(mA.t[0:nt, :, 0:nt], p1.t[0:nt, :].rearrange("p (q t) -> p q t", t=128)[:, :, 0:nt], wmask.t[0:nt, 0:4, 0:nt],
                        ALU.mult, reads=[p1.b, wmask.b], writes=[mA.b])
                self.tt(mB.t[0:nt, 0:nt], pb2.t[0:nt, hd * 128:hd * 128 + nt], wmask.t[0:nt, 4, 0:nt], ALU.mult,
                        reads=[pb2.b, wmask.b], writes=[mB.b])
                if Builder.BF16_CHAIN:
                    mP = _rr(self.mP, self.mP_i)
                    self.mP_i += 1
                    self.cp(mP.t[0:nt, :, 0:nt], mA.t[0:nt, 0:2, 0:nt], reads=[mA.b], writes=[mP.b], eng="act")
                    mPs.append(mP)
                mAs.append(mA)
                mBs.append(mB)
                yield
            yield
            px = self.ps()
            for hd in range(2):
                h = 2 * c + hd
                sl = slice(64 * hd, 64 * hd + 64)
                o = px.t[0:nt, hd * 64:(hd + 1) * 64]
                if sample:
                    self.tt(self.qm.t[:, :, :], atm.t[:, hd, 0:nt].unsqueeze(1).to_broadcast([128, NSEQ, 64]), self.seqselT.t[:, :, :],
                            ALU.mult, reads=[atm.b, self.seqselT.b], writes=[self.qm.b])
                    for s in range(NSEQ):
                        self.mm(o, self.qm.t[:, s, :], self.Hwkv_s.t[:, s, :], s == 0, False,
                                reads=[self.qm.b, Hb], writes=[px.b], signal=False)
                else:
                    self.mm(o, atm.t[:, hd, 0:nt], self.Hwkv.t[:, c, :], True, False, reads=[atm.b, Hb], writes=[px.b], signal=False)
                self.mm(o, mAs[hd].t[0:nt, 2, 0:nt], vtk.t[0:nt, h * 64:(h + 1) * 64], False, True,
                        reads=[mAs[hd].b, vtk.b], writes=[px.b], signal=True)
            bfc = Builder.BF16_CHAIN
            if bfc:
                U = _rr(self.Ub, self.Ub_i)
                self.Ub_i += 1
            else:
                U = _rr(self.Up, self.Up_i)
                self.Up_i += 1
            self.cp(U.t[0:nt, :], px.t[0:nt, 0:128], reads=[px.b], writes=[U.b], eng="act")
            if bfc:
                PT = [mPs[0].t[0:nt, 0, 0:nt], mPs[1].t[0:nt, 0, 0:nt]]
                Pm = [mPs[0].t[0:nt, 1, 0:nt], mPs[1].t[0:nt, 1, 0:nt]]
                Pb = [[mPs[0].b], [mPs[1].b]]
                idl, idb = self.identb.t[0:nt, 0:nt], self.identb.b
            else:
                PT = [mAs[0].t[0:nt, 0, 0:nt], mAs[1].t[0:nt, 0, 0:nt]]
                Pm = [mAs[0].t[0:nt, 1, 0:nt], mAs[1].t[0:nt, 1, 0:nt]]
                Pb = [[mAs[0].b], [mAs[1].b]]
                idl, idb = idn, self.ident.b
            for lv in range(nlev):
                pu = self.ps()
                for hd in range(2):
                    o = pu.t[0:nt, hd * 64:(hd + 1) * 64]
                    self.mm(o, idl, U.t[0:nt, hd * 64:(hd + 1) * 64], True, False, reads=[idb, U.b], writes=[pu.b], signal=False)
                    self.mm(o, PT[hd], U.t[0:nt, hd * 64:(hd + 1) * 64], False, True, reads=Pb[hd] + [U.b], writes=[pu.b], signal=(hd == 1))
                if bfc and lv < nlev - 1:
                    Un = _rr(self.Ub, self.Ub_i)
                    self.Ub_i += 1
                else:
                    Un = _rr(self.Up, self.Up_i)
                    self.Up_i += 1
                self.cp(Un.t[0:nt, :], pu.t[0:nt, 0:128], reads=[pu.b], writes=[Un.b], eng="act")
                U = Un
                yield
                if lv < nlev - 1:
                    pp = self.ps()
                    for hd in range(2):
                        self.mm(pp.t[0:nt, (2 * hd) * 128:(2 * hd) * 128 + nt], Pm[hd], PT[hd], True, True,
                                reads=Pb[hd], writes=[pp.b], signal=False)
                        self.mm(pp.t[0:nt, (2 * hd + 1) * 128:(2 * hd + 1) * 128 + nt], PT[hd], Pm[hd], True, True,
                                reads=Pb[hd], writes=[pp.b], signal=(hd == 1))
                    if bfc:
                        PPt = _rr(self.PPb, self.PPb_i)
                        self.PPb_i += 1
                    else:
                        PPt = _rr(self.PP, self.PP_i)
                        self.PP_i += 1
                    self.cp(PPt.t[0:nt, :, 0:nt], pp.t[0:nt, :].rearrange("p (q t) -> p q t", t=128)[:, :, 0:nt],
                            reads=[pp.b], writes=[PPt.b], eng=Builder.PP_ENG)
                    PT = [PPt.t[0:nt, 0, 0:nt], PPt.t[0:nt, 2, 0:nt]]
                    Pm = [PPt.t[0:nt, 1, 0:nt], PPt.t[0:nt, 3, 0:nt]]
                    Pb = [[PPt.b], [PPt.b]]
                    yield
            yield
            for hd in range(2):
                h = 2 * c + hd
                sl = slice(64 * hd, 64 * hd + 64)
                o = py.t[0:nt, h * 64:(h + 1) * 64]
                if sample:
                    self.tt(self.qm2.t[:, :, :], rtm.t[:, hd, 0:nt].unsqueeze(1).to_broadcast([128, NSEQ, 64]), self.seqselT.t[:, :, :],
                            ALU.mult, reads=[rtm.b, self.seqselT.b], writes=[self.qm2.b])
                    for s in range(NSEQ):
                        self.mm(o, self.qm2.t[:, s, :], self.Hwkv_s.t[:, s, :], s == 0, False,
                                reads=[self.qm2.b, Hb], writes=[py.b], signal=False)
                else:
                    self.mm(o, rtm.t[:, hd, 0:nt], self.Hwkv.t[:, c, :], True, False, reads=[rtm.b, Hb], writes=[py.b], signal=False)
                self.mm(o, mAs[hd].t[0:nt, 3, 0:nt], U.t[0:nt, hd * 64:(hd + 1) * 64], False, False,
                        reads=[mAs[hd].b, U.b], writes=[py.b], signal=False)
                self.mm(o, mBs[hd].t[0:nt, 0:nt], vtk.t[0:nt, h * 64:(h + 1) * 64], False, True,
                        reads=[mBs[hd].b, vtk.b], writes=[py.b], signal=True)
            yield
            if not sample:
                ph = self.ps()
                self.mm(ph.t[:, 0:128], btok.t[0:nt, c * 128:(c + 1) * 128], U.t[0:nt, :], True, False,
                        reads=[btok.b, U.b], writes=[ph.b], signal=False)
                self.mm(ph.t[:, 0:128], ktok.t[0:nt, c * 128:(c + 1) * 128], vtk.t[0:nt, c * 128:(c + 1) * 128], False, True,
                        reads=[ktok.b, vtk.b], writes=[ph.b], signal=True)
                for hd in range(2):
                    sl = slice(64 * hd, 64 * hd + 64)
                    self.tt(self.Hwkv.t[sl, c, :], self.Hwkv.t[sl, c, :], ph.t[sl, 64 * hd:64 * hd + 64], ALU.add,
                            reads=[Hb, ph.b], writes=[Hb])
                    self.ts(self.Hwkv.t[sl, c, :], self.Hwkv.t[sl, c, :], gc.t[sl, c, 0:1], None, ALU.mult, None,
                            reads=[Hb, gc.b], writes=[Hb])
            else:
                self.tt(self.km.t[:, :, :], btok.t[0:64, c * 128:(c + 1) * 128].unsqueeze(1).to_broadcast([64, NSEQ, 128]),
                        self.seqsel.t[:, :].unsqueeze(2).to_broadcast([64, NSEQ, 128]), ALU.mult,
                        reads=[btok.b, self.seqsel.b], writes=[self.km.b])
                self.tt(self.km2.t[:, :, :], ktok.t[0:64, c * 128:(c + 1) * 128].unsqueeze(1).to_broadcast([64, NSEQ, 128]),
                        self.seqsel.t[:, :].unsqueeze(2).to_broadcast([64, NSEQ, 128]), ALU.mult,
                        reads=[ktok.b, self.seqsel.b], writes=[self.km2.b])
                for s0 in range(0, NSEQ, 4):
                    ph = self.ps()
                    for s in range(s0, s0 + 4):
                        o = ph.t[:, (s - s0) * 128:(s - s0 + 1) * 128]
                        self.mm(o, self.km.t[:, s, :], U.t[0:nt, :], True, False, reads=[self.km.b, U.b], writes=[ph.b], signal=False)
                        self.mm(o, self.km2.t[:, s, :], vtk.t[0:nt, c * 128:(c + 1) * 128], False, True,
                                reads=[self.km2.b, vtk.b], writes=[ph.b], signal=(s == s0 + 3))
                    for hd in range(2):
                        sl = slice(64 * hd, 64 * hd + 64)
                        hv = self.Hwkv_s.t[sl, s0:s0 + 4, :]
                        pv_ = ph.t[sl, :].rearrange("p (s x) -> p s x", x=128)[:, :, 64 * hd:64 * hd + 64]
                        self.tt(hv, hv, pv_, ALU.add, reads=[Hb, ph.b], writes=[Hb])
                        self.tt(hv, hv, gc.t[sl, c, s0:s0 + 4].unsqueeze(2).to_broadcast([64, 4, 64]), ALU.mult,
                                reads=[Hb, gc.b], writes=[Hb])
                for s0 in range(0, NSEQ, 4):
                    p = self.ps()
                    for s in range(s0, s0 + 4):
                        self.tr(p.t[0:64, (s - s0) * 128:(s - s0 + 1) * 128], self.Hwkv_s.t[:, s, :], self.ident.t[:, :],
                                reads=[Hb, self.ident.b], writes=[p.b], signal=(s == s0 + 3))
                    self.cp(self.wkin.t[:, s0:s0 + 4, :, :].rearrange("v s h k -> v s (h k)"),
                            p.t[0:64, :].rearrange("v (s x) -> v s x", x=128), reads=[p.b], writes=[self.wkin.b], eng="act")
                for hd in range(2):
                    ev = S.dma(self.q_misc, self.d_wkv_s[:, 2 * c + hd, :, :].rearrange("s v k -> v s k"), self.wkin.t[:, :, hd, :],
                               reads=[self.wkin.b])
                    self.out_events.append(ev)

        if sample or not Builder.PAIR_IL:
            for c in range(4):
                yield from pair(c)
        else:
            for cs in (((0, 1, 2, 3),) if Builder.NPAIR_IL == 4 else ((0, 1), (2, 3))):
                gens = [pair(c) for c in cs]
                alive = [True] * len(gens)
                while any(alive):
                    for gi, g_ in enumerate(gens):
                        if alive[gi]:
                            try:
                                next(g_)
                            except StopIteration:
                                alive[gi] = False
                    yield
        yield
        ysb, ysq, yst = self.ysb, self.ysq, self.yst
        self.cp(ysb.t[0:nt, :], py.t[0:nt, :], reads=[py.b], writes=[ysb.b], eng="act")
        self.release(py)
        y3 = ysb.t[0:nt, :].rearrange("p (h v) -> p h v", v=64)
        q3 = ysq.t[0:nt, :].rearrange("p (h v) -> p h v", v=64)
        S.op("dve", lambda e: e.tensor_reduce(out=yst.t[0:nt, 0, :], in_=y3, axis=AX.X, op=ALU.add), reads=[ysb.b], writes=[yst.b])
        self.ts(yst.t[0:nt, 1, :], yst.t[0:nt, 0, :], -1.0 / 64, None, ALU.mult, None, reads=[yst.b], writes=[yst.b])
        self.tt(y3, y3, yst.t[0:nt, 1, :].unsqueeze(2).to_broadcast([nt, 8, 64]), ALU.add, reads=[ysb.b, yst.b], writes=[ysb.b])
        self.act(ysq.t[0:nt, :], ysb.t[0:nt, :], AF.Square, reads=[ysb.b], writes=[ysq.b])
        S.op("dve", lambda e: e.tensor_reduce(out=yst.t[0:nt, 2, :], in_=q3, axis=AX.X, op=ALU.add), reads=[ysq.b], writes=[yst.b])
        self.act(yst.t[0:nt, 3, :], yst.t[0:nt, 2, :], AF.Sqrt, reads=[yst.b, self.eps_norm.b], writes=[yst.b],
                 bias=self.eps_norm.t[0:nt, 2:3], scale=1.0 / 64)
        S.op("dve", lambda e: e.reciprocal(out=yst.t[0:nt, 3, :], in_=yst.t[0:nt, 3, :]), reads=[yst.b], writes=[yst.b])
        self.tt(y3, y3, yst.t[0:nt, 3, :].unsqueeze(2).to_broadcast([nt, 8, 64]), ALU.mult, reads=[ysb.b, yst.b], writes=[ysb.b])
        pt = self.ps()
        for c in range(4):
            self.tr(pt.t[:, c * 128:c * 128 + nt], ysb.t[0:nt, c * 128:(c + 1) * 128], idn,
                    reads=[ysb.b, self.ident.b], writes=[pt.b], signal=(c == 3))
        self.tt(R("tmp"), pt.t[:, :].rearrange("p (c t) -> p c t", t=128)[:, :, 0:nt], self.bc4(self.vcol("lnx_g"), nt), ALU.mult,
                reads=[pt.b, V.b], writes=B("tmp"))
        self.tt(R("tmp"), R("tmp"), self.bc4(self.vcol("lnx_b"), nt), ALU.add, reads=B("tmp") + [V.b], writes=B("tmp"))
        self.tt(R("tmp"), R("tmp"), R("bon"), ALU.add, reads=B("tmp", "bon"), writes=B("tmp"))
        self.tt(self.orwT.t[:, :, c0:c0 + nt], R("tmp"), R("g"), ALU.mult, reads=B("tmp", "g"), writes=self.orwT.bs)
        yield
        if sample:
            pz = self.ps()
            pc4 = pcur.rearrange("p c (s j) -> p c s j", j=LS)
            for ch in range(14):
                self.tr(pz.t[0:NSEQ, (ch % 4) * 128:(ch % 4 + 1) * 128], pc4[:, ch, :, LS - 1], self.ident.t[:, :],
                        reads=[pT.b, self.ident.b], writes=[pz.b], signal=(ch % 4 == 3 or ch == 13))
                if ch % 4 == 3 or ch == 13:
                    c_lo = ch - (ch % 4)
                    self.cp(self.shs.t[0:NSEQ, c_lo * 128:(ch + 1) * 128], pz.t[0:NSEQ, 0:(ch - c_lo + 1) * 128], reads=[pz.b],
                            writes=[self.shs.b], eng="act")
                    if ch != 13:
                        pz = self.ps()
            ev = S.dma(self.q_misc, self.d_shift_s[:, :], self.shs.t[0:NSEQ, :], reads=[self.shs.b])
            self.out_events.append(ev)
        else:
            is_last_prompt_tile_of_block = (kind == "meta") or (c0 + nt == self.TBcur)
            if is_last_prompt_tile_of_block:
                self.cp(self.pprev.t[:, :], pT.t[:, :, c0 + nt], reads=[pT.b], writes=[self.pprev.b], eng="act")

    def store_wkv(self, H, dst, s):
        S = self.S
        p = self.ps()
        for c in range(4):
            self.tr(p.t[0:64, c * 128:(c + 1) * 128], self.Hwkv.t[:, c, :], self.ident.t[:, :],
                    reads=[self.Hwkv.bs[c], self.ident.b], writes=[p.b], signal=(c == 3))
        self.cp(self.ysb.t[0:64, :], p.t[0:64, :], reads=[p.b], writes=[self.ysb.b], eng="act")
        ev = S.dma(self.q_misc, self.d_wkv_p.rearrange("h v k -> v h k"), self.ysb.t[0:64, :].rearrange("v (h k) -> v h k", k=64),
                   reads=[self.ysb.b])
        self.out_events.append(ev)
        p2 = self.ps()
        self.tr(p2.t[0:14, 0:128], self.pprev.t[:, :], self.ident.t[:, :], reads=[self.pprev.b, self.ident.b], writes=[p2.b])
        self.cp(self.osq.t[0:14, 0:128], p2.t[0:14, 0:128], reads=[p2.b], writes=[self.osq.b], eng="act")
        ev = S.dma(self.q_misc, self.d_shift_p[:, :], self.osq.t[0:14, 0:128], reads=[self.osq.b])
        self.out_events.append(ev)


def pack_vecs(cfg, inp):
    def fm(v):
        v = np.asarray(v, np.float32).reshape(-1)
        return v.reshape(-1, 128).T
    cols = [fm(inp["ffn1_norm"]), fm(inp["mix_norm"]), fm(inp["ffn2_norm"]), fm(inp["final_norm"]),
            fm(inp["mu_shift"]), fm(inp["w0"]), fm(inp["a0"]), fm(inp["k_k"]), fm(inp["k_a"]),
            fm(inp["r_k"]), fm(inp["lnx_g"]), fm(inp["lnx_b"])]
    return np.ascontiguousarray(np.concatenate(cols, axis=1), dtype=np.float32)


def make_in_maps(cfg, inp, n_cores):
    consts = host_constants(cfg)
    vecs = pack_vecs(cfg, inp)
    gng = np.ascontiguousarray(np.broadcast_to(np.asarray(inp["ret_gn_g"], np.float32)[None, :], (128, 512)))
    maps = []
    for c in range(n_cores):
        m = {
            "xp": np.ascontiguousarray(inp["x_prompt"][c]),
            "xs": np.ascontiguousarray(inp["x_sample"][c * NSEQ:(c + 1) * NSEQ].reshape(NSEQ * LS, cfg.DM)),
            "meta": np.ascontiguousarray(inp["meta_tokens"]),
            "st_ret": np.ascontiguousarray(inp["state_ret"][c * NSEQ:(c + 1) * NSEQ]),
            "st_wkv": np.ascontiguousarray(inp["state_wkv"][c * NSEQ:(c + 1) * NSEQ]),
            "st_shift": np.ascontiguousarray(inp["state_shift"][c * NSEQ:(c + 1) * NSEQ]),
            "vecs": vecs, "gng": gng,
            "w2": np.ascontiguousarray(inp["w2"]), "a2": np.ascontiguousarray(inp["a2"]),
            "g2": np.ascontiguousarray(inp["g2"]),
        }
        for k in Builder.WEIGHTS:
            m[k] = np.ascontiguousarray(inp[k], dtype=np.float32)
        for k, v in consts.items():
            if not k.startswith("_"):
                m["c_" + k] = v
        maps.append(m)
    return maps


def gather_outputs(cfg, res, n_cores):
    B = n_cores
    yp = np.stack([res[c]["yp"] for c in range(B)]).reshape(B, 128 * cfg.NT, cfg.DM)
    ys = np.concatenate([res[c]["ys"].reshape(NSEQ, LS, cfg.DM) for c in range(B)])
    ret_p = np.stack([res[c]["ret_p"] for c in range(B)])
    wkv_p = np.stack([res[c]["wkv_p"] for c in range(B)])
    shift_p = np.stack([res[c]["shift_p"].reshape(SHIFT_W) for c in range(B)])
    ret_s = np.concatenate([res[c]["ret_s"] for c in range(B)])
    wkv_s = np.concatenate([res[c]["wkv_s"] for c in range(B)])
    shift_s = np.concatenate([res[c]["shift_s"] for c in range(B)])
    return tuple(np.ascontiguousarray(a, dtype=np.float32) for a in (yp, ys, ret_p, wkv_p, shift_p, ret_s, wkv_s, shift_s))


def kernel(**inputs):
    cfg = Cfg()
    inp = {k: np.asarray(v) for k, v in inputs.items()}
    nc = Builder(cfg).build()
    in_maps = make_in_maps(cfg, inp, N_CORES)
    res = run_bass_kernel_spmd(nc, in_maps, core_ids=list(range(N_CORES)))
    return gather_outputs(cfg, res.results, N_CORES)
```

```python
import contextlib
import numpy as np
import concourse.bass as bass
import concourse.mybir as mybir
from concourse.bass_utils import run_bass_kernel_spmd

F32 = mybir.dt.float32
BF16 = mybir.dt.bfloat16
ALU = mybir.AluOpType
AF = mybir.ActivationFunctionType
AX = mybir.AxisListType

N_CORES = 8
N_META = 16
RET_HEADS, RET_DK, RET_DV = 4, 64, 128
RWKV_HEADS, RWKV_HD = 8, 64
RWKV_W = 512
SHIFT_W = 1792
ROPE_BASE = 10000.0
NORM_EPS = 1e-6
RET_GN_EPS = 1e-6
RWKV_GN_EPS = 64e-5
PAST_LEN = 16384
LS = 4
NSEQ = 16
SAME_ENGINE_SYNC = True


class Buf:
    __slots__ = ("w", "r")

    def __init__(self):
        self.w = None
        self.r = []


class Sched:
    ENG = ("pe", "act", "dve", "pool", "sp")

    def __init__(self, nc, es, n_dma_sems=20):
        self.nc = nc
        self.prog = {e: [] for e in self.ENG}
        self.cnt = {e: 0 for e in self.ENG}
        self.waited = {e: {} for e in self.ENG}
        self.sems = {}
        for e in self.ENG:
            self.sems[e] = es.enter_context(nc.semaphore("s_" + e))
        self.dma_sems = {}
        self.dma_rr = {}
        self.dma_uses = {}
        for q in ("sp", "pool", "act"):
            ks = []
            for i in range(n_dma_sems if q != "act" else 8):
                k = "d_%s_%d" % (q, i)
                self.sems[k] = es.enter_context(nc.semaphore(k))
                self.dma_uses[k] = 0
                ks.append(k)
            self.dma_sems[q] = ks
            self.dma_rr[q] = 0
        self.n_ops = 0

    def _deps(self, e, reads, writes):
        deps = {}

        def add(ev):
            if ev is None:
                return
            k, v = ev
            if deps.get(k, 0) < v:
                deps[k] = v

        for b in reads:
            add(b.w)
        for b in writes:
            add(b.w)
            for ev in b.r:
                add(ev)
        waits = []
        for k, v in deps.items():
            if k == e:
                if e == "pe" or not SAME_ENGINE_SYNC:
                    continue
                if v > self.cnt[e]:
                    continue
            if self.waited[e].get(k, 0) >= v:
                continue
            if k in self.cnt and k != e and v > self.cnt[k]:
                import traceback
                self.pending_waits = getattr(self, "pending_waits", 0) + 1
                if self.pending_waits <= 3:
                    print("WARNING: %s waits on unsignalled %s event %d (cnt %d)" % (e, k, v, self.cnt[k]))
                    traceback.print_stack(limit=6)
            self.waited[e][k] = v
            waits.append((k, v))
        return waits

    def op(self, e, fn, reads=(), writes=(), signal=True):
        waits = self._deps(e, reads, writes)
        ev = (e, self.cnt[e] + 1)
        if signal:
            self.cnt[e] += 1
        self.prog[e].append((waits, fn, (e, 1) if signal else None))
        for b in reads:
            b.r.append(ev)
        for b in writes:
            b.w = ev
            b.r = []
        self.n_ops += 1

    def dma(self, q, out, in_, reads=(), writes=()):
        waits = self._deps(q, reads, writes)
        ks = self.dma_sems[q]
        k = ks[self.dma_rr[q] % len(ks)]
        self.dma_rr[q] += 1
        prev = 16 * self.dma_uses[k]
        if prev and self.waited[q].get(k, 0) < prev:
            self.waited[q][k] = prev
            waits.append((k, prev))
        self.dma_uses[k] += 1
        ev = (k, 16 * self.dma_uses[k])
        self.prog[q].append((waits, lambda eng, o=out, i=in_: eng.dma_start(out=o, in_=i), (k, 16)))
        for b in reads:
            b.r.append(ev)
        for b in writes:
            b.w = ev
            b.r = []
        return ev

    def wait_event(self, e, ev):
        k, v = ev
        if self.waited[e].get(k, 0) < v:
            self.waited[e][k] = v
            self.prog[e].append(([(k, v)], None, None))

    def barrier(self):
        for e in self.ENG:
            for f in self.ENG:
                if f != e and self.cnt[f] > 0:
                    self.wait_event(e, (f, self.cnt[f]))
            for k, u in self.dma_uses.items():
                if u > 0:
                    self.wait_event(e, (k, 16 * u))

    def emit(self, block):
        nc = self.nc
        sems = self.sems

        def runner(name):
            def run(eng):
                for waits, fn, sig in self.prog[name]:
                    for k, v in waits:
                        eng.wait_ge(sems[k], v)
                    if fn is None:
                        continue
                    ins = fn(eng)
                    if sig is not None:
                        ins.then_inc(sems[sig[0]], sig[1])
            return run

        block.tensor(runner("pe"))
        block.scalar(runner("act"))
        block.vector(runner("dve"))
        block.gpsimd(runner("pool"))
        block.sync(runner("sp"))
        self._clear = True

    def clear(self):
        self.prog = {e: [] for e in self.ENG}


def _rr(lst, i):
    return lst[i % len(lst)]


class Cfg:
    def __init__(self, DM=1024, DFF=2816, NT=16):
        self.DM, self.DFF, self.NT = DM, DFF, NT
        self.KC = DM // 128
        self.FC = DFF // 128
        self.PROJ_W = 512 + 1024 + SHIFT_W + 2 * DM
        self.NSP = N_META + NSEQ * LS
        self.TB = 512
        assert NT % 2 == 0 or NT < 2
        self.TPB = min(2, NT)
        self.NBLK = 1 + NT // self.TPB
        self.NCOL = self.NSP + 128 * NT


def host_constants(cfg):
    f32 = np.float32
    c = {}
    c["ident"] = np.eye(128, dtype=f32)
    bo = np.zeros((128, 128), f32)
    bo[:64, :64] = 1.0
    bo[64:, 64:] = 1.0
    c["blockones"] = bo
    hs = np.zeros((128, 2), f32)
    hs[:64, 0] = 1.0
    hs[64:, 1] = 1.0
    c["hsel"] = hs
    P = np.zeros((128, 128), f32)
    for p in range(128):
        d = p % 64
        src = p + 32 if d < 32 else p - 32
        P[src, p] = 1.0
    c["rotP"] = P
    half = 32
    inv_freq = (f32(ROPE_BASE) ** (-np.arange(half, dtype=f32) / f32(half))).astype(f32)
    pos = np.concatenate([
        np.arange(N_META, dtype=np.int32),
        np.tile(PAST_LEN + np.arange(LS, dtype=np.int32), NSEQ),
        N_META + np.arange(128 * cfg.NT, dtype=np.int32),
    ])
    ang = pos.astype(f32)[None, :] * inv_freq[:, None]
    cs, sn = np.cos(ang).astype(f32), np.sin(ang).astype(f32)
    cosT = np.zeros((128, cfg.NCOL), f32)
    sinT = np.zeros((128, cfg.NCOL), f32)
    for p in range(128):
        d = p % 64
        j = d % 32
        cosT[p] = cs[j]
        sinT[p] = -sn[j] if d < 32 else sn[j]
    c["cosT"], c["sinT"] = cosT, sinT
    lg = np.log1p(-np.exp2(-5.0 - np.arange(RET_HEADS, dtype=f32))).astype(f32)
    c["_lg"] = lg
    idx = np.arange(128, dtype=f32)
    diff = idx[None, :] - idx[:, None]
    m = np.zeros((128, 4, 128), f32)
    for h in range(4):
        m[:, h, :] = np.where(diff >= 0, np.exp(lg[h] * np.maximum(diff, 0.0)), 0.0)
    c["rmaskT"] = m
    ms = np.zeros((64, 4, 64), f32)
    seq = np.arange(64) // LS
    same = (seq[:, None] == seq[None, :])
    d64 = (np.arange(64, dtype=f32)[None, :] - np.arange(64, dtype=f32)[:, None])
    for h in range(4):
        ms[:, h, :] = np.where((d64 >= 0) & same, np.exp(lg[h] * np.maximum(d64, 0.0)), 0.0)
    c["rmaskT_s"] = ms
    qd = np.zeros((128, 2, 128), f32)
    qds = np.zeros((128, 2, 64), f32)
    for ch in range(2):
        for hp in range(2):
            h = 2 * ch + hp
            qd[64 * hp:64 * hp + 64, ch, :] = np.exp(lg[h] * (idx + 1.0))[None, :]
            qds[64 * hp:64 * hp + 64, ch, :] = np.exp(lg[h] * ((np.arange(64) % LS) + 1.0))[None, :].astype(f32)
    c["qdec"], c["qdec_s"] = qd, qds
    kd = np.zeros((128, 3, 4), f32)
    for h in range(4):
        kd[:, 0, h] = np.exp(lg[h] * (127.0 - idx))
        kd[:16, 1, h] = np.exp(lg[h] * (15.0 - idx[:16]))
        kd[:64, 2, h] = np.exp(lg[h] * (LS - 1.0 - (np.arange(64) % LS)))
    c["kdec"] = kd
    su = (idx[:, None] < idx[None, :]).astype(f32)
    iu = (idx[:, None] <= idx[None, :]).astype(f32)
    sl = (idx[:, None] > idx[None, :]).astype(f32)
    wm = np.stack([su, sl, su, iu, iu, su], axis=1)
    c["wmask"] = wm.astype(f32)
    sm = same.astype(f32)
    wms = np.zeros((64, 6, 64), f32)
    for i, mm in enumerate([su, sl, su, iu, iu, su]):
        wms[:, i, :] = mm[:64, :64] * sm
    c["wmask_s"] = wms
    c["seqsel"] = (seq[:, None] == np.arange(NSEQ)[None, :]).astype(f32)
    c["seqselT"] = np.broadcast_to((np.arange(NSEQ)[:, None] == seq[None, :]).astype(f32)[None], (128, NSEQ, 64)).copy()
    return c


class Tile:
    def __init__(self, t, nb=1):
        self.t = t
        self.bs = [Buf() for _ in range(nb)]

    @property
    def b(self):
        return self.bs[0]


class Builder:
    D_PER_M = 1
    PAIR_IL = True
    NPAIR_IL = 2
    NW8 = 11
    SEP_RET = True
    BF16_CHAIN = True
    PP_ENG = "dve"

    def __init__(self, cfg, debug=()):
        self.cfg = cfg
        self.debug = debug
        self.dbg_outs = {}

    def sb(self, name, shape, dt=F32, nb=1):
        self._uid = getattr(self, "_uid", 0) + 1
        t = self.es.enter_context(self.nc.sbuf_tensor("%s_%d" % (name, self._uid), list(shape), dt))
        return Tile(t, nb)

    def dram_in(self, name, shape):
        return self.nc.dram_tensor(name, list(shape), F32, kind="ExternalInput").ap()

    def dram_out(self, name, shape):
        return self.nc.dram_tensor(name, list(shape), F32, kind="ExternalOutput").ap()

    def ps(self, hold=False):
        while True:
            p = self.psb[self.ps_i % len(self.psb)]
            self.ps_i += 1
            if id(p) not in self.held:
                break
        if hold:
            self.held.add(id(p))
        return p

    def release(self, p):
        self.held.discard(id(p))

    def mm(self, out, lhsT, rhs, start, stop, reads, writes, signal):
        self.S.op("pe", lambda e: e.matmul(out, lhsT, rhs, start=start, stop=stop),
                  reads=reads, writes=writes, signal=signal)

    def tr(self, out, in_, ident, reads, writes, signal=True):
        self.S.op("pe", lambda e: e.transpose(out, in_, ident), reads=reads, writes=writes, signal=signal)

    def act(self, out, in_, func, reads, writes, bias=None, scale=None):
        kw = {}
        if bias is not None:
            kw["bias"] = bias
        if scale is not None:
            kw["scale"] = scale
        self.S.op("act", lambda e: e.activation(out=out, in_=in_, func=func, **kw), reads=reads, writes=writes)

    def tt(self, out, in0, in1, op, reads, writes, eng="dve"):
        self.S.op(eng, lambda e: e.tensor_tensor(out=out, in0=in0, in1=in1, op=op), reads=reads, writes=writes)

    def ts(self, out, in0, s1, s2, op0, op1, reads, writes, eng="dve"):
        if op1 is None:
            self.S.op(eng, lambda e: e.tensor_scalar(out=out, in0=in0, scalar1=s1, scalar2=None, op0=op0),
                      reads=reads, writes=writes)
        else:
            self.S.op(eng, lambda e: e.tensor_scalar(out=out, in0=in0, scalar1=s1, scalar2=s2, op0=op0, op1=op1),
                      reads=reads, writes=writes)

    def stt(self, out, in0, scalar, in1, op0, op1, reads, writes, eng="dve"):
        self.S.op(eng, lambda e: e.scalar_tensor_tensor(out=out, in0=in0, scalar=scalar, in1=in1, op0=op0, op1=op1),
                  reads=reads, writes=writes)

    def cp(self, out, in_, reads, writes, eng="dve"):
        if eng == "act":
            self.S.op("act", lambda e: e.copy(out=out, in_=in_), reads=reads, writes=writes)
        else:
            self.S.op(eng, lambda e: e.tensor_copy(out=out, in_=in_), reads=reads, writes=writes)

    def dbg(self, name, tile_ap, shape, reads):
        if name not in self.debug:
            return
        d = self.dram_out("dbg_" + name, shape)
        self.dbg_outs[name] = shape
        ev = self.S.dma(self.q_misc, d, tile_ap, reads=reads, writes=[])
        self.out_events.append(ev)

    def _wsrc(self, name, nk, j, k0, k1):
        if self.use_scr or (name, j, k0) in self.scr_done:
            return self.d_scr[name][j, :, k0:k1, :], [self.scr_buf[(name, j, k0)]]
        W = self.d_w[name]
        return W[k0 * 128:k1 * 128, j * 128:(j + 1) * 128].rearrange("(k p) m -> p k m", p=128), []

    def _wload(self, t, name, nk, c0, dst3=None):
        j = c0 // 128
        for k0 in range(0, nk, 8):
            k1 = min(nk, k0 + 8)
            src, rb = self._wsrc(name, nk, j, k0, k1)
            dst = t.t[:, k0:k1, :] if dst3 is None else dst3[:, k0:k1, :]
            from_scr = self.use_scr or (name, j, k0) in self.scr_done
            q = self.q_w if self.use_scr else "pool"
            self.S.dma(q, dst, src, reads=rb, writes=[t.b])
            if self.d_scr and not from_scr:
                key = (name, j, k0)
                self.scr_done.add(key)
                self.S.dma("sp", self.d_scr[name][j, :, k0:k1, :], dst, reads=[t.b], writes=[self.scr_buf[key]])

    def load_w8(self, name, nk, c0):
        t = _rr(self.w8, self.w8_i)
        self.w8_i += 1
        self._wload(t, name, nk, c0)
        return t

    def load_wd(self, name, nk, c0):
        t = _rr(self.wd, self.wd_i)
        self.wd_i += 1
        self._wload(t, name, nk, c0)
        return t

    def load_wt(self, name, nk, c0):
        t = _rr(self.wt, self.wt_i)
        self.wt_i += 1
        for q in range(4):
            self._wload(t, name, nk, c0 + q * 128, dst3=t.t[:, :, q * 128:(q + 1) * 128])
        return t

    def conv_phase_b(self):
        cfg = self.cfg
        names = ["w_out_ret", "w_out_rwkv", "w_out", "ffn2_w_gate", "ffn2_w_up", "ffn2_w_down"]
        todo = [("w_in", j) for j in range((512 + 1024 + SHIFT_W) // 128, cfg.PROJ_W // 128)]
        for name in names:
            todo += [(name, j) for j in range(self.weight_shapes()[name][1] // 128)]
        for name, j in todo:
            R_ = self.weight_shapes()[name][0] // 128
            for k0 in range(0, R_, 8):
                self.conv_queue.append((name, j, k0, min(R_, k0 + 8)))
        while self.conv_queue:
            self.emit_conv(1)
            yield

    def emit_conv(self, n):
        while n > 0 and self.conv_queue:
            name, j, k0, k1 = self.conv_queue.pop(0)
            W = self.d_w[name]
            src = W[k0 * 128:k1 * 128, j * 128:(j + 1) * 128].rearrange("(k p) m -> p k m", p=128)
            self.S.dma("pool", self.d_scr[name][j, :, k0:k1, :], src, reads=[], writes=[self.scr_buf[(name, j, k0)]])
            self.scr_done.add((name, j, k0))
            n -= 1

    def rms_stats(self, xT, n):
        cfg = self.cfg
        KC = cfg.KC
        pss = self.ps()
        for kc in range(KC):
            sq = _rr(self.sq, self.sq_i)
            self.sq_i += 1
            self.act(sq.t[:, 0:n], xT.t[:, kc, 0:n], AF.Square, reads=[xT.bs[kc]], writes=[sq.b])
            self.mm(pss.t[:, 0:n], self.ones_bf.t[:, :], sq.t[:, 0:n], kc == 0, kc == KC - 1,
                    reads=[sq.b, self.ones_bf.b], writes=[pss.b], signal=True)
        rs = self.rstd
        self.act(rs.t[:, 0:n], pss.t[:, 0:n], AF.Sqrt, reads=[pss.b, self.eps_norm.b], writes=[rs.b],
                 bias=self.eps_norm.t[:, 0:1], scale=1.0 / cfg.DM)
        self.S.op("dve", lambda e: e.reciprocal(out=rs.t[:, 0:n], in_=rs.t[:, 0:n]), reads=[rs.b], writes=[rs.b])

    def rms_apply(self, xT, gcol, out, c0, nt, o0):
        rs = self.rstd
        for kc in range(self.cfg.KC):
            self.stt(out.t[:, kc, o0:o0 + nt], xT.t[:, kc, c0:c0 + nt], self.vec.t[:, gcol + kc:gcol + kc + 1],
                     rs.t[:, c0:c0 + nt], ALU.mult, ALU.mult, reads=[xT.bs[kc], rs.b, self.vec.b], writes=[out.bs[kc]])

    def rmsnorm(self, xT, n, gcol, out):
        self.rms_stats(xT, n)
        self.rms_apply(xT, gcol, out, 0, n, 0)

    def ffn(self, xT, n, gcol, Wg, Wu, Wd):
        cfg = self.cfg
        KC, FC = cfg.KC, cfg.FC
        xn, h1 = self.xn, self.h1
        self.rmsnorm(xT, n, gcol, xn)
        yield
        for j in range(FC):
            wg = self.load_w8(Wg, KC, j * 128)
            wu = self.load_w8(Wu, KC, j * 128)
            pg, pu = self.ps(hold=True), self.ps(hold=True)
            for kc in range(KC):
                self.mm(pg.t[:, 0:n], wg.t[:, kc, :], xn.t[:, kc, 0:n], kc == 0, kc == KC - 1,
                        reads=[wg.b, xn.bs[kc]], writes=[pg.b], signal=(kc == KC - 1))
            yield
            for kc in range(KC):
                self.mm(pu.t[:, 0:n], wu.t[:, kc, :], xn.t[:, kc, 0:n], kc == 0, kc == KC - 1,
                        reads=[wu.b, xn.bs[kc]], writes=[pu.b], signal=(kc == KC - 1))
            sg = _rr(self.sg, self.sg_i)
            self.sg_i += 1
            self.act(sg.t[:, 0:n], pg.t[:, 0:n], AF.Silu, reads=[pg.b], writes=[sg.b])
            self.tt(h1.t[:, j, 0:n], sg.t[:, 0:n], pu.t[:, 0:n], ALU.mult, reads=[sg.b, pu.b], writes=[h1.bs[j]])
            self.release(pg)
            self.release(pu)
            yield
        for m in range(KC):
            wd = self.load_wd(Wd, FC, m * 128)
            po = self.ps(hold=True)
            for j in range(FC):
                self.mm(po.t[:, 0:n], wd.t[:, j, :], h1.t[:, j, 0:n], j == 0, j == FC - 1,
                        reads=[wd.b, h1.bs[j]], writes=[po.b], signal=(j == FC - 1))
                if j % 8 == 7 and j != FC - 1:
                    yield
            self.stt(xT.t[:, m, 0:n], po.t[:, 0:n], 0.5, xT.t[:, m, 0:n], ALU.mult, ALU.add,
                     reads=[po.b, xT.bs[m]], writes=[xT.bs[m]])
            self.release(po)
            yield

    def load_block_x(self, blk, xT):
        cfg = self.cfg
        KC = cfg.KC
        if blk == 0:
            tiles = [(0, cfg.NSP, None)]
        else:
            tiles = [(128 * i, 128, (blk - 1) * cfg.TPB + i) for i in range(cfg.TPB)]
        for c0, nt, pt in tiles:
            xin = _rr(self.xin, self.xin_i)
            self.xin_i += 1
            if pt is None:
                self.S.dma(self.q_misc, xin.t[0:N_META, :], self.d_meta[:, :], reads=[], writes=[xin.b])
                self.S.dma(self.q_misc, xin.t[N_META:cfg.NSP, :], self.d_xs[:, :], reads=[], writes=[xin.b])
            else:
                self.S.dma(self.q_misc, xin.t[:, :], self.d_xp[pt * 128:(pt + 1) * 128, :], reads=[], writes=[xin.b])
            for k0 in range(0, KC, 4):
                nk = min(4, KC - k0)
                p = self.ps()
                for kk in range(nk):
                    self.tr(p.t[:, kk * 128:kk * 128 + nt], xin.t[0:nt, (k0 + kk) * 128:(k0 + kk + 1) * 128],
                            self.ident.t[0:nt, 0:nt], reads=[xin.b, self.ident.b], writes=[p.b], signal=(kk == nk - 1))
                pv = p.t[:, 0:nk * 128].rearrange("p (k t) -> p k t", t=128)
                self.cp(xT.t[:, k0:k0 + nk, c0:c0 + nt], pv[:, :, 0:nt], reads=[p.b],
                        writes=[xT.bs[k] for k in range(k0, k0 + nk)], eng="act")
            yield

    def store_block_y(self, blk, xT, n):
        cfg = self.cfg
        KC = cfg.KC
        yT = self.yT
        self.rms_stats(xT, n)
        if blk == 0:
            tiles = [(N_META, NSEQ * LS, None)]
        else:
            tiles = [(128 * i, 128, (blk - 1) * cfg.TPB + i) for i in range(cfg.TPB)]
        for c0, nt, pt in tiles:
            self.rms_apply(xT, 3 * KC, yT, c0, nt, 0)
            yo = _rr(self.yo, self.yo_i)
            self.yo_i += 1
            for k0 in range(0, KC, 4):
                nk = min(4, KC - k0)
                p = self.ps()
                for kk in range(nk):
                    self.tr(p.t[0:nt, kk * 128:(kk + 1) * 128], yT.t[:, k0 + kk, 0:nt], self.ident.t[:, :],
                            reads=[yT.bs[k0 + kk], self.ident.b], writes=[p.b], signal=(kk == nk - 1))
                self.cp(yo.t[0:nt, k0 * 128:(k0 + nk) * 128], p.t[0:nt, 0:nk * 128], reads=[p.b], writes=[yo.b],
                        eng="act")
            dst = self.d_ys[:, :] if pt is None else self.d_yp[pt * 128:(pt + 1) * 128, :]
            ev = self.S.dma(self.q_misc, dst, yo.t[0:nt, :], reads=[yo.b], writes=[])
            self.out_events.append(ev)
            yield

    def in_proj(self, blk, hT, n, mtiles):
        cfg = self.cfg
        KC = cfg.KC
        uT = self.uT
        self.rmsnorm(hT, n, KC, uT)
        W = "w_in"
        col0 = self.blk_col0(blk)
        self.S.dma(self.q_misc, self.cosb.t[:, 0:n], self.d_c["cosT"][:, col0:col0 + n], reads=[], writes=[self.cosb.b])
        self.S.dma(self.q_misc, self.sinb.t[:, 0:n], self.d_c["sinT"][:, col0:col0 + n], reads=[], writes=[self.sinb.b])
        qkr = self.qkr
        for c in range(4):
            w = self.load_w8(W, KC, c * 128)
            p = self.ps()
            for kc in range(KC):
                self.mm(p.t[:, 0:n], w.t[:, kc, :], uT.t[:, kc, 0:n], kc == 0, kc == KC - 1,
                        reads=[w.b, uT.bs[kc]], writes=[p.b], signal=(kc == KC - 1))
            qc = _rr(self.sg, self.sg_i)
            self.sg_i += 1
            self.act(qc.t[:, 0:n], p.t[:, 0:n], AF.Copy, reads=[p.b], writes=[qc.b],
                     scale=(1.0 if c < 2 else RET_DK ** -0.5))
            p2 = self.ps()
            self.mm(p2.t[:, 0:n], self.rotP.t[:, :], qc.t[:, 0:n], True, True,
                    reads=[self.rotP.b, qc.b], writes=[p2.b], signal=True)
            t1 = _rr(self.sg, self.sg_i)
            self.sg_i += 1
            self.tt(t1.t[:, 0:n], qc.t[:, 0:n], self.cosb.t[:, 0:n], ALU.mult, reads=[qc.b, self.cosb.b], writes=[t1.b])
            self.tt(qkr.t[:, c, 0:n], p2.t[:, 0:n], self.sinb.t[:, 0:n], ALU.mult, reads=[p2.b, self.sinb.b], writes=[qkr.bs[c]])
            self.tt(qkr.t[:, c, 0:n], qkr.t[:, c, 0:n], t1.t[:, 0:n], ALU.add, reads=[qkr.bs[c], t1.b], writes=[qkr.bs[c]])
        pT = self.pT
        for ch in range(14):
            w = self.load_w8(W, KC, 1536 + ch * 128)
            p = self.ps()
            for kc in range(KC):
                self.mm(p.t[:, 0:n], w.t[:, kc, :], uT.t[:, kc, 0:n], kc == 0, kc == KC - 1,
                        reads=[w.b, uT.bs[kc]], writes=[p.b], signal=(kc == KC - 1))
            self.cp(pT.t[:, ch, 1:n + 1], p.t[:, 0:n], reads=[p.b], writes=[pT.bs[ch]], eng="act")
        self.cp(pT.t[:, :, 0], self.pprev.t[:, :], reads=[self.pprev.b], writes=[pT.b], eng="act")
        for (cbase, is_g) in ((512, False), (1024, True)):
            banks = [self.ps(hold=True) for _ in mtiles]
            for q in range(4):
                w = self.load_w8(W, KC, cbase + q * 128)
                for i, (c0, nt, kind) in enumerate(mtiles):
                    for kc in range(KC):
                        self.mm(banks[i].t[0:nt, q * 128:(q + 1) * 128], uT.t[:, kc, c0:c0 + nt], w.t[:, kc, :], kc == 0, kc == KC - 1,
                                reads=[w.b, uT.bs[kc]], writes=[banks[i].b], signal=(kc == KC - 1))
            for i, (c0, nt, kind) in enumerate(mtiles):
                if is_g:
                    self.act(self.gtok.t[0:nt, i, :], banks[i].t[0:nt, :], AF.Silu, reads=[banks[i].b], writes=[self.gtok.bs[i]])
                else:
                    self.cp(self.vtok.t[0:nt, i, :], banks[i].t[0:nt, :], reads=[banks[i].b], writes=[self.vtok.bs[i]], eng="dve")
                self.release(banks[i])

    @staticmethod
    def run(gen):
        if gen is not None:
            for _ in gen:
                pass

    @staticmethod
    def interleave(gm, gd, m_per_d):
        am, ad = True, True
        while am or ad:
            for _ in range(m_per_d):
                if am:
                    try:
                        next(gm)
                    except StopIteration:
                        am = False
            for _ in range(Builder.D_PER_M):
                if ad:
                    try:
                        next(gd)
                    except StopIteration:
                        ad = False

    def blk_col0(self, blk):
        return 0 if blk == 0 else self.cfg.NSP + (blk - 1) * 128 * self.cfg.TPB

    def merge_out(self, blk, hT, n):
        cfg = self.cfg
        KC = cfg.KC
        uT, oret, orw, mg = self.uT, self.oretT, self.orwT, self.mgT
        W = "w_in"
        gcol = 512 + 1024 + SHIFT_W
        for m in range(KC):
            wa = self.load_w8("w_out_ret", 4, m * 128)
            wb = self.load_w8("w_out_rwkv", 4, m * 128)
            wga = self.load_w8(W, KC, gcol + m * 128)
            wgb = self.load_w8(W, KC, gcol + cfg.DM + m * 128)
            pa, pb, pga, pgb = self.ps(), self.ps(), self.ps(), self.ps()
            for c in range(4):
                self.mm(pa.t[:, 0:n], wa.t[:, c, :], oret.t[:, c, 0:n], c == 0, c == 3,
                        reads=[wa.b, oret.bs[c]], writes=[pa.b], signal=(c == 3))
            for c in range(4):
                self.mm(pb.t[:, 0:n], wb.t[:, c, :], orw.t[:, c, 0:n], c == 0, c == 3,
                        reads=[wb.b, orw.bs[c]], writes=[pb.b], signal=(c == 3))
            for kc in range(KC):
                self.mm(pga.t[:, 0:n], wga.t[:, kc, :], uT.t[:, kc, 0:n], kc == 0, kc == KC - 1,
                        reads=[wga.b, uT.bs[kc]], writes=[pga.b], signal=(kc == KC - 1))
            for kc in range(KC):
                self.mm(pgb.t[:, 0:n], wgb.t[:, kc, :], uT.t[:, kc, 0:n], kc == 0, kc == KC - 1,
                        reads=[wgb.b, uT.bs[kc]], writes=[pgb.b], signal=(kc == KC - 1))
            ga = _rr(self.sg, self.sg_i)
            self.sg_i += 1
            gb = _rr(self.sg, self.sg_i)
            self.sg_i += 1
            self.act(ga.t[:, 0:n], pga.t[:, 0:n], AF.Sigmoid, reads=[pga.b], writes=[ga.b])
            self.act(gb.t[:, 0:n], pgb.t[:, 0:n], AF.Sigmoid, reads=[pgb.b], writes=[gb.b])
            self.tt(ga.t[:, 0:n], ga.t[:, 0:n], pa.t[:, 0:n], ALU.mult, reads=[ga.b, pa.b], writes=[ga.b])
            self.tt(gb.t[:, 0:n], gb.t[:, 0:n], pb.t[:, 0:n], ALU.mult, reads=[gb.b, pb.b], writes=[gb.b])
            self.tt(mg.t[:, m, 0:n], ga.t[:, 0:n], gb.t[:, 0:n], ALU.add, reads=[ga.b, gb.b], writes=[mg.bs[m]])
        for m in range(KC):
            w = self.load_w8("w_out", KC, m * 128)
            po = self.ps()
            for kc in range(KC):
                self.mm(po.t[:, 0:n], w.t[:, kc, :], mg.t[:, kc, 0:n], kc == 0, kc == KC - 1,
                        reads=[w.b, mg.bs[kc]], writes=[po.b], signal=(kc == KC - 1))
            self.tt(hT.t[:, m, 0:n], hT.t[:, m, 0:n], po.t[:, 0:n], ALU.add, reads=[hT.bs[m], po.b], writes=[hT.bs[m]])

    WEIGHTS = ("ffn1_w_gate", "ffn1_w_up", "ffn1_w_down", "w_in", "w_out_ret", "w_out_rwkv", "w_out",
               "ffn2_w_gate", "ffn2_w_up", "ffn2_w_down")

    def weight_shapes(self):
        c = self.cfg
        return {"ffn1_w_gate": (c.DM, c.DFF), "ffn1_w_up": (c.DM, c.DFF), "ffn1_w_down": (c.DFF, c.DM),
                "w_in": (c.DM, c.PROJ_W), "w_out_ret": (512, c.DM), "w_out_rwkv": (512, c.DM), "w_out": (c.DM, c.DM),
                "ffn2_w_gate": (c.DM, c.DFF), "ffn2_w_up": (c.DM, c.DFF), "ffn2_w_down": (c.DFF, c.DM)}

    def alloc_block_buffers(self, es, TB, pipelined=False):
        cfg = self.cfg
        KC, FC = cfg.KC, cfg.FC
        self.es = es
        self.TBcur = TB
        self.xT = self.sb("xT", [128, KC, TB], F32, KC)
        self.xTs = [self.xT, self.sb("xTb", [128, KC, TB], F32, KC)] if pipelined else [self.xT, self.xT]
        self.xn = self.sb("xn", [128, KC, TB], BF16, KC)
        self.mgT = self.xn
        self.uT = self.sb("uT", [128, KC, TB], BF16, KC)
        nbig = 14 * (TB + 1) if pipelined else max(14 * (TB + 1), (FC * TB + 1) // 2)
        big = self.sb("big", [128, nbig], F32, 1)
        self.big = big
        self.pT = Tile(big.t[:, 0:14 * (TB + 1)].rearrange("p (c t) -> p c t", t=TB + 1), 1)
        self.pT.bs = [big.b] * 14
        if pipelined:
            self.h1 = self.sb("h1", [128, FC, TB], BF16, FC)
        else:
            h1ap = big.t[:, :].bitcast(BF16)[:, 0:FC * TB].rearrange("p (f t) -> p f t", t=TB)
            self.h1 = Tile(h1ap, 1)
            self.h1.bs = [big.b] * FC
        self.yT = self.sb("yT", [128, KC, 128], F32, KC)
        self.sq = [self.sb("sq%d" % i, [128, TB], BF16) for i in range(2)]
        self.rstd = self.sb("rstd", [128, TB], F32)
        self.sg = [self.sb("sg%d" % i, [128, TB], F32) for i in range(3)]
        self.cosb = self.sb("cosb", [128, TB], F32)
        self.sinb = self.sb("sinb", [128, TB], F32)
        self.qkr = self.sb("qkr", [128, 4, TB], F32, 4)
        ntile = max(2, TB // 128)
        self.vtok = self.sb("vtok", [128, ntile, 512], F32, ntile)
        self.gtok = self.sb("gtok", [128, ntile, 512], BF16, ntile)
        self.oretT = self.sb("oretT", [128, 4, TB], BF16, 4)
        self.orwT = self.sb("orwT", [128, 4, TB], BF16, 4)
        self.sq_i = self.sg_i = 0

    def build(self, stub_mixers=False, stub_rwkv=False):
        cfg = self.cfg
        KC, FC = cfg.KC, cfg.FC
        nc = bass.Bass("TRN2", target_bir_lowering=False)
        self.nc = nc
        self.out_events = []
        self.d_xp = self.dram_in("xp", [128 * cfg.NT, cfg.DM])
        self.d_xs = self.dram_in("xs", [NSEQ * LS, cfg.DM])
        self.d_meta = self.dram_in("meta", [N_META, cfg.DM])
        self.d_st_ret = self.dram_in("st_ret", [NSEQ, 4, 64, 128])
        self.d_st_wkv = self.dram_in("st_wkv", [NSEQ, 8, 64, 64])
        self.d_st_shift = self.dram_in("st_shift", [NSEQ, SHIFT_W])
        self.NV = 4 * KC + 14 + 7 * 4
        self.d_vec = self.dram_in("vecs", [128, self.NV])
        self.d_gng = self.dram_in("gng", [128, 512])
        self.d_small = {"w2": self.dram_in("w2", [64, 512]), "a2": self.dram_in("a2", [64, 512]),
                        "g2": self.dram_in("g2", [128, 512])}
        self.d_w = {k: self.dram_in(k, list(s)) for k, s in self.weight_shapes().items()}
        self.d_scr, self.scr_buf, self.conv_queue = {}, {}, []
        self.scr_done = set()
        self.wq_i = 0
        self.use_scr = False
        self.q_misc, self.q_w = "sp", "pool"
        if getattr(self, "scratch", True) and cfg.NBLK > 2:
            order = ["ffn2_w_gate", "ffn2_w_up", "ffn2_w_down", "ffn1_w_gate", "ffn1_w_up", "ffn1_w_down",
                     "w_in", "w_out_ret", "w_out_rwkv", "w_out"]
            for name in ["ffn1_w_gate", "ffn1_w_up", "ffn1_w_down", "w_in", "w_out_ret", "w_out_rwkv", "w_out",
                         "ffn2_w_gate", "ffn2_w_up", "ffn2_w_down"]:
                R_, C_ = self.weight_shapes()[name]
                self.d_scr[name] = nc.dram_tensor("scr_" + name, [C_ // 128, 128, R_ // 128, 128], BF16, kind="Internal").ap()
                for j in range(C_ // 128):
                    for k0 in range(0, R_ // 128, 8):
                        self.scr_buf[(name, j, k0)] = Buf()
        consts = host_constants(cfg)
        self.d_c = {k: self.dram_in("c_" + k, list(v.shape)) for k, v in consts.items() if not k.startswith("_")}
        self.lg = consts["_lg"]
        self.d_yp = self.dram_out("yp", [128 * cfg.NT, cfg.DM])
        self.d_ys = self.dram_out("ys", [NSEQ * LS, cfg.DM])
        self.d_ret_p = self.dram_out("ret_p", [4, 64, 128])
        self.d_wkv_p = self.dram_out("wkv_p", [8, 64, 64])
        self.d_shift_p = self.dram_out("shift_p", [14, 128])
        self.d_ret_s = self.dram_out("ret_s", [NSEQ, 4, 64, 128])
        self.d_wkv_s = self.dram_out("wkv_s", [NSEQ, 8, 64, 64])
        self.d_shift_s = self.dram_out("shift_s", [NSEQ, SHIFT_W])

        with contextlib.ExitStack() as es:
            self.es = es
            self.S = Sched(nc, es)
            S = self.S
            self.psb = []
            for i in range(8):
                t = es.enter_context(nc.psum_tensor("ps%d" % i, [128, 512], F32))
                self.psb.append(Tile(t))
            self.ps_i = 0
            self.held = set()
            if getattr(self, "pad_kb", 0):
                self.sb("pad", [128, self.pad_kb * 256])
            self.vec = self.sb("vec", [128, self.NV])
            self.ident = self.sb("ident", [128, 128])
            self.identb = self.sb("identb", [128, 128], BF16)
            self.ones_bf = self.sb("ones_bf", [128, 128], BF16)
            self.rotP = self.sb("rotP", [128, 128])
            self.blockones = self.sb("blockones", [128, 128])
            self.eps_norm = self.sb("eps_norm", [128, 4])
            self.w8 = [self.sb("w8_%d" % i, [128, 8, 128], BF16) for i in range(Builder.NW8)]
            self.wd = [self.sb("wd_%d" % i, [128, FC, 128], BF16) for i in range(2)]
            self.xin = [self.sb("xin%d" % i, [128, cfg.DM]) for i in range(1)]
            self.yo = self.xin
            self.w8_i = self.wd_i = self.wt_i = self.xin_i = self.yo_i = 0
            self.alloc_common_mixer()
            S.dma(self.q_misc, self.vec.t[:, :], self.d_vec[:, :], writes=[self.vec.b])
            S.dma(self.q_misc, self.ident.t[:, :], self.d_c["ident"][:, :], writes=[self.ident.b])
            S.dma(self.q_misc, self.rotP.t[:, :], self.d_c["rotP"][:, :], writes=[self.rotP.b])
            S.dma(self.q_misc, self.blockones.t[:, :], self.d_c["blockones"][:, :], writes=[self.blockones.b])
            S.op("dve", lambda e: e.memset(self.ones_bf.t[:, :], 1.0), writes=[self.ones_bf.b])
            S.op("dve", lambda e: e.tensor_copy(out=self.identb.t[:, :], in_=self.ident.t[:, :]),
                 reads=[self.ident.b], writes=[self.identb.b])
            S.op("dve", lambda e: e.memset(self.eps_norm.t[:, 0:1], NORM_EPS), writes=[self.eps_norm.b])
            S.op("dve", lambda e: e.memset(self.eps_norm.t[:, 1:2], RET_GN_EPS), writes=[self.eps_norm.b])
            S.op("dve", lambda e: e.memset(self.eps_norm.t[:, 2:3], RWKV_GN_EPS), writes=[self.eps_norm.b])
            self.load_common_mixer()

            for scope in (0, 1):
                with contextlib.ExitStack() as es2:
                    TB = cfg.NSP if scope == 0 else 128 * cfg.TPB
                    pipelined = (scope == 1) and (not stub_mixers) and getattr(self, "pipeline", True) and cfg.NBLK > 2
                    self.alloc_block_buffers(es2, TB, pipelined=pipelined)
                    if scope == 0:
                        self.alloc_sample(es2)
                    if stub_mixers or stub_rwkv:
                        for t in ((self.oretT, self.orwT) if (stub_mixers or getattr(self, "rtrunc", 99) < 5) else (self.orwT,)):
                            S.op("dve", lambda e, t=t: e.memset(t.t[:, :, :], 0.0), writes=t.bs)
                    self.sbuf_used = max(getattr(self, "sbuf_used", 0), 229344 - nc.sbuf_bytes_remaining)
                    self.sbuf_scope = getattr(self, "sbuf_scope", []) + [229344 - nc.sbuf_bytes_remaining]
                    blocks = [0] if scope == 0 else list(range(1, cfg.NBLK))
                    W = self.d_w
                    F1 = ("ffn1_w_gate", "ffn1_w_up", "ffn1_w_down")
                    F2 = ("ffn2_w_gate", "ffn2_w_up", "ffn2_w_down")

                    def mtiles_of(blk):
                        if blk == 0:
                            return [(0, N_META, "meta"), (N_META, NSEQ * LS, "sample")]
                        return [(128 * i, 128, "full") for i in range(cfg.TPB)]

                    def mixers(blk):
                        mt = mtiles_of(blk)
                        if pipelined and Builder.SEP_RET and not stub_rwkv:
                            def R_():
                                for i, (c0, nt, kind) in enumerate(mt):
                                    yield from self.retention(blk, i, c0, nt, kind)

                            def W_():
                                for i, (c0, nt, kind) in enumerate(mt):
                                    yield from self.rwkv(blk, i, c0, nt, kind, last=(blk == cfg.NBLK - 1 and i == len(mt) - 1))
                            gr, gw = R_(), W_()
                            ar = aw = True
                            while ar or aw:
                                for _ in range(2):
                                    if aw:
                                        try:
                                            next(gw)
                                            yield
                                        except StopIteration:
                                            aw = False
                                if ar:
                                    try:
                                        next(gr)
                                        yield
                                    except StopIteration:
                                        ar = False
                            return
                        for i, (c0, nt, kind) in enumerate(mt):
                            yield from self.retention(blk, i, c0, nt, kind)
                            if not stub_rwkv:
                                yield from self.rwkv(blk, i, c0, nt, kind, last=(blk == cfg.NBLK - 1 and i == len(mt) - 1))

                    run = self.run
                    if not pipelined:
                        for blk in blocks:
                            n = cfg.NSP if blk == 0 else 128 * cfg.TPB
                            xT = self.xTs[0]
                            trunc = getattr(self, "trunc", 99)
                            run(self.load_block_x(blk, xT))
                            if trunc >= 2:
                                run(self.ffn(xT, n, 0, *F1))
                            if trunc >= 3:
                                self.in_proj(blk, xT, n, mtiles_of(blk))
                            if not stub_mixers:
                                if scope == 0 and self.d_scr:
                                    self.interleave(mixers(blk), self.conv_phase_b(), 2)
                                else:
                                    run(mixers(blk))
                            if trunc >= 4:
                                self.merge_out(blk, xT, n)
                            if trunc >= 5:
                                run(self.ffn(xT, n, 2 * KC, *F2))
                            if trunc >= 1:
                                run(self.store_block_y(blk, xT, n))
                    else:
                        n = 128 * cfg.TPB
                        first, lastb = blocks[0], blocks[-1]
                        x0 = self.xTs[first % 2]
                        run(self.load_block_x(first, x0))
                        run(self.ffn(x0, n, 0, *F1))
                        self.in_proj(first, x0, n, mtiles_of(first))
                        for blk in blocks:
                            def dense(blk=blk):
                                if blk > first:
                                    xp = self.xTs[(blk - 1) % 2]
                                    yield from self.ffn(xp, n, 2 * KC, *F2)
                                    yield from self.store_block_y(blk - 1, xp, n)
                                if blk < lastb:
                                    xq = self.xTs[(blk + 1) % 2]
                                    yield from self.load_block_x(blk + 1, xq)
                                    yield from self.ffn(xq, n, 0, *F1)
                            self.interleave(mixers(blk), dense(), getattr(self, "m_per_d", 1))
                            self.merge_out(blk, self.xTs[blk % 2], n)
                            if blk < lastb:
                                self.in_proj(blk + 1, self.xTs[(blk + 1) % 2], n, mtiles_of(blk + 1))
                        xl = self.xTs[lastb % 2]
                        run(self.ffn(xl, n, 2 * KC, *F2))
                        run(self.store_block_y(lastb, xl, n))
                    if scope == 0 and self.d_scr:
                        assert len(self.scr_done) == len(self.scr_buf), (len(self.scr_done), len(self.scr_buf))
                        self.use_scr = True
                        self.q_misc, self.q_w = "pool", "sp"
                    if scope == 1:
                        if not stub_mixers:
                            self.store_prompt_states()
                        for ev in self.out_events:
                            S.wait_event("sp", ev)
                    S.barrier()
                    with nc.Block() as block:
                        S.emit(block)
                    S.clear()
        return nc

    def alloc_common_mixer(self):
        self.rmaskT = self.sb("rmaskT", [128, 4, 128])
        self.qdec = self.sb("qdec", [128, 2, 128])
        self.kdec = self.sb("kdec", [128, 3, 4])
        self.gng = self.sb("gng", [128, 512])
        self.wmask = self.sb("wmask", [128, 6, 128])
        self.Sret = self.sb("Sret", [128, 2, 128], F32, 2)
        self.Hwkv = self.sb("Hwkv", [128, 4, 64], F32, 4)
        self.pprev = self.sb("pprev", [128, 14])
        self.w2t = self.sb("w2t", [128, 512])
        self.a2t = self.sb("a2t", [128, 512])
        self.g2t = self.sb("g2t", [128, 512])
        self.omka = self.sb("omka", [128, 4])
        self.onesf = self.sb("onesf", [128, 128])
        self.qtm = self.sb("qtm", [128, 2, 2, 128])
        self.kmsk = self.sb("kmskh", [128, 2, 2, 128])
        self.hsel = self.sb("hsel", [128, 2])
        self.atm = self.sb("atm", [128, 2, 128])
        self.rtm = self.sb("rtm", [128, 2, 128])
        self.ktk = self.sb("ktk", [128, 4, 64])
        self.sc = self.sb("sc", [128, 4, 128])
        self.osb = self.sb("osb", [128, 512])
        self.osq = self.sb("osq", [128, 512])
        self.onb = self.sb("onb", [128, 512], BF16)
        self.st4 = self.sb("st4", [128, 4, 4])
        self.pm = self.sb("pm", [128, 14, 128])
        self.rw = {nm: self.sb("rw_" + nm, [128, 4, 128]) for nm in
                   ("logw", "cum", "eg", "ee", "a", "kk", "km", "b", "g", "bon", "tmp")}
        for a_, b_ in (("rt", "eg"), ("kt", "km"), ("bt", "b"), ("at", "ee"), ("ei", "tmp")):
            self.rw[a_] = self.rw[b_]
        self.thw = self.sb("thw", [128, 128])
        self.sgx = self.sb("sgx", [128, 128])
        self.vtk = self.sb("vtk", [128, 512])
        self.ktok = self.sb("ktok", [128, 512])
        self.btok = self.sb("btok", [128, 512])
        self.mA = [self.sb("mA%d" % i, [128, 4, 128]) for i in range(2)]
        self.mB = [self.sb("mB%d" % i, [128, 128]) for i in range(2)]
        self.PP = [] if Builder.BF16_CHAIN else [self.sb("PP%d" % i, [128, 4, 128]) for i in range(2)]
        self.Up = [self.sb("Up%d" % i, [128, 128]) for i in range(3)]
        self.ysb = self.osb
        self.ysq = self.osq
        self.yst = self.sb("yst", [128, 4, 8])
        self.gcum = self.sb("gcum", [128, 4, NSEQ])
        self.mA_i = self.mB_i = self.PP_i = self.Up_i = self.am_i = 0

        def alias(tile, view):
            t = Tile(view)
            t.bs = tile.bs
            return t
        if Builder.SEP_RET:
            self.mA = self.mA + [self.sb("mAs%d" % i, [128, 4, 128]) for i in range(2)]
            if not Builder.BF16_CHAIN:
                self.PP = self.PP + [self.sb("PPs%d" % i, [128, 4, 128]) for i in range(2)]
            self.ysb = self.sb("ysb", [128, 512])
            self.ysq = self.sb("ysq", [128, 512])
        else:
            self.mA = self.mA + [self.sc, alias(self.qtm, self.qtm.t[:, :, :, :].rearrange("p a b t -> p (a b) t"))]
            self.PP = self.PP + [alias(self.kmsk, self.kmsk.t[:, :, :, :].rearrange("p a b t -> p (a b) t")),
                                 alias(self.osb, self.osb.t[:, :].rearrange("p (q t) -> p q t", t=128))]
        self.mP = [self.sb("mP%d" % i, [128, 2, 128], BF16) for i in range(4)]
        self.PPb = [self.sb("PPb%d" % i, [128, 4, 128], BF16) for i in range(4)]
        self.Ub = [self.sb("Ub%d" % i, [128, 128], BF16) for i in range(4)]
        self.mP_i = self.PPb_i = self.Ub_i = 0
        np_ = Builder.NPAIR_IL
        if np_ == 4:
            self.mA = self.mA + [self.sb("mAx%d" % i, [128, 4, 128]) for i in range(4)]
            self.PP = self.PP + [self.sb("PPx%d" % i, [128, 4, 128]) for i in range(4)]
        self.mB = self.mB + [self.sb("mB%d" % i, [128, 128]) for i in range(2, 2 * np_)]
        self.Up = self.Up + [self.sb("Upx%d" % i, [128, 128]) for i in range(3, 2 * np_)]
        self.atms = [self.atm] + [self.sb("atm%d" % i, [128, 2, 128]) for i in range(2, np_ + 1)]
        self.rtms = [self.rtm] + [self.sb("rtm%d" % i, [128, 2, 128]) for i in range(2, np_ + 1)]

    def load_common_mixer(self):
        S = self.S
        for t, k in ((self.rmaskT, "rmaskT"), (self.qdec, "qdec"), (self.kdec, "kdec"), (self.wmask, "wmask")):
            S.dma(self.q_misc, t.t[:], self.d_c[k][:], writes=[t.b])
        S.dma(self.q_misc, self.gng.t[:, :], self.d_gng[:, :], writes=[self.gng.b])
        S.op("dve", lambda e: e.memset(self.w2t.t[:, :], 0.0), writes=[self.w2t.b])
        S.op("dve", lambda e: e.memset(self.a2t.t[:, :], 0.0), writes=[self.a2t.b])
        S.dma(self.q_misc, self.w2t.t[0:64, :], self.d_small["w2"][:, :], writes=[self.w2t.b])
        S.dma(self.q_misc, self.a2t.t[64:128, :], self.d_small["a2"][:, :], writes=[self.a2t.b])
        S.dma(self.q_misc, self.hsel.t[:, :], self.d_c["hsel"][:, :], writes=[self.hsel.b])
        S.dma(self.q_misc, self.g2t.t[:, :], self.d_small["g2"][:, :], writes=[self.g2t.b])
        for t in (self.Sret, self.Hwkv):
            S.op("dve", lambda e, t=t: e.memset(t.t[:], 0.0), writes=t.bs)
        S.op("dve", lambda e: e.memset(self.pprev.t[:, :], 0.0), writes=[self.pprev.b])
        S.op("dve", lambda e: e.memset(self.onesf.t[:, :], 1.0), writes=[self.onesf.b])
        ka = self.vcol("k_a")
        self.ts(self.omka.t[:, :], self.vec.t[:, ka:ka + 4], -1.0, 1.0, ALU.mult, ALU.add,
                reads=[self.vec.b], writes=[self.omka.b])

    def vcol(self, name):
        KC = self.cfg.KC
        off = {"mu": 4 * KC, "w0": 4 * KC + 14, "a0": 4 * KC + 18, "k_k": 4 * KC + 22, "k_a": 4 * KC + 26,
               "r_k": 4 * KC + 30, "lnx_g": 4 * KC + 34, "lnx_b": 4 * KC + 38}
        return off[name]

    def alloc_sample(self, es):
        self.es = es
        S = self.S
        self.rmaskT_s = self.sb("rmaskT_s", [64, 4, 64])
        self.qdec_s = self.sb("qdec_s", [128, 2, 64])
        self.wmask_s = self.sb("wmask_s", [64, 6, 64])
        self.seqsel = self.sb("seqsel", [64, NSEQ])
        self.seqselT = self.sb("seqselT", [128, NSEQ, 64])
        self.Sret_s = self.sb("Sret_s", [128, NSEQ, 128])
        self.Hwkv_s = self.sb("Hwkv_s", [128, NSEQ, 64])
        self.qm = self.sb("qm", [128, NSEQ, 64])
        self.km = self.sb("kmsk", [64, NSEQ, 128])
        self.wkin = self.sb("wkin", [64, NSEQ, 2, 64])
        self.psh = self.sb("psh", [128, 14, 64])
        self.qm2 = self.sb("qm2", [128, NSEQ, 64])
        self.km2 = Tile(self.wkin.t[:, :, :, :].rearrange("v s h k -> v s (h k)"))
        self.km2.bs = self.wkin.bs
        self.shs = Tile(self.km.t[0:NSEQ, :, :].rearrange("p s d -> p (s d)")[:, 0:SHIFT_W])
        self.shs.bs = self.km.bs
        for t, k in ((self.rmaskT_s, "rmaskT_s"), (self.qdec_s, "qdec_s"), (self.wmask_s, "wmask_s"),
                     (self.seqsel, "seqsel"), (self.seqselT, "seqselT")):
            S.dma(self.q_misc, t.t[:], self.d_c[k][:], writes=[t.b])

    def retention(self, blk, i, c0, nt, kind):
        S = self.S
        sample = kind == "sample"
        qkr = self.qkr
        lg = self.lg
        Lseq = LS if sample else nt
        rmask = self.rmaskT_s if sample else self.rmaskT
        qdec = self.qdec_s if sample else self.qdec
        ksel = 2 if sample else (1 if kind == "meta" else 0)
        qtm, ktk, sc, kmsk, hsel = self.qtm, self.ktk, self.sc, self.kmsk, self.hsel
        for hp in range(2):
            self.stt(qtm.t[:, hp, :, 0:nt], qkr.t[:, 0:2, c0:c0 + nt], hsel.t[:, hp:hp + 1], qdec.t[:, :, 0:nt], ALU.mult, ALU.mult,
                     reads=[qkr.bs[0], qkr.bs[1], qdec.b, hsel.b], writes=[qtm.b])
            self.ts(kmsk.t[:, hp, :, 0:nt], qkr.t[:, 2:4, c0:c0 + nt], hsel.t[:, hp:hp + 1], None, ALU.mult, None,
                    reads=[qkr.bs[2], qkr.bs[3], hsel.b], writes=[kmsk.b])
        pk = self.ps()
        for c in range(2):
            self.tr(pk.t[0:nt, c * 128:(c + 1) * 128], qkr.t[:, 2 + c, c0:c0 + nt], self.ident.t[:, :],
                    reads=[qkr.bs[2 + c], self.ident.b], writes=[pk.b], signal=(c == 1))
        self.tt(ktk.t[0:nt, :, :], pk.t[0:nt, 0:256].rearrange("p (h d) -> p h d", d=64),
                self.kdec.t[0:nt, ksel, :].unsqueeze(2).to_broadcast([nt, 4, 64]), ALU.mult,
                reads=[pk.b, self.kdec.b], writes=[ktk.b])
        rtr = getattr(self, "rtrunc", 99)
        yield
        psc = self.ps()
        for h in range(4):
            c, pb = h // 2, 64 * (h % 2)
            self.mm(psc.t[0:nt, h * 128:h * 128 + nt], kmsk.t[:, h % 2, c, 0:nt], qkr.t[:, c, c0:c0 + nt],
                    True, True, reads=[kmsk.b, qkr.bs[c]], writes=[psc.b], signal=(h == 3))
        self.tt(sc.t[0:nt, :, 0:nt], psc.t[0:nt, :].rearrange("p (h t) -> p h t", t=128)[:, :, 0:nt],
                rmask.t[0:nt, :, 0:nt], ALU.mult, reads=[psc.b, rmask.b], writes=[sc.b])
        yield
        po = self.ps(hold=True)
        vt = self.vtok
        for h in range(4):
            c, pb = h // 2, 64 * (h % 2)
            if sample and pb == 0:
                for hp in range(2):
                    src = self.d_st_ret[:, 2 * c + hp, :, :].rearrange("s d v -> d s v")
                    S.dma(self.q_misc, self.Sret_s.t[64 * hp:64 * hp + 64, :, :], src, writes=[self.Sret_s.b])
            if sample:
                self.tt(self.qm.t[:, :, :], qtm.t[:, h % 2, c, 0:nt].unsqueeze(1).to_broadcast([128, NSEQ, 64]),
                        self.seqselT.t[:, :, :], ALU.mult, reads=[qtm.b, self.seqselT.b], writes=[self.qm.b])
            self.mm(po.t[0:nt, h * 128:(h + 1) * 128], sc.t[0:nt, h, 0:nt], vt.t[0:nt, i, h * 128:(h + 1) * 128],
                    True, False, reads=[sc.b, vt.bs[i]], writes=[po.b], signal=False)
            if not sample:
                self.mm(po.t[0:nt, h * 128:(h + 1) * 128], qtm.t[:, h % 2, c, 0:nt], self.Sret.t[:, c, :],
                        False, True, reads=[qtm.b, self.Sret.bs[c]], writes=[po.b], signal=True)
            else:
                for s in range(NSEQ):
                    self.mm(po.t[0:nt, h * 128:(h + 1) * 128], self.qm.t[:, s, :], self.Sret_s.t[:, s, :],
                            False, s == NSEQ - 1, reads=[self.qm.b, self.Sret_s.b], writes=[po.b],
                            signal=(s == NSEQ - 1))
            yield
            if pb == 0:
                continue
            if not sample:
                pu = self.ps()
                self.mm(pu.t[:, 0:256], ktk.t[0:nt, 2 * c:2 * c + 2, :].rearrange("p h d -> p (h d)"),
                        vt.t[0:nt, i, 256 * c:256 * (c + 1)], True, True, reads=[ktk.b, vt.bs[i]], writes=[pu.b], signal=True)
                for hp in range(2):
                    g = float(np.exp(np.float32(lg[2 * c + hp]) * np.float32(Lseq)))
                    self.stt(self.Sret.t[64 * hp:64 * hp + 64, c, :], self.Sret.t[64 * hp:64 * hp + 64, c, :], g,
                             pu.t[64 * hp:64 * hp + 64, 128 * hp:128 * hp + 128], ALU.mult, ALU.add,
                             reads=[self.Sret.bs[c], pu.b], writes=[self.Sret.bs[c]])
            else:
                self.tt(self.km.t[:, :, :], ktk.t[0:64, 2 * c:2 * c + 2, :].rearrange("p h d -> p (h d)").unsqueeze(1).to_broadcast([64, NSEQ, 128]),
                        self.seqsel.t[:, :].unsqueeze(2).to_broadcast([64, NSEQ, 128]), ALU.mult,
                        reads=[ktk.b, self.seqsel.b], writes=[self.km.b])
                for s0 in range(0, NSEQ, 2):
                    pu = self.ps()
                    for s in (s0, s0 + 1):
                        self.mm(pu.t[:, 256 * (s - s0):256 * (s - s0 + 1)], self.km.t[:, s, :], vt.t[0:nt, i, 256 * c:256 * (c + 1)],
                                True, True, reads=[self.km.b, vt.bs[i]], writes=[pu.b], signal=(s == s0 + 1))
                    for hp in range(2):
                        g = float(np.exp(np.float32(lg[2 * c + hp]) * np.float32(Lseq)))
                        sl = slice(64 * hp, 64 * hp + 64)
                        sv = self.Sret_s.t[sl, s0:s0 + 2, :]
                        pv_ = pu.t[sl, :].rearrange("p (s x) -> p s x", x=256)[:, :, 128 * hp:128 * hp + 128]
                        self.stt(sv, sv, g, pv_, ALU.mult, ALU.add, reads=[self.Sret_s.b, pu.b], writes=[self.Sret_s.b])
                for hp in range(2):
                    dst = self.d_ret_s[:, 2 * c + hp, :, :].rearrange("s d v -> d s v")
                    ev = S.dma(self.q_misc, dst, self.Sret_s.t[64 * hp:64 * hp + 64, :, :], reads=[self.Sret_s.b])
                    self.out_events.append(ev)
        yield
        osb, osq, st4 = self.osb, self.osq, self.st4
        self.cp(osb.t[0:nt, :], po.t[0:nt, :], reads=[po.b], writes=[osb.b], eng="act")
        self.release(po)
        o3 = osb.t[0:nt, :].rearrange("p (h v) -> p h v", v=128)
        q3 = osq.t[0:nt, :].rearrange("p (h v) -> p h v", v=128)
        S.op("dve", lambda e: e.tensor_reduce(out=st4.t[0:nt, 0, :], in_=o3, axis=AX.X, op=ALU.add),
             reads=[osb.b], writes=[st4.b])
        self.ts(st4.t[0:nt, 1, :], st4.t[0:nt, 0, :], -1.0 / 128, None, ALU.mult, None, reads=[st4.b], writes=[st4.b])
        self.tt(o3, o3, st4.t[0:nt, 1, :].unsqueeze(2).to_broadcast([nt, 4, 128]), ALU.add, reads=[osb.b, st4.b], writes=[osb.b])
        self.act(osq.t[0:nt, :], osb.t[0:nt, :], AF.Square, reads=[osb.b], writes=[osq.b])
        S.op("dve", lambda e: e.tensor_reduce(out=st4.t[0:nt, 2, :], in_=q3, axis=AX.X, op=ALU.add),
             reads=[osq.b], writes=[st4.b])
        self.act(st4.t[0:nt, 3, :], st4.t[0:nt, 2, :], AF.Sqrt, reads=[st4.b, self.eps_norm.b], writes=[st4.b],
                 bias=self.eps_norm.t[0:nt, 1:2], scale=1.0 / 128)
        S.op("dve", lambda e: e.reciprocal(out=st4.t[0:nt, 3, :], in_=st4.t[0:nt, 3, :]), reads=[st4.b], writes=[st4.b])
        self.tt(o3, o3, st4.t[0:nt, 3, :].unsqueeze(2).to_broadcast([nt, 4, 128]), ALU.mult, reads=[osb.b, st4.b], writes=[osb.b])
        self.tt(osb.t[0:nt, :], osb.t[0:nt, :], self.gng.t[0:nt, :], ALU.mult, reads=[osb.b, self.gng.b], writes=[osb.b])
        self.tt(self.onb.t[0:nt, :], osb.t[0:nt, :], self.gtok.t[0:nt, i, :], ALU.mult,
                reads=[osb.b, self.gtok.bs[i]], writes=[self.onb.b])
        yield
        pt = self.ps()
        ptb = pt.t[:, :].bitcast(BF16)
        for c in range(4):
            self.tr(ptb[:, c * 128:c * 128 + nt], self.onb.t[0:nt, c * 128:(c + 1) * 128], self.identb.t[0:nt, 0:nt],
                    reads=[self.onb.b, self.identb.b], writes=[pt.b], signal=(c == 3))
        self.cp(self.oretT.t[:, :, c0:c0 + nt], ptb[:, 0:512].rearrange("p (c t) -> p c t", t=128)[:, :, 0:nt],
                reads=[pt.b], writes=self.oretT.bs, eng="act")

    def store_prompt_states(self):
        S = self.S
        for c in range(2):
            for hp in range(2):
                ev = S.dma(self.q_misc, self.d_ret_p[2 * c + hp, :, :], self.Sret.t[64 * hp:64 * hp + 64, c, :], reads=[self.Sret.bs[c]])
                self.out_events.append(ev)
        self.store_wkv(self.Hwkv.t, self.d_wkv_p, None)

    def bc4(self, col, nt):
        return self.vec.t[:, col:col + 4].unsqueeze(2).to_broadcast([128, 4, nt])

    def rwkv(self, blk, i, c0, nt, kind, last=False):
        S = self.S
        cfg = self.cfg
        sample = kind == "sample"
        Ls = LS if sample else nt
        nlev = int(np.ceil(np.log2(Ls)))
        nseq = NSEQ if sample else 1
        pT, pm, rw = self.pT, self.pm, self.rw
        wmask = self.wmask_s if sample else self.wmask
        V = self.vec
        pcur = pT.t[:, :, c0 + 1:c0 + nt + 1]
        if sample:
            S.dma(self.q_misc, self.shs.t[:, :], self.d_st_shift[:, :], writes=[self.shs.b])
            pz = self.ps()
            for ch in range(14):
                self.tr(pz.t[:, ch * NSEQ:(ch + 1) * NSEQ], self.shs.t[0:NSEQ, ch * 128:(ch + 1) * 128], self.ident.t[0:NSEQ, 0:NSEQ],
                        reads=[self.shs.b, self.ident.b], writes=[pz.b], signal=(ch == 13))
            psh4 = self.psh.t[:, :, :].rearrange("p c (s j) -> p c s j", j=LS)
            self.cp(psh4[:, :, :, 0], pz.t[:, 0:14 * NSEQ].rearrange("p (c s) -> p c s", s=NSEQ), reads=[pz.b], writes=[self.psh.b])
            pc4 = pcur.rearrange("p c (s j) -> p c s j", j=LS)
            self.cp(psh4[:, :, :, 1:LS], pc4[:, :, :, 0:LS - 1], reads=[pT.b], writes=[self.psh.b])
            pprev_ap = self.psh.t[:, :, 0:nt]
            prev_reads = [self.psh.b]
        else:
            pprev_ap = pT.t[:, :, c0:c0 + nt]
            prev_reads = [pT.b]
        mu0 = self.vcol("mu")
        pmv = pm.t[:, :, 0:nt]
        self.tt(pmv, pprev_ap, pcur, ALU.subtract, reads=prev_reads + [pT.b], writes=[pm.b])
        self.tt(pmv, pmv, V.t[:, mu0:mu0 + 14].unsqueeze(2).to_broadcast([128, 14, nt]), ALU.mult, reads=[pm.b, V.b], writes=[pm.b])
        self.tt(pmv, pmv, pcur, ALU.add, reads=[pm.b, pT.b], writes=[pm.b])
        r_ = pm.t[:, 0:4, 0:nt]
        k_ = pm.t[:, 4:8, 0:nt]
        v_ = pm.t[:, 8:12, 0:nt]

        def R(nm):
            return rw[nm].t[:, :, 0:nt]

        def B(*nms):
            return [rw[n].b for n in nms]

        yield
        self.act(self.thw.t[:, 0:nt], pm.t[:, 12, 0:nt], AF.Tanh, reads=[pm.b], writes=[self.thw.b])
        pz = self.ps()
        for c in range(4):
            self.mm(pz.t[:, c * 128:c * 128 + nt], self.w2t.t[:, c * 128:(c + 1) * 128], self.thw.t[:, 0:nt], True, True,
                    reads=[self.w2t.b, self.thw.b], writes=[pz.b], signal=(c == 3))
        w0c = self.vcol("w0")
        for c in range(4):
            self.act(rw["logw"].t[:, c, 0:nt], pz.t[:, c * 128:c * 128 + nt], AF.Sigmoid, reads=[pz.b, V.b], writes=B("logw"),
                     bias=V.t[:, w0c + c:w0c + c + 1])
        self.ts(R("logw"), R("logw"), -float(np.exp(-0.5)), None, ALU.mult, None, reads=B("logw"), writes=B("logw"))
        yield
        pa = self.ps()
        for c in range(4):
            self.mm(pa.t[:, c * 128:c * 128 + nt], self.a2t.t[:, c * 128:(c + 1) * 128], pm.t[:, 12, 0:nt], True, True,
                    reads=[self.a2t.b, pm.b], writes=[pa.b], signal=(c == 3))
        a0c = self.vcol("a0")
        for c in range(4):
            self.act(rw["a"].t[:, c, 0:nt], pa.t[:, c * 128:c * 128 + nt], AF.Sigmoid, reads=[pa.b, V.b], writes=B("a"),
                     bias=V.t[:, a0c + c:a0c + c + 1])
        yield
        self.act(self.sgx.t[:, 0:nt], pm.t[:, 13, 0:nt], AF.Sigmoid, reads=[pm.b], writes=[self.sgx.b])
        pg = self.ps()
        for c in range(4):
            self.mm(pg.t[:, c * 128:c * 128 + nt], self.g2t.t[:, c * 128:(c + 1) * 128], self.sgx.t[:, 0:nt], True, True,
                    reads=[self.g2t.b, self.sgx.b], writes=[pg.b], signal=(c == 3))
        self.cp(R("g"), pg.t[:, :].rearrange("p (c t) -> p c t", t=128)[:, :, 0:nt], reads=[pg.b], writes=B("g"), eng="act")
        yield
        self.tt(R("kk"), k_, self.bc4(self.vcol("k_k"), nt), ALU.mult, reads=[pm.b, V.b], writes=B("kk"))
        self.tt(R("tmp"), R("kk"), R("kk"), ALU.mult, reads=B("kk"), writes=B("tmp"))
        pn = self.ps()
        for c in range(4):
            self.mm(pn.t[:, c * 128:c * 128 + nt], self.blockones.t[:, :], rw["tmp"].t[:, c, 0:nt], True, True,
                    reads=[self.blockones.b, rw["tmp"].b], writes=[pn.b], signal=(c == 3))
        self.ts(R("tmp"), pn.t[:, :].rearrange("p (c t) -> p c t", t=128)[:, :, 0:nt], 1e-24, None, ALU.max, None,
                reads=[pn.b], writes=B("tmp"))
        self.act(R("tmp"), R("tmp"), AF.Sqrt, reads=B("tmp"), writes=B("tmp"))
        S.op("dve", lambda e: e.reciprocal(out=R("tmp"), in_=R("tmp")), reads=B("tmp"), writes=B("tmp"))
        self.tt(R("kk"), R("kk"), R("tmp"), ALU.mult, reads=B("kk", "tmp"), writes=B("kk"))
        yield
        self.tt(R("tmp"), R("a"), self.bc4(self.vcol("k_a"), nt), ALU.mult, reads=B("a") + [V.b], writes=B("tmp"))
        self.tt(R("tmp"), R("tmp"), self.omka.t[:, :].unsqueeze(2).to_broadcast([128, 4, nt]), ALU.add,
                reads=B("tmp") + [self.omka.b], writes=B("tmp"))
        self.tt(R("km"), k_, R("tmp"), ALU.mult, reads=[pm.b] + B("tmp"), writes=B("km"))
        self.tt(R("b"), R("kk"), R("a"), ALU.mult, reads=B("kk", "a"), writes=B("b"))
        yield
        self.tt(R("tmp"), r_, R("km"), ALU.mult, reads=[pm.b] + B("km"), writes=B("tmp"))
        self.tt(R("tmp"), R("tmp"), self.bc4(self.vcol("r_k"), nt), ALU.mult, reads=B("tmp") + [V.b], writes=B("tmp"))
        pb_ = self.ps()
        for c in range(4):
            self.mm(pb_.t[:, c * 128:c * 128 + nt], self.blockones.t[:, :], rw["tmp"].t[:, c, 0:nt], True, True,
                    reads=[self.blockones.b, rw["tmp"].b], writes=[pb_.b], signal=(c == 3))
        self.tt(R("bon"), pb_.t[:, :].rearrange("p (c t) -> p c t", t=128)[:, :, 0:nt], v_, ALU.mult, reads=[pb_.b, pm.b], writes=B("bon"))
        yield
        for c in range(4):
            S.op("dve", lambda e, c=c: e.tensor_tensor_scan(out=rw["cum"].t[:, c, 0:nt], data0=self.onesf.t[:, 0:nt],
                                                         data1=rw["logw"].t[:, c, 0:nt], initial=0.0, op0=ALU.mult, op1=ALU.add),
                 reads=B("logw") + [self.onesf.b], writes=B("cum"))
        gc = self.gcum
        if sample:
            cum4 = rw["cum"].t[:, :, 0:nt].rearrange("p c (s j) -> p c s j", j=LS)
            S.op("dve", lambda e: e.memset(gc.t[:, :, 0:1], 0.0), writes=[gc.b])
            self.cp(gc.t[:, :, 1:NSEQ], cum4[:, :, 0:NSEQ - 1, LS - 1], reads=B("cum"), writes=[gc.b])
            self.tt(cum4, cum4, gc.t[:, :, :].unsqueeze(3).to_broadcast([128, 4, NSEQ, LS]), ALU.subtract,
                    reads=B("cum") + [gc.b], writes=B("cum"))
        self.act(R("eg"), R("cum"), AF.Exp, reads=B("cum"), writes=B("eg"))
        self.tt(R("tmp"), R("cum"), R("logw"), ALU.subtract, reads=B("cum", "logw"), writes=B("tmp"))
        self.act(R("ee"), R("tmp"), AF.Exp, reads=B("tmp"), writes=B("ee"))
        self.act(R("ei"), R("cum"), AF.Exp, reads=B("cum"), writes=B("ei"), scale=-1.0)
        yield
        if sample:
            eg4 = rw["eg"].t[:, :, 0:nt].rearrange("p c (s j) -> p c s j", j=LS)
            self.cp(gc.t[:, :, :], eg4[:, :, :, LS - 1], reads=B("eg"), writes=[gc.b])
        else:
            self.cp(gc.t[:, :, 0:1], rw["eg"].t[:, :, nt - 1:nt], reads=B("eg"), writes=[gc.b])
        self.tt(R("rt"), r_, R("eg"), ALU.mult, reads=[pm.b] + B("eg"), writes=B("eg"))
        self.tt(R("kt"), R("km"), R("ei"), ALU.mult, reads=B("km", "ei"), writes=B("km"))
        self.tt(R("bt"), R("b"), R("ei"), ALU.mult, reads=B("b", "ei"), writes=B("b"))
        self.stt(R("at"), R("kk"), -1.0, R("ee"), ALU.mult, ALU.mult, reads=B("kk", "ee"), writes=B("ee"))
        yield
        for src, srcb, dstt in ((v_, pm.b, self.vtk), (R("kt"), rw["kt"].b, self.ktok), (R("bt"), rw["bt"].b, self.btok)):
            pq = self.ps()
            for c in range(4):
                self.tr(pq.t[0:nt, c * 128:(c + 1) * 128], src[:, c, :], self.ident.t[:, :],
                        reads=[srcb, self.ident.b], writes=[pq.b], signal=(c == 3))
            self.cp(dstt.t[0:nt, :], pq.t[0:nt, :], reads=[pq.b], writes=[dstt.b], eng="act")
        vtk, ktok, btok = self.vtk, self.ktok, self.btok
        rt, kt, bt, at = rw["rt"], rw["kt"], rw["bt"], rw["at"]
        py = self.ps(hold=True)
        idn = self.ident.t[0:nt, 0:nt]
        def pair(c):
            if sample:
                for hd in range(2):
                    S.dma(self.q_misc, self.wkin.t[:, :, hd, :], self.d_st_wkv[:, 2 * c + hd, :, :].rearrange("s v k -> v s k"),
                          writes=[self.wkin.b])
                for s0 in range(0, NSEQ, 8):
                    p = self.ps()
                    for s in range(s0, s0 + 8):
                        self.tr(p.t[:, (s - s0) * 64:(s - s0 + 1) * 64], self.wkin.t[:, s, :, :].rearrange("v h k -> v (h k)"),
                                self.ident.t[0:64, 0:64], reads=[self.wkin.b, self.ident.b], writes=[p.b], signal=(s == s0 + 7))
                    self.cp(self.Hwkv_s.t[:, s0:s0 + 8, :], p.t[:, :].rearrange("p (s v) -> p s v", v=64), reads=[p.b],
                            writes=[self.Hwkv_s.b], eng="act")
                Hb = self.Hwkv_s.b
            else:
                Hb = self.Hwkv.bs[c]
            mAs, mBs, mPs = [], [], []
            atm = _rr(self.atms, self.am_i)
            rtm = _rr(self.rtms, self.am_i)
            self.am_i += 1
            hsel = self.hsel
            for hd in range(2):
                self.ts(atm.t[:, hd, 0:nt], at.t[:, c, 0:nt], hsel.t[:, hd:hd + 1], None, ALU.mult, None,
                        reads=[at.b, hsel.b], writes=[atm.b])
                self.ts(rtm.t[:, hd, 0:nt], rt.t[:, c, 0:nt], hsel.t[:, hd:hd + 1], None, ALU.mult, None,
                        reads=[rt.b, hsel.b], writes=[rtm.b])
            for hd in range(2):
                p1 = self.ps()
                pb2 = self.ps()
                A_ = atm.t[:, hd, 0:nt]
                R_ = rtm.t[:, hd, 0:nt]
                pairs = ((bt.t[:, c, 0:nt], A_, [bt.b, atm.b]), (A_, bt.t[:, c, 0:nt], [bt.b, atm.b]),
                         (kt.t[:, c, 0:nt], A_, [kt.b, atm.b]), (bt.t[:, c, 0:nt], R_, [bt.b, rtm.b]))
                for q, (l_, r2, rd) in enumerate(pairs):
                    self.mm(p1.t[0:nt, q * 128:q * 128 + nt], l_, r2, True, True,
                            reads=rd, writes=[p1.b], signal=(q == 3))
                self.mm(pb2.t[0:nt, hd * 128:hd * 128 + nt], kt.t[:, c, 0:nt], R_, True, True,
                        reads=[kt.b, rtm.b], writes=[pb2.b], signal=True)
                mA = _rr(self.mA, self.mA_i)
                self.mA_i += 1
                mB = _rr(self.mB, self.mB_i)
                self.mB_i += 1
                self.tt(mA.t[0:nt, :, 0:nt], p1.t[0:nt, :].rearrange("p (q t) -> p q t", t=128)[:, :, 0:nt], wmask.t[0:nt, 0:4, 0:nt],
                        ALU.mult, reads=[p1.b, wmask.b], writes=[mA.b])
                self.tt(mB.t[0:nt, 0:nt], pb2.t[0:nt, hd * 128:hd * 128 + nt], wmask.t[0:nt, 4, 0:nt], ALU.mult,
                        reads=[pb2.b, wmask.b], writes=[mB.b])
                if Builder.BF16_CHAIN:
                    mP = _rr(self.mP, self.mP_i)
                    self.mP_i += 1
                    self.cp(mP.t[0:nt, :, 0:nt], mA.t[0:nt, 0:2, 0:nt], reads=[mA.b], writes=[mP.b], eng="act")
                    mPs.append(mP)
                mAs.append(mA)
                mBs.append(mB)
                yield
            yield
            px = self.ps()
            for hd in range(2):
                h = 2 * c + hd
                sl = slice(64 * hd, 64 * hd + 64)
                o = px.t[0:nt, hd * 64:(hd + 1) * 64]
                if sample:
                    self.tt(self.qm.t[:, :, :], atm.t[:, hd, 0:nt].unsqueeze(1).to_broadcast([128, NSEQ, 64]), self.seqselT.t[:, :, :],
                            ALU.mult, reads=[atm.b, self.seqselT.b], writes=[self.qm.b])
                    for s in range(NSEQ):
                        self.mm(o, self.qm.t[:, s, :], self.Hwkv_s.t[:, s, :], s == 0, False,
                                reads=[self.qm.b, Hb], writes=[px.b], signal=False)
                else:
                    self.mm(o, atm.t[:, hd, 0:nt], self.Hwkv.t[:, c, :], True, False, reads=[atm.b, Hb], writes=[px.b], signal=False)
                self.mm(o, mAs[hd].t[0:nt, 2, 0:nt], vtk.t[0:nt, h * 64:(h + 1) * 64], False, True,
                        reads=[mAs[hd].b, vtk.b], writes=[px.b], signal=True)
            bfc = Builder.BF16_CHAIN
            if bfc:
                U = _rr(self.Ub, self.Ub_i)
                self.Ub_i += 1
            else:
                U = _rr(self.Up, self.Up_i)
                self.Up_i += 1
            self.cp(U.t[0:nt, :], px.t[0:nt, 0:128], reads=[px.b], writes=[U.b], eng="act")
            if bfc:
                PT = [mPs[0].t[0:nt, 0, 0:nt], mPs[1].t[0:nt, 0, 0:nt]]
                Pm = [mPs[0].t[0:nt, 1, 0:nt], mPs[1].t[0:nt, 1, 0:nt]]
                Pb = [[mPs[0].b], [mPs[1].b]]
                idl, idb = self.identb.t[0:nt, 0:nt], self.identb.b
            else:
                PT = [mAs[0].t[0:nt, 0, 0:nt], mAs[1].t[0:nt, 0, 0:nt]]
                Pm = [mAs[0].t[0:nt, 1, 0:nt], mAs[1].t[0:nt, 1, 0:nt]]
                Pb = [[mAs[0].b], [mAs[1].b]]
                idl, idb = idn, self.ident.b
            for lv in range(nlev):
                pu = self.ps()
                for hd in range(2):
                    o = pu.t[0:nt, hd * 64:(hd + 1) * 64]
                    self.mm(o, idl, U.t[0:nt, hd * 64:(hd + 1) * 64], True, False, reads=[idb, U.b], writes=[pu.b], signal=False)
                    self.mm(o, PT[hd], U.t[0:nt, hd * 64:(hd + 1) * 64], False, True, reads=Pb[hd] + [U.b], writes=[pu.b], signal=(hd == 1))
                if bfc and lv < nlev - 1:
                    Un = _rr(self.Ub, self.Ub_i)
                    self.Ub_i += 1
                else:
                    Un = _rr(self.Up, self.Up_i)
                    self.Up_i += 1
                self.cp(Un.t[0:nt, :], pu.t[0:nt, 0:128], reads=[pu.b], writes=[Un.b], eng="act")
                U = Un
                if lv == nlev - 1:
                    yield
                if lv < nlev - 1:
                    pp = self.ps()
                    for hd in range(2):
                        self.mm(pp.t[0:nt, (2 * hd) * 128:(2 * hd) * 128 + nt], Pm[hd], PT[hd], True, True,
                                reads=Pb[hd], writes=[pp.b], signal=False)
                        self.mm(pp.t[0:nt, (2 * hd + 1) * 128:(2 * hd + 1) * 128 + nt], PT[hd], Pm[hd], True, True,
                                reads=Pb[hd], writes=[pp.b], signal=(hd == 1))
                    if bfc:
                        PPt = _rr(self.PPb, self.PPb_i)
                        self.PPb_i += 1
                    else:
                        PPt = _rr(self.PP, self.PP_i)
                        self.PP_i += 1
                    self.cp(PPt.t[0:nt, :, 0:nt], pp.t[0:nt, :].rearrange("p (q t) -> p q t", t=128)[:, :, 0:nt],
                            reads=[pp.b], writes=[PPt.b], eng=Builder.PP_ENG)
                    PT = [PPt.t[0:nt, 0, 0:nt], PPt.t[0:nt, 2, 0:nt]]
                    Pm = [PPt.t[0:nt, 1, 0:nt], PPt.t[0:nt, 3, 0:nt]]
                    Pb = [[PPt.b], [PPt.b]]
                    yield
            yield
            for hd in range(2):
                h = 2 * c + hd
                sl = slice(64 * hd, 64 * hd + 64)
                o = py.t[0:nt, h * 64:(h + 1) * 64]
                if sample:
                    self.tt(self.qm2.t[:, :, :], rtm.t[:, hd, 0:nt].unsqueeze(1).to_broadcast([128, NSEQ, 64]), self.seqselT.t[:, :, :],
                            ALU.mult, reads=[rtm.b, self.seqselT.b], writes=[self.qm2.b])
                    for s in range(NSEQ):
                        self.mm(o, self.qm2.t[:, s, :], self.Hwkv_s.t[:, s, :], s == 0, False,
                                reads=[self.qm2.b, Hb], writes=[py.b], signal=False)
                else:
                    self.mm(o, rtm.t[:, hd, 0:nt], self.Hwkv.t[:, c, :], True, False, reads=[rtm.b, Hb], writes=[py.b], signal=False)
                self.mm(o, mAs[hd].t[0:nt, 3, 0:nt], U.t[0:nt, hd * 64:(hd + 1) * 64], False, False,
                        reads=[mAs[hd].b, U.b], writes=[py.b], signal=False)
                self.mm(o, mBs[hd].t[0:nt, 0:nt], vtk.t[0:nt, h * 64:(h + 1) * 64], False, True,
                        reads=[mBs[hd].b, vtk.b], writes=[py.b], signal=True)
            yield
            if not sample:
                ph = self.ps()
                self.mm(ph.t[:, 0:128], btok.t[0:nt, c * 128:(c + 1) * 128], U.t[0:nt, :], True, False,
                        reads=[btok.b, U.b], writes=[ph.b], signal=False)
                self.mm(ph.t[:, 0:128], ktok.t[0:nt, c * 128:(c + 1) * 128], vtk.t[0:nt, c * 128:(c + 1) * 128], False, True,
                        reads=[ktok.b, vtk.b], writes=[ph.b], signal=True)
                for hd in range(2):
                    sl = slice(64 * hd, 64 * hd + 64)
                    self.tt(self.Hwkv.t[sl, c, :], self.Hwkv.t[sl, c, :], ph.t[sl, 64 * hd:64 * hd + 64], ALU.add,
                            reads=[Hb, ph.b], writes=[Hb])
                    self.ts(self.Hwkv.t[sl, c, :], self.Hwkv.t[sl, c, :], gc.t[sl, c, 0:1], None, ALU.mult, None,
                            reads=[Hb, gc.b], writes=[Hb])
            else:
                self.tt(self.km.t[:, :, :], btok.t[0:64, c * 128:(c + 1) * 128].unsqueeze(1).to_broadcast([64, NSEQ, 128]),
                        self.seqsel.t[:, :].unsqueeze(2).to_broadcast([64, NSEQ, 128]), ALU.mult,
                        reads=[btok.b, self.seqsel.b], writes=[self.km.b])
                self.tt(self.km2.t[:, :, :], ktok.t[0:64, c * 128:(c + 1) * 128].unsqueeze(1).to_broadcast([64, NSEQ, 128]),
                        self.seqsel.t[:, :].unsqueeze(2).to_broadcast([64, NSEQ, 128]), ALU.mult,
                        reads=[ktok.b, self.seqsel.b], writes=[self.km2.b])
                for s0 in range(0, NSEQ, 4):
                    ph = self.ps()
                    for s in range(s0, s0 + 4):
                        o = ph.t[:, (s - s0) * 128:(s - s0 + 1) * 128]
                        self.mm(o, self.km.t[:, s, :], U.t[0:nt, :], True, False, reads=[self.km.b, U.b], writes=[ph.b], signal=False)
                        self.mm(o, self.km2.t[:, s, :], vtk.t[0:nt, c * 128:(c + 1) * 128], False, True,
                                reads=[self.km2.b, vtk.b], writes=[ph.b], signal=(s == s0 + 3))
                    for hd in range(2):
                        sl = slice(64 * hd, 64 * hd + 64)
                        hv = self.Hwkv_s.t[sl, s0:s0 + 4, :]
                        pv_ = ph.t[sl, :].rearrange("p (s x) -> p s x", x=128)[:, :, 64 * hd:64 * hd + 64]
                        self.tt(hv, hv, pv_, ALU.add, reads=[Hb, ph.b], writes=[Hb])
                        self.tt(hv, hv, gc.t[sl, c, s0:s0 + 4].unsqueeze(2).to_broadcast([64, 4, 64]), ALU.mult,
                                reads=[Hb, gc.b], writes=[Hb])
                for s0 in range(0, NSEQ, 4):
                    p = self.ps()
                    for s in range(s0, s0 + 4):
                        self.tr(p.t[0:64, (s - s0) * 128:(s - s0 + 1) * 128], self.Hwkv_s.t[:, s, :], self.ident.t[:, :],
                                reads=[Hb, self.ident.b], writes=[p.b], signal=(s == s0 + 3))
                    self.cp(self.wkin.t[:, s0:s0 + 4, :, :].rearrange("v s h k -> v s (h k)"),
                            p.t[0:64, :].rearrange("v (s x) -> v s x", x=128), reads=[p.b], writes=[self.wkin.b], eng="act")
                for hd in range(2):
                    ev = S.dma(self.q_misc, self.d_wkv_s[:, 2 * c + hd, :, :].rearrange("s v k -> v s k"), self.wkin.t[:, :, hd, :],
                               reads=[self.wkin.b])
                    self.out_events.append(ev)

        if sample or not Builder.PAIR_IL:
            for c in range(4):
                yield from pair(c)
        else:
            for cs in (((0, 1, 2, 3),) if Builder.NPAIR_IL == 4 else ((0, 1), (2, 3))):
                gens = [pair(c) for c in cs]
                alive = [True] * len(gens)
                while any(alive):
                    for gi, g_ in enumerate(gens):
                        if alive[gi]:
                            try:
                                next(g_)
                            except StopIteration:
                                alive[gi] = False
                    yield
        yield
        ysb, ysq, yst = self.ysb, self.ysq, self.yst
        self.cp(ysb.t[0:nt, :], py.t[0:nt, :], reads=[py.b], writes=[ysb.b], eng="act")
        self.release(py)
        y3 = ysb.t[0:nt, :].rearrange("p (h v) -> p h v", v=64)
        q3 = ysq.t[0:nt, :].rearrange("p (h v) -> p h v", v=64)
        S.op("dve", lambda e: e.tensor_reduce(out=yst.t[0:nt, 0, :], in_=y3, axis=AX.X, op=ALU.add), reads=[ysb.b], writes=[yst.b])
        self.ts(yst.t[0:nt, 1, :], yst.t[0:nt, 0, :], -1.0 / 64, None, ALU.mult, None, reads=[yst.b], writes=[yst.b])
        self.tt(y3, y3, yst.t[0:nt, 1, :].unsqueeze(2).to_broadcast([nt, 8, 64]), ALU.add, reads=[ysb.b, yst.b], writes=[ysb.b])
        self.act(ysq.t[0:nt, :], ysb.t[0:nt, :], AF.Square, reads=[ysb.b], writes=[ysq.b])
        S.op("dve", lambda e: e.tensor_reduce(out=yst.t[0:nt, 2, :], in_=q3, axis=AX.X, op=ALU.add), reads=[ysq.b], writes=[yst.b])
        self.act(yst.t[0:nt, 3, :], yst.t[0:nt, 2, :], AF.Sqrt, reads=[yst.b, self.eps_norm.b], writes=[yst.b],
                 bias=self.eps_norm.t[0:nt, 2:3], scale=1.0 / 64)
        S.op("dve", lambda e: e.reciprocal(out=yst.t[0:nt, 3, :], in_=yst.t[0:nt, 3, :]), reads=[yst.b], writes=[yst.b])
        self.tt(y3, y3, yst.t[0:nt, 3, :].unsqueeze(2).to_broadcast([nt, 8, 64]), ALU.mult, reads=[ysb.b, yst.b], writes=[ysb.b])
        pt = self.ps()
        for c in range(4):
            self.tr(pt.t[:, c * 128:c * 128 + nt], ysb.t[0:nt, c * 128:(c + 1) * 128], idn,
                    reads=[ysb.b, self.ident.b], writes=[pt.b], signal=(c == 3))
        self.tt(R("tmp"), pt.t[:, :].rearrange("p (c t) -> p c t", t=128)[:, :, 0:nt], self.bc4(self.vcol("lnx_g"), nt), ALU.mult,
                reads=[pt.b, V.b], writes=B("tmp"))
        self.tt(R("tmp"), R("tmp"), self.bc4(self.vcol("lnx_b"), nt), ALU.add, reads=B("tmp") + [V.b], writes=B("tmp"))
        self.tt(R("tmp"), R("tmp"), R("bon"), ALU.add, reads=B("tmp", "bon"), writes=B("tmp"))
        self.tt(self.orwT.t[:, :, c0:c0 + nt], R("tmp"), R("g"), ALU.mult, reads=B("tmp", "g"), writes=self.orwT.bs)
        yield
        if sample:
            pz = self.ps()
            pc4 = pcur.rearrange("p c (s j) -> p c s j", j=LS)
            for ch in range(14):
                self.tr(pz.t[0:NSEQ, (ch % 4) * 128:(ch % 4 + 1) * 128], pc4[:, ch, :, LS - 1], self.ident.t[:, :],
                        reads=[pT.b, self.ident.b], writes=[pz.b], signal=(ch % 4 == 3 or ch == 13))
                if ch % 4 == 3 or ch == 13:
                    c_lo = ch - (ch % 4)
                    self.cp(self.shs.t[0:NSEQ, c_lo * 128:(ch + 1) * 128], pz.t[0:NSEQ, 0:(ch - c_lo + 1) * 128], reads=[pz.b],
                            writes=[self.shs.b], eng="act")
                    if ch != 13:
                        pz = self.ps()
            ev = S.dma(self.q_misc, self.d_shift_s[:, :], self.shs.t[0:NSEQ, :], reads=[self.shs.b])
            self.out_events.append(ev)
        else:
            is_last_prompt_tile_of_block = (kind == "meta") or (c0 + nt == self.TBcur)
            if is_last_prompt_tile_of_block:
                self.cp(self.pprev.t[:, :], pT.t[:, :, c0 + nt], reads=[pT.b], writes=[self.pprev.b], eng="act")

    def store_wkv(self, H, dst, s):
        S = self.S
        p = self.ps()
        for c in range(4):
            self.tr(p.t[0:64, c * 128:(c + 1) * 128], self.Hwkv.t[:, c, :], self.ident.t[:, :],
                    reads=[self.Hwkv.bs[c], self.ident.b], writes=[p.b], signal=(c == 3))
        self.cp(self.ysb.t[0:64, :], p.t[0:64, :], reads=[p.b], writes=[self.ysb.b], eng="act")
        ev = S.dma(self.q_misc, self.d_wkv_p.rearrange("h v k -> v h k"), self.ysb.t[0:64, :].rearrange("v (h k) -> v h k", k=64),
                   reads=[self.ysb.b])
        self.out_events.append(ev)
        p2 = self.ps()
        self.tr(p2.t[0:14, 0:128], self.pprev.t[:, :], self.ident.t[:, :], reads=[self.pprev.b, self.ident.b], writes=[p2.b])
        self.cp(self.osq.t[0:14, 0:128], p2.t[0:14, 0:128], reads=[p2.b], writes=[self.osq.b], eng="act")
        ev = S.dma(self.q_misc, self.d_shift_p[:, :], self.osq.t[0:14, 0:128], reads=[self.osq.b])
        self.out_events.append(ev)


def pack_vecs(cfg, inp):
    def fm(v):
        v = np.asarray(v, np.float32).reshape(-1)
        return v.reshape(-1, 128).T
    cols = [fm(inp["ffn1_norm"]), fm(inp["mix_norm"]), fm(inp["ffn2_norm"]), fm(inp["final_norm"]),
            fm(inp["mu_shift"]), fm(inp["w0"]), fm(inp["a0"]), fm(inp["k_k"]), fm(inp["k_a"]),
            fm(inp["r_k"]), fm(inp["lnx_g"]), fm(inp["lnx_b"])]
    return np.ascontiguousarray(np.concatenate(cols, axis=1), dtype=np.float32)


def make_in_maps(cfg, inp, n_cores):
    consts = host_constants(cfg)
    vecs = pack_vecs(cfg, inp)
    gng = np.ascontiguousarray(np.broadcast_to(np.asarray(inp["ret_gn_g"], np.float32)[None, :], (128, 512)))
    maps = []
    for c in range(n_cores):
        m = {
            "xp": np.ascontiguousarray(inp["x_prompt"][c]),
            "xs": np.ascontiguousarray(inp["x_sample"][c * NSEQ:(c + 1) * NSEQ].reshape(NSEQ * LS, cfg.DM)),
            "meta": np.ascontiguousarray(inp["meta_tokens"]),
            "st_ret": np.ascontiguousarray(inp["state_ret"][c * NSEQ:(c + 1) * NSEQ]),
            "st_wkv": np.ascontiguousarray(inp["state_wkv"][c * NSEQ:(c + 1) * NSEQ]),
            "st_shift": np.ascontiguousarray(inp["state_shift"][c * NSEQ:(c + 1) * NSEQ]),
            "vecs": vecs, "gng": gng,
            "w2": np.ascontiguousarray(inp["w2"]), "a2": np.ascontiguousarray(inp["a2"]),
            "g2": np.ascontiguousarray(inp["g2"]),
        }
        for k in Builder.WEIGHTS:
            m[k] = np.ascontiguousarray(inp[k], dtype=np.float32)
        for k, v in consts.items():
            if not k.startswith("_"):
                m["c_" + k] = v
        maps.append(m)
    return maps


def gather_outputs(cfg, res, n_cores):
    B = n_cores
    yp = np.stack([res[c]["yp"] for c in range(B)]).reshape(B, 128 * cfg.NT, cfg.DM)
    ys = np.concatenate([res[c]["ys"].reshape(NSEQ, LS, cfg.DM) for c in range(B)])
    ret_p = np.stack([res[c]["ret_p"] for c in range(B)])
    wkv_p = np.stack([res[c]["wkv_p"] for c in range(B)])
    shift_p = np.stack([res[c]["shift_p"].reshape(SHIFT_W) for c in range(B)])
    ret_s = np.concatenate([res[c]["ret_s"] for c in range(B)])
    wkv_s = np.concatenate([res[c]["wkv_s"] for c in range(B)])
    shift_s = np.concatenate([res[c]["shift_s"] for c in range(B)])
    return tuple(np.ascontiguousarray(a, dtype=np.float32) for a in (yp, ys, ret_p, wkv_p, shift_p, ret_s, wkv_s, shift_s))


def kernel(**inputs):
    cfg = Cfg()
    inp = {k: np.asarray(v) for k, v in inputs.items()}
    nc = Builder(cfg).build()
    in_maps = make_in_maps(cfg, inp, N_CORES)
    res = run_bass_kernel_spmd(nc, in_maps, core_ids=list(range(N_CORES)))
    return gather_outputs(cfg, res.results, N_CORES)
```
